# Optimizing a Trainium2 kernel written in Bass

```python
import math
import jax, jax.numpy as jnp
from jax import lax
import numpy as np

D_MODEL = 2048
BATCH = 4
SEQ = 8192
DEPTH = 1

CHUNK = 64
Q_BLOCK = 128
SSM_WIDTH = D_MODEL // 2
SSM_GROUP = 16
SSM_GROUPS = SSM_WIDTH // SSM_GROUP
SSM_STATE = 64
ATTN_WIDTH = D_MODEL - SSM_WIDTH
ATTN_HEADS = 8
ATTN_V_DIM = ATTN_WIDTH // ATTN_HEADS
ATTN_QK_DIM = ATTN_V_DIM // 2
IN_WIDTH = SSM_WIDTH + 3 * ATTN_WIDTH
MLP_HIDDEN = 4 * D_MODEL
RMS_EPS = 1e-6
DT_MIN = 0.001
DT_MAX = 0.1

kernel_name = "hybrid_s5_diffattn_adaln_block"


def _rms_f32(x, g):
    xf = x.astype(jnp.float32)
    return xf * lax.rsqrt(jnp.mean(xf * xf, axis=-1, keepdims=True) + RMS_EPS) * g.astype(jnp.float32)


def _rmsnorm(x, g):
    return _rms_f32(x, g).astype(x.dtype)


def _s5_combine(e1, e2):
    a1r, a1i, b1r, b1i = e1
    a2r, a2i, b2r, b2i = e2
    return (a2r * a1r - a2i * a1i,
            a2r * a1i + a2i * a1r,
            a2r * b1r - a2i * b1i + b2r,
            a2r * b1i + a2i * b1r + b2i)


def _s5_mixer(u, lam_re, lam_im, b_re, b_im, c_re, c_im, d, log_step, w_glu, b_glu):
    f32 = jnp.float32
    bsz, seq, _ = u.shape
    n_chunks = seq // CHUNK
    u4 = u.astype(f32).reshape(bsz, seq, SSM_GROUPS, SSM_GROUP)
    lr = lam_re.astype(f32)
    li = lam_im.astype(f32)
    dt = jnp.exp(log_step.astype(f32))[:, None]
    mag = jnp.exp(dt * lr)
    ar = mag * jnp.cos(dt * li)
    ai = mag * jnp.sin(dt * li)
    den = lr * lr + li * li
    zr = ar - 1.0
    kr = (zr * lr + ai * li) / den
    ki = (ai * lr - zr * li) / den
    br = b_re.astype(f32)
    bi = b_im.astype(f32)
    bbar_r = kr[..., None] * br - ki[..., None] * bi
    bbar_i = kr[..., None] * bi + ki[..., None] * br
    steps = jnp.arange(1, CHUNK + 1, dtype=f32)[:, None, None]
    pmag = jnp.exp(steps * dt * lr)
    pw_r = pmag * jnp.cos(steps * dt * li)
    pw_i = pmag * jnp.sin(steps * dt * li)
    a_r = jnp.broadcast_to(ar, (bsz, CHUNK, SSM_GROUPS, SSM_STATE))
    a_i = jnp.broadcast_to(ai, (bsz, CHUNK, SSM_GROUPS, SSM_STATE))
    cr = c_re.astype(f32)
    ci = c_im.astype(f32)
    u_chunks = u4.reshape(bsz, n_chunks, CHUNK, SSM_GROUPS, SSM_GROUP).transpose(1, 0, 2, 3, 4)

    def step(carry, uc):
        hr0, hi0 = carry
        bu_r = jnp.einsum('bsgh,gph->bsgp', uc, bbar_r)
        bu_i = jnp.einsum('bsgh,gph->bsgp', uc, bbar_i)
        _, _, loc_r, loc_i = lax.associative_scan(_s5_combine, (a_r, a_i, bu_r, bu_i), axis=1)
        hr = loc_r + pw_r * hr0[:, None] - pw_i * hi0[:, None]
        hi = loc_i + pw_r * hi0[:, None] + pw_i * hr0[:, None]
        y = jnp.einsum('gqp,bsgp->bsgq', cr, hr) - jnp.einsum('gqp,bsgp->bsgq', ci, hi)
        return (hr[:, -1], hi[:, -1]), y

    init = (jnp.zeros((bsz, SSM_GROUPS, SSM_STATE), f32), jnp.zeros((bsz, SSM_GROUPS, SSM_STATE), f32))
    _, ys = lax.scan(step, init, u_chunks)
    y = ys.transpose(1, 0, 2, 3, 4).reshape(bsz, seq, SSM_GROUPS, SSM_GROUP) + d.astype(f32) * u4
    y = jax.nn.gelu(y.reshape(bsz, seq, SSM_WIDTH), approximate=False).astype(u.dtype)
    return y * jax.nn.sigmoid(y @ w_glu + b_glu)


def _diff_attention(q, k, v, g_q, g_k, lq1, lk1, lq2, lk2, g_subln, lambda_init):
    f32 = jnp.float32
    out_dtype = v.dtype
    bsz, seq, _ = q.shape
    q = _rms_f32(q.reshape(bsz, seq, ATTN_HEADS, 2, ATTN_QK_DIM), g_q) * (ATTN_QK_DIM ** -0.5)
    k = _rms_f32(k.reshape(bsz, seq, ATTN_HEADS, 2, ATTN_QK_DIM), g_k)
    v = v.reshape(bsz, seq, ATTN_HEADS, ATTN_V_DIM).astype(f32)
    lam = (jnp.exp(jnp.sum(lq1.astype(f32) * lk1.astype(f32)))
           - jnp.exp(jnp.sum(lq2.astype(f32) * lk2.astype(f32))) + lambda_init)
    n_blocks = seq // Q_BLOCK
    qb = q.reshape(bsz, n_blocks, Q_BLOCK, ATTN_HEADS, 2, ATTN_QK_DIM).transpose(1, 0, 2, 3, 4, 5)
    k_chunk = jnp.arange(seq) // CHUNK

    def block(args):
        qblk, idx = args
        s = jnp.einsum('bqhcd,bkhcd->bhcqk', qblk, k)
        q_chunk = (idx * Q_BLOCK + jnp.arange(Q_BLOCK)) // CHUNK
        mask = k_chunk[None, :] <= q_chunk[:, None]
        p = jax.nn.softmax(jnp.where(mask, s, -jnp.inf), axis=-1)
        w = p[:, :, 0] - lam * p[:, :, 1]
        return jnp.einsum('bhqk,bkhe->bqhe', w, v)

    o = lax.map(block, (qb, jnp.arange(n_blocks)))
    o = o.transpose(1, 0, 2, 3, 4).reshape(bsz, seq, ATTN_HEADS, ATTN_V_DIM)
    o = _rms_f32(o, g_subln) * (1.0 - lambda_init)
    return o.reshape(bsz, seq, ATTN_WIDTH).astype(out_dtype)


def setup_inputs(seed: int = 0) -> dict:
    key = jax.random.key(seed)
    ks = jax.random.split(key, 32)
    f32 = jnp.float32

    def nrm(k, shape, scale):
        return jax.random.normal(k, shape, f32) * scale

    G, P, H = SSM_GROUPS, SSM_STATE, SSM_GROUP
    n_idx = jnp.arange(P, dtype=f32)
    return {
        "x": nrm(ks[0], (BATCH, SEQ, D_MODEL), 1.0),
        "c": nrm(ks[1], (BATCH, D_MODEL), 1.0),
        "w_ada": nrm(ks[2], (DEPTH, D_MODEL, 6 * D_MODEL), 0.5 * D_MODEL ** -0.5),
        "b_ada": nrm(ks[3], (DEPTH, 6 * D_MODEL), 0.02),
        "g_norm_mix": 1.0 + nrm(ks[4], (DEPTH, D_MODEL), 0.02),
        "g_norm_mlp": 1.0 + nrm(ks[5], (DEPTH, D_MODEL), 0.02),
        "w_in": nrm(ks[6], (DEPTH, D_MODEL, IN_WIDTH), D_MODEL ** -0.5),
        "ssm_lambda_re": -0.5 * jnp.exp(nrm(ks[7], (DEPTH, G, P), 0.05)),
        "ssm_lambda_im": jnp.pi * n_idx + nrm(ks[8], (DEPTH, G, P), 0.01),
        "ssm_b_re": nrm(ks[9], (DEPTH, G, P, H), (2 * H) ** -0.5),
        "ssm_b_im": nrm(ks[10], (DEPTH, G, P, H), (2 * H) ** -0.5),
        "ssm_c_re": nrm(ks[11], (DEPTH, G, H, P), P ** -0.5),
        "ssm_c_im": nrm(ks[12], (DEPTH, G, H, P), P ** -0.5),
        "ssm_d": nrm(ks[13], (DEPTH, G, H), 1.0),
        "ssm_log_step": jax.random.uniform(ks[14], (DEPTH, G), f32, math.log(DT_MIN), math.log(DT_MAX)),
        "w_glu": nrm(ks[15], (DEPTH, SSM_WIDTH, SSM_WIDTH), SSM_WIDTH ** -0.5),
        "b_glu": nrm(ks[16], (DEPTH, SSM_WIDTH), 0.02),
        "g_q": 1.0 + nrm(ks[17], (DEPTH, ATTN_QK_DIM), 0.02),
        "g_k": 1.0 + nrm(ks[18], (DEPTH, ATTN_QK_DIM), 0.02),
        "lambda_q1": nrm(ks[19], (DEPTH, ATTN_QK_DIM), 0.1),
        "lambda_k1": nrm(ks[20], (DEPTH, ATTN_QK_DIM), 0.1),
        "lambda_q2": nrm(ks[21], (DEPTH, ATTN_QK_DIM), 0.1),
        "lambda_k2": nrm(ks[22], (DEPTH, ATTN_QK_DIM), 0.1),
        "g_subln": 1.0 + nrm(ks[23], (DEPTH, ATTN_V_DIM), 0.02),
        "w_out": nrm(ks[24], (DEPTH, D_MODEL, D_MODEL), D_MODEL ** -0.5),
        "w_mlp1": nrm(ks[25], (DEPTH, D_MODEL, MLP_HIDDEN), D_MODEL ** -0.5),
        "w_mlp2": nrm(ks[26], (DEPTH, MLP_HIDDEN, D_MODEL), MLP_HIDDEN ** -0.5),
    }


def reference(x, c, w_ada, b_ada, g_norm_mix, g_norm_mlp, w_in, ssm_lambda_re, ssm_lambda_im,
              ssm_b_re, ssm_b_im, ssm_c_re, ssm_c_im, ssm_d, ssm_log_step, w_glu, b_glu,
              g_q, g_k, lambda_q1, lambda_k1, lambda_q2, lambda_k2, g_subln, w_out, w_mlp1, w_mlp2):
    c_act = jax.nn.silu(c)
    for l in range(DEPTH):
        lambda_init = 0.8 - 0.6 * math.exp(-0.3 * l)
        mod = c_act @ w_ada[l] + b_ada[l]
        shift1, scale1, gate1, shift2, scale2, gate2 = jnp.split(mod, 6, axis=-1)
        h = _rmsnorm(x, g_norm_mix[l]) * (1.0 + scale1[:, None]) + shift1[:, None]
        proj = h @ w_in[l]
        u, q, k, v = jnp.split(proj, [SSM_WIDTH, SSM_WIDTH + ATTN_WIDTH, SSM_WIDTH + 2 * ATTN_WIDTH], axis=-1)
        y_ssm = _s5_mixer(u, ssm_lambda_re[l], ssm_lambda_im[l], ssm_b_re[l], ssm_b_im[l],
                          ssm_c_re[l], ssm_c_im[l], ssm_d[l], ssm_log_step[l], w_glu[l], b_glu[l])
        y_att = _diff_attention(q, k, v, g_q[l], g_k[l], lambda_q1[l], lambda_k1[l],
                                lambda_q2[l], lambda_k2[l], g_subln[l], lambda_init)
        mixed = jnp.concatenate([y_ssm, y_att], axis=-1) @ w_out[l]
        x = x + gate1[:, None] * mixed
        h = _rmsnorm(x, g_norm_mlp[l]) * (1.0 + scale2[:, None]) + shift2[:, None]
        x = x + gate2[:, None] * (jnp.square(jax.nn.relu(h @ w_mlp1[l])) @ w_mlp2[l])
    return x
```

```python
import math
import numpy as np
from contextlib import ExitStack
import concourse.bass as bass
import concourse.mybir as mybir
from concourse.bass_utils import run_bass_kernel_spmd

F32 = mybir.dt.float32
BF16 = mybir.dt.bfloat16
I32 = mybir.dt.int32
AF = mybir.ActivationFunctionType
ALU = mybir.AluOpType
AX = mybir.AxisListType

D = 2048
NKB = 16
NH = 8
EPS = 1e-6
LAMBDA_INIT = 0.8 - 0.6 * math.exp(-0.3 * 0)
ENGS = ["pe", "act", "dve", "pool", "sp"]
SAME_ENGINE_SYNC = True
NSTORE = 6


class Sched:
    def __init__(self, nc, es, name):
        self.nc, self.es, self.name = nc, es, name
        self.ops, self.last_w, self.readers = [], {}, {}
        self.allsems = []
        self.esem = {e: self._sem(f"{name}_{e}") for e in ENGS}
        self.dsem = {}
        self.psem = [self._sem(f"{name}_st{i}") for i in range(NSTORE)]

    def _sem(self, name):
        h = self.nc.alloc_semaphore(name=name)
        self.allsems.append(h)
        return h

    def op(self, eng, fn, r=(), w=(), dma=False, dkey=None):
        deps = set()
        for k in r:
            if k in self.last_w:
                deps.add(self.last_w[k])
        for k in w:
            if k in self.last_w:
                deps.add(self.last_w[k])
            deps.update(self.readers.get(k, ()))
        idx = len(self.ops)
        self.ops.append(dict(eng=eng, fn=fn, deps=sorted(deps), dma=dma, dkey=dkey, sig=None, need=False, idx=idx))
        for k in r:
            self.readers.setdefault(k, []).append(idx)
        for k in w:
            self.last_w[k] = idx
            self.readers[k] = []
        return idx

    def pe(self, fn, r=(), w=()): return self.op("pe", fn, r, w)
    def act(self, fn, r=(), w=()): return self.op("act", fn, r, w)
    def dve(self, fn, r=(), w=()): return self.op("dve", fn, r, w)
    def pool(self, fn, r=(), w=()): return self.op("pool", fn, r, w)

    def load(self, fn, r=(), w=(), eng="sp"):
        return self.op(eng, fn, r, w, dma=True, dkey=("L", w[0]))

    def store(self, fn, r=(), w=(), eng="pool"):
        return self.op(eng, fn, r, w, dma=True, dkey=None)

    def run(self):
        nc, ops = self.nc, self.ops
        for o in ops:
            keep = []
            for d in o["deps"]:
                p = ops[d]
                if not p["dma"] and not o["dma"] and p["eng"] == o["eng"]:
                    if o["eng"] == "pe" or not SAME_ENGINE_SYNC:
                        continue
                keep.append(d)
                p["need"] = True
            o["deps"] = keep
        last = {}
        for o in ops:
            if not o["dma"]:
                last[o["eng"]] = o
        for o in last.values():
            o["need"] = True
        cnt = {e: 0 for e in ENGS}
        dcnt = {}
        pcnt = [0] * NSTORE
        plast = [None] * NSTORE
        nst = 0
        for o in ops:
            if o["dma"]:
                if o["dkey"] is None:
                    s = nst % NSTORE
                    nst += 1
                    if plast[s] is not None:
                        o["deps"].append(plast[s])
                    pcnt[s] += 16
                    o["sig"] = (self.psem[s], pcnt[s])
                    plast[s] = o["idx"]
                else:
                    k = o["dkey"]
                    if k not in self.dsem:
                        self.dsem[k] = self._sem(f"{self.name}_d{len(self.dsem)}")
                        dcnt[k] = 0
                    dcnt[k] += 16
                    o["sig"] = (self.dsem[k], dcnt[k])
            elif o["need"]:
                cnt[o["eng"]] += 1
                o["sig"] = (self.esem[o["eng"]], cnt[o["eng"]])
        finals = [(self.esem[e], cnt[e]) for e in ENGS if cnt[e] > 0]
        finals += [(self.dsem[k], dcnt[k]) for k in self.dsem]
        finals += [(self.psem[i], pcnt[i]) for i in range(NSTORE) if pcnt[i] > 0]
        by_eng = {e: [o for o in ops if o["eng"] == e] for e in ENGS}

        def body(e):
            def f(eng):
                waited = {}
                for o in by_eng[e]:
                    for d in o["deps"]:
                        sem, val = ops[d]["sig"]
                        if waited.get(id(sem), 0) < val:
                            eng.wait_ge(sem, val)
                            waited[id(sem)] = val
                    ins = o["fn"](eng)
                    if o["sig"] is not None:
                        ins.then_inc(o["sig"][0], 16 if o["dma"] else 1)
                for sem, val in finals:
                    if waited.get(id(sem), 0) < val:
                        eng.wait_ge(sem, val)
            return f

        with nc.Block() as block:
            block.tensor(body("pe"))
            block.scalar(body("act"))
            block.vector(body("dve"))
            block.gpsimd(body("pool"))
            block.sync(body("sp"))
        nc.all_engine_barrier()
        nc.clear_and_free_semaphores(self.allsems)
        nc.all_engine_barrier()


PARAMS = [
    ("c", [D]), ("w_ada", [D, 6 * D]), ("b_ada", [6 * D]), ("g_norm_mix", [D]), ("g_norm_mlp", [D]),
    ("w_in", [D, 4096]), ("ssm_lambda_re", [64, 64]), ("ssm_lambda_im", [64, 64]),
    ("ssm_b_re", [64, 64, 16]), ("ssm_b_im", [64, 64, 16]), ("ssm_c_re", [64, 16, 64]), ("ssm_c_im", [64, 16, 64]),
    ("ssm_d", [64, 16]), ("ssm_log_step", [64]), ("w_glu", [1024, 1024]), ("b_glu", [1024]),
    ("g_q", [64]), ("g_k", [64]), ("lambda_q1", [64]), ("lambda_k1", [64]), ("lambda_q2", [64]), ("lambda_k2", [64]),
    ("g_subln", [128]), ("w_out", [D, D]), ("w_mlp1", [D, 8192]), ("w_mlp2", [8192, D]),
    ("valid0", [128, 1]), ("kbias0", [128, 1]),
]


def build(NT=64, debug=False, phases="01234"):
    NTOK = NT * 128
    NOWN = NTOK // 2
    NQB = NT // 8
    NCT = NT // 8
    NB1 = NT // 4
    nc = bass.Bass("TRN2", target_bir_lowering=False)
    I = {}
    I["xs"] = nc.dram_tensor("xs", [NTOK, D], F32, kind="ExternalInput").ap()
    for n, shp in PARAMS:
        I[n] = nc.dram_tensor(n, shp, F32, kind="ExternalInput").ap()
    out = nc.dram_tensor("out", [NOWN, D], F32, kind="ExternalOutput").ap()
    sk = "ExternalOutput" if debug else "Internal"
    WIN = nc.dram_tensor("WIN", [D, 4096], BF16, kind="Internal").ap()
    WGLU = nc.dram_tensor("WGLU", [1024, 1024], BF16, kind="Internal").ap()
    WOUT = nc.dram_tensor("WOUT", [D, D], BF16, kind="Internal").ap()
    W1 = nc.dram_tensor("W1", [D, 8192], BF16, kind="Internal").ap()
    W2 = nc.dram_tensor("W2", [8192, D], BF16, kind="Internal").ap()
    KT = nc.dram_tensor("KT", [NH, 128, NTOK], BF16, kind=sk).ap()
    QT = nc.dram_tensor("QT", [NH, 128, NOWN], BF16, kind=sk).ap()
    Vd = nc.dram_tensor("Vd", [NTOK, 1024], BF16, kind=sk).ap()
    Ud = nc.dram_tensor("Ud", [NTOK, 1024], BF16, kind=sk).ap()
    YC = nc.dram_tensor("YC", [D, NOWN], BF16, kind=sk).ap()
    MODR = nc.dram_tensor("MODR", [6, D], F32, kind=sk).ap()

    with ExitStack() as pes:
        PT = lambda name, shape, dt: pes.enter_context(nc.sbuf_tensor(name, shape, dt))
        ident = PT("ident", [128, 128], BF16)
        identf = PT("identf", [128, 128], F32)
        onesf = PT("onesf", [128, 128], F32)
        onesb = PT("onesb", [128, 128], BF16)
        bd64 = PT("bd64", [128, 128], BF16)
        cact = PT("cact", [128, 16], F32)
        modv = PT("modv", [128, 96], F32)
        bada = PT("bada", [128, 96], F32)
        g1s = PT("g1s", [128, 16], F32)
        g2s = PT("g2s", [128, 16], F32)
        gtmp = PT("gtmp", [128, 16], F32)
        valid0 = PT("valid0s", [128, 1], F32)
        kbias0 = PT("kbias0s", [128, 1], F32)
        gq2 = PT("gq2", [128, 1], F32)
        gk2 = PT("gk2", [128, 1], F32)
        gsub = PT("gsub", [128, 1], F32)
        nlam = PT("nlam", [128, 1], F32)
        lamt = PT("lamt", [128, 4, 64], F32)
        lamr = PT("lamr", [128, 4], F32)

        def mod_vectors(S, T, P, vecs, tag):
            wt = [T(f"wada{tag}{i}", [128, 16, 256], F32) for i in range(2)]
            pm = P(f"pmod{tag}", [128, 96], F32)
            n = 0
            for v in vecs:
                for cb in range(8):
                    b = n % 2
                    n += 1
                    col0 = v * D + cb * 256
                    S.load(lambda e, b=b, col0=col0: e.dma_start(out=wt[b][:], in_=I["w_ada"][:, col0:col0 + 256].rearrange("(kb p) n -> p kb n", p=128)),
                           w=[f"wada{b}"])
                    for j in range(2):
                        col = v * 16 + cb * 2 + j
                        for kb in range(NKB):
                            S.pe(lambda e, b=b, j=j, kb=kb, col=col: e.matmul(pm[:, col:col + 1], lhsT=wt[b][:, kb, j * 128:(j + 1) * 128], rhs=cact[:, kb:kb + 1], start=(kb == 0), stop=(kb == NKB - 1)),
                                 r=[f"wada{b}", "cact"], w=["pmod"])
                S.dve(lambda e, v=v: e.tensor_tensor(out=modv[:, v * 16:(v + 1) * 16], in0=pm[:, v * 16:(v + 1) * 16], in1=bada[:, v * 16:(v + 1) * 16], op=ALU.add),
                      r=["pmod", "bada"], w=[f"modv{v}"])
                S.store(lambda e, v=v: e.dma_start(out=MODR[v].rearrange("(kb p) -> p kb", p=128), in_=modv[:, v * 16:(v + 1) * 16], allow_slow_non_contiguous=True),
                        r=[f"modv{v}"], w=[f"MODR{v}"])

        def convert(S, src, dst, rows, step=128):
            for r0 in range(0, rows, step):
                S.store(lambda e, r0=r0: e.dma_start(out=dst[r0:r0 + step, :], in_=src[r0:r0 + step, :]), w=[("cv", id(dst), r0)])

        if "0" in phases:
            with ExitStack() as es:
                T = lambda name, shape, dt: es.enter_context(nc.sbuf_tensor(name, shape, dt))
                P = lambda name, shape, dt: es.enter_context(nc.psum_tensor(name, shape, dt))
                S = Sched(nc, es, "p0")
                convert(S, I["w_in"], WIN, D)
                S.pool(lambda e: e.memset(onesf[:], 1.0), w=["onesf"])
                S.pool(lambda e: e.memset(onesb[:], 1.0), w=["onesb"])
                S.pool(lambda e: e.affine_select(out=identf[:], in_=onesf[:], pattern=[[1, 128]], compare_op=ALU.is_equal, fill=0.0, base=0, channel_multiplier=-1), r=["onesf"], w=["identf"])
                S.pool(lambda e: e.tensor_copy(out=ident[:], in_=identf[:]), r=["identf"], w=["ident"])
                S.pool(lambda e: e.memset(bd64[:], 0.0), w=["bd64"])
                S.pool(lambda e: e.memset(bd64[0:64, 0:64], 1.0), w=["bd64"])
                S.pool(lambda e: e.memset(bd64[64:128, 64:128], 1.0), w=["bd64"])
                S.load(lambda e: e.dma_start(out=cact[:], in_=I["c"].rearrange("(kb p) -> p kb", p=128), allow_slow_non_contiguous=True), w=["cact"])
                S.load(lambda e: e.dma_start(out=bada[:], in_=I["b_ada"].rearrange("(j p) -> p j", p=128), allow_slow_non_contiguous=True), w=["bada"])
                S.load(lambda e: e.dma_start(out=g1s[:], in_=I["g_norm_mix"].rearrange("(kb p) -> p kb", p=128), allow_slow_non_contiguous=True), w=["g1s"])
                S.load(lambda e: e.dma_start(out=g2s[:], in_=I["g_norm_mlp"].rearrange("(kb p) -> p kb", p=128), allow_slow_non_contiguous=True), w=["g2s"])
                S.load(lambda e: e.dma_start(out=valid0[:], in_=I["valid0"]), w=["valid0"])
                S.load(lambda e: e.dma_start(out=kbias0[:], in_=I["kbias0"]), w=["kbias0"])
                for hh in range(2):
                    S.load(lambda e, hh=hh: e.dma_start(out=gq2[64 * hh:64 * hh + 64, :], in_=I["g_q"].rearrange("(p o) -> p o", o=1)), w=[f"gq2{hh}"])
                    S.load(lambda e, hh=hh: e.dma_start(out=gk2[64 * hh:64 * hh + 64, :], in_=I["g_k"].rearrange("(p o) -> p o", o=1)), w=[f"gk2{hh}"])
                S.load(lambda e: e.dma_start(out=gsub[:], in_=I["g_subln"].rearrange("(p o) -> p o", o=1)), w=["gsub"])
                for i, nme in enumerate(["lambda_q1", "lambda_k1", "lambda_q2", "lambda_k2"]):
                    S.load(lambda e, i=i, nme=nme: e.dma_start(out=lamt[:, i, :], in_=I[nme].rearrange("(o n) -> o n", o=1).to_broadcast([128, 64])), w=[f"lamt{i}"])
                S.act(lambda e: e.activation(out=cact[:], in_=cact[:], func=AF.Silu), r=["cact"], w=["cact"])
                S.dve(lambda e: e.tensor_scalar(out=gq2[:], in0=gq2[:], scalar1=0.125, scalar2=None, op0=ALU.mult), r=["gq20", "gq21"], w=["gq2"])
                S.dve(lambda e: e.tensor_scalar(out=gsub[:], in0=gsub[:], scalar1=1.0 - LAMBDA_INIT, scalar2=None, op0=ALU.mult), r=["gsub"], w=["gsub"])
                S.dve(lambda e: e.tensor_tensor(out=lamt[:, 0, :], in0=lamt[:, 0, :], in1=lamt[:, 1, :], op=ALU.mult), r=["lamt0", "lamt1"], w=["lamt0"])
                S.dve(lambda e: e.tensor_tensor(out=lamt[:, 2, :], in0=lamt[:, 2, :], in1=lamt[:, 3, :], op=ALU.mult), r=["lamt2", "lamt3"], w=["lamt2"])
                S.dve(lambda e: e.tensor_reduce(out=lamr[:, 0:1], in_=lamt[:, 0, :], axis=AX.X, op=ALU.add), r=["lamt0"], w=["lamr0"])
                S.dve(lambda e: e.tensor_reduce(out=lamr[:, 1:2], in_=lamt[:, 2, :], axis=AX.X, op=ALU.add), r=["lamt2"], w=["lamr1"])
                S.act(lambda e: e.activation(out=lamr[:, 2:4], in_=lamr[:, 0:2], func=AF.Exp), r=["lamr0", "lamr1"], w=["lamr2"])
                S.dve(lambda e: e.tensor_tensor(out=nlam[:], in0=lamr[:, 3:4], in1=lamr[:, 2:3], op=ALU.subtract), r=["lamr2"], w=["nlam"])
                S.dve(lambda e: e.tensor_scalar(out=nlam[:], in0=nlam[:], scalar1=-LAMBDA_INIT, scalar2=None, op0=ALU.add), r=["nlam"], w=["nlam"])
                mod_vectors(S, T, P, [0, 1], "a")
                S.dve(lambda e: e.tensor_scalar(out=gtmp[:], in0=modv[:, 16:32], scalar1=1.0, scalar2=None, op0=ALU.add), r=["modv1"], w=["gtmp"])
                S.dve(lambda e: e.tensor_tensor(out=g1s[:], in0=g1s[:], in1=gtmp[:], op=ALU.mult), r=["gtmp", "g1s"], w=["g1s"])
                S.run()

        if "1" in phases:
            with ExitStack() as es:
                T = lambda name, shape, dt: es.enter_context(nc.sbuf_tensor(name, shape, dt))
                P = lambda name, shape, dt: es.enter_context(nc.psum_tensor(name, shape, dt))
                S = Sched(nc, es, "p1")
                win = T("win", [128, NKB, 4096], BF16)
                xt = [T(f"xt{i}", [128, D], F32) for i in range(2)]
                junk = T("junk", [128, D], BF16)
                xn = [T(f"xn{i}", [128, D], BF16) for i in range(2)]
                ss = T("ss", [128, 2], F32)
                rs = T("rs", [128, 2], F32)
                hn = [T(f"hn{i}", [128, NKB, 512], BF16) for i in range(2)]
                sqb = [T(f"sqb{i}", [128, 512], BF16) for i in range(2)]
                lf = [T(f"lf{i}", [128, 512], F32) for i in range(2)]
                ko = [T(f"ko{i}", [128, 512], BF16) for i in range(2)]
                vo = [T(f"vo{i}", [128, 1024], BF16) for i in range(2)]
                pt = [P(f"pt{i}", [128, 4, 128], BF16) for i in range(2)]
                pk = [P(f"pk{i}", [128, 512], F32) for i in range(3)]
                pn = [P(f"pn{i}", [128, 512], F32) for i in range(2)]
                for kb in range(NKB):
                    S.load(lambda e, kb=kb: e.dma_start(out=win[:, kb, :], in_=WIN[kb * 128:(kb + 1) * 128, :]), w=[f"win{kb}"])
                WINR = [f"win{kb}" for kb in range(NKB)]
                convert(S, I["w_glu"], WGLU, 1024)
                convert(S, I["w_out"], WOUT, D)
                convert(S, I["w_mlp1"], W1, D)
                convert(S, I["w_mlp2"], W2, 8192, step=512)
                npk = 0
                nep = 0
                nvo = 0
                for b in range(NB1):
                    hb = b % 2
                    for i in range(4):
                        tg = b * 4 + i
                        xb = tg % 2
                        S.load(lambda e, xb=xb, tg=tg: e.dma_start(out=xt[xb][:], in_=I["xs"][tg * 128:(tg + 1) * 128, :]), w=[f"xt{xb}"])
                        S.act(lambda e, xb=xb: e.activation(out=junk[:], in_=xt[xb][:], func=AF.Square, accum_out=ss[:, xb:xb + 1]), r=[f"xt{xb}"], w=["junk", f"ss{xb}"])
                        S.act(lambda e, xb=xb: e.activation(out=rs[:, xb:xb + 1], in_=ss[:, xb:xb + 1], func=AF.Ln, scale=1.0 / D, bias=EPS), r=[f"ss{xb}"], w=[f"rs{xb}"])
                        S.act(lambda e, xb=xb: e.activation(out=rs[:, xb:xb + 1], in_=rs[:, xb:xb + 1], func=AF.Exp, scale=-0.5), r=[f"rs{xb}"], w=[f"rs{xb}"])
                        S.act(lambda e, xb=xb: e.activation(out=xn[xb][:], in_=xt[xb][:], func=AF.Copy, scale=rs[:, xb:xb + 1]), r=[f"xt{xb}", f"rs{xb}"], w=[f"xn{xb}"])
                        for q in range(4):
                            pb = q % 2
                            for j in range(4):
                                kb = q * 4 + j
                                S.pe(lambda e, xb=xb, kb=kb, pb=pb, j=j: e.transpose(out=pt[pb][:, j, :], in_=xn[xb][:, kb * 128:(kb + 1) * 128], identity=ident[:]),
                                     r=[f"xn{xb}"], w=[f"pt{pb}"])
                            for j in range(4):
                                kb = q * 4 + j
                                S.dve(lambda e, hb=hb, i=i, kb=kb, pb=pb, j=j: e.tensor_scalar(out=hn[hb][:, kb, i * 128:(i + 1) * 128], in0=pt[pb][:, j, :], scalar1=g1s[:, kb:kb + 1], scalar2=modv[:, kb:kb + 1], op0=ALU.mult, op1=ALU.add),
                                      r=[f"pt{pb}"], w=[f"hn{hb}"])
                    for which in ("k", "q"):
                        for h in range(NH):
                            pkb = npk % 3
                            npk += 1
                            eb = nep % 2
                            nep += 1
                            if which == "k":
                                N = 512
                                col0 = 2048 + h * 128
                                rhs_of = lambda kb, hb=hb: hn[hb][:, kb, :]
                                gcol = gk2
                            else:
                                N = 256
                                col0 = 1024 + h * 128
                                rhs_of = lambda kb, hb=hb: hn[hb][:, kb, :].rearrange("p (t two i) -> p t two i", two=2, i=64)[:, :, 1, :]
                                gcol = gq2
                            for kb in range(NKB):
                                S.pe(lambda e, kb=kb, pkb=pkb, col0=col0, N=N, rhs_of=rhs_of: e.matmul(pk[pkb][:, 0:N], lhsT=win[:, kb, col0:col0 + 128], rhs=rhs_of(kb), start=(kb == 0), stop=(kb == NKB - 1)),
                                     r=[f"hn{hb}", f"win{kb}"], w=[f"pk{pkb}"])
                            S.act(lambda e, pkb=pkb, eb=eb, N=N: e.activation(out=sqb[eb][:, 0:N], in_=pk[pkb][:, 0:N], func=AF.Square), r=[f"pk{pkb}"], w=[f"sqb{eb}"])
                            S.pe(lambda e, eb=eb, N=N: e.matmul(pn[eb][:, 0:N], lhsT=bd64[:], rhs=sqb[eb][:, 0:N], start=True, stop=True), r=[f"sqb{eb}"], w=[f"pn{eb}"])
                            S.act(lambda e, eb=eb, N=N: e.activation(out=lf[eb][:, 0:N], in_=pn[eb][:, 0:N], func=AF.Ln, scale=1.0 / 64, bias=EPS), r=[f"pn{eb}"], w=[f"lf{eb}"])
                            S.act(lambda e, eb=eb, N=N: e.activation(out=lf[eb][:, 0:N], in_=lf[eb][:, 0:N], func=AF.Exp, scale=-0.5), r=[f"lf{eb}"], w=[f"lf{eb}"])
                            S.dve(lambda e, pkb=pkb, eb=eb, N=N, gcol=gcol: e.scalar_tensor_tensor(out=ko[eb][:, 0:N], in0=pk[pkb][:, 0:N], scalar=gcol[:, 0:1], in1=lf[eb][:, 0:N], op0=ALU.mult, op1=ALU.mult),
                                  r=[f"pk{pkb}", f"lf{eb}"], w=[f"ko{eb}"])
                            if which == "k":
                                S.store(lambda e, eb=eb, h=h, b=b: e.dma_start(out=KT[h][:, b * 512:(b + 1) * 512], in_=ko[eb][:, 0:512]), r=[f"ko{eb}"], w=[("KT", h, b)])
                            else:
                                S.store(lambda e, eb=eb, h=h, b=b: e.dma_start(out=QT[h][:, b * 256:(b + 1) * 256], in_=ko[eb][:, 0:256]), r=[f"ko{eb}"], w=[("QT", h, b)])
                    for which in ("v", "u"):
                        for i in range(4):
                            tg = b * 4 + i
                            vb = nvo % 2
                            nvo += 1
                            for nb in range(2):
                                pkb = npk % 3
                                npk += 1
                                col0 = (3072 if which == "v" else 0) + nb * 512
                                for kb in range(NKB):
                                    S.pe(lambda e, kb=kb, pkb=pkb, col0=col0, i=i, hb=hb: e.matmul(pk[pkb][:], lhsT=hn[hb][:, kb, i * 128:(i + 1) * 128], rhs=win[:, kb, col0:col0 + 512], start=(kb == 0), stop=(kb == NKB - 1)),
                                         r=[f"hn{hb}", f"win{kb}"], w=[f"pk{pkb}"])
                                if which == "u" and tg == 0:
                                    S.act(lambda e, pkb=pkb, vb=vb, nb=nb: e.activation(out=vo[vb][:, nb * 512:(nb + 1) * 512], in_=pk[pkb][:], func=AF.Copy, scale=valid0[:, 0:1]), r=[f"pk{pkb}"], w=[f"vo{vb}"])
                                else:
                                    S.act(lambda e, pkb=pkb, vb=vb, nb=nb: e.activation(out=vo[vb][:, nb * 512:(nb + 1) * 512], in_=pk[pkb][:], func=AF.Copy), r=[f"pk{pkb}"], w=[f"vo{vb}"])
                            dst = Vd if which == "v" else Ud
                            S.store(lambda e, vb=vb, tg=tg, dst=dst: e.dma_start(out=dst[tg * 128:(tg + 1) * 128, :], in_=vo[vb][:]), r=[f"vo{vb}"], w=[(which, tg)])
                S.run()

        if "2" in phases or "a" in phases or "b" in phases:
            with ExitStack() as wes:
                WT = lambda name, shape, dt: wes.enter_context(nc.sbuf_tensor(name, shape, dt))
                Tm = WT("Tm", [128, 64, 128], BF16)
                WXr = WT("WXr", [128, 64, 64], BF16)
                WXi = WT("WXi", [128, 64, 64], BF16)
                WYr = WT("WYr", [64, 64, 128], BF16)
                WYi = WT("WYi", [64, 64, 128], BF16)
                A8r = WT("A8r", [64, 64], F32)
                A8i = WT("A8i", [64, 64], F32)
                wglu = WT("wglu", [128, 8, 1024], BF16)
                bglu = WT("bglu", [128, 8], F32)
                with ExitStack() as es:
                    T = lambda name, shape, dt: es.enter_context(nc.sbuf_tensor(name, shape, dt))
                    P = lambda name, shape, dt: es.enter_context(nc.psum_tensor(name, shape, dt))
                    S = Sched(nc, es, "p2v")
                    mod_vectors(S, T, P, [2, 3, 4, 5], "b")
                    S.dve(lambda e: e.tensor_scalar(out=gtmp[:], in0=modv[:, 64:80], scalar1=1.0, scalar2=None, op0=ALU.add), r=["modv4"], w=["gtmp"])
                    S.dve(lambda e: e.tensor_tensor(out=g2s[:], in0=g2s[:], in1=gtmp[:], op=ALU.mult), r=["gtmp"], w=["g2s"])
                    S.run()
                with ExitStack() as es:
                  if "2" in phases or "b" in phases:
                      T = lambda name, shape, dt: es.enter_context(nc.sbuf_tensor(name, shape, dt))
                      P = lambda name, shape, dt: es.enter_context(nc.psum_tensor(name, shape, dt))
                      S = Sched(nc, es, "p2s")
                      import os as _os
                      CUT = int(_os.environ.get("P2S_CUT", "99"))
                      class _Stop(Exception): pass
                      def stage(n):
                          if CUT < n: raise _Stop()
                      try:
                          for cb in range(8):
                              S.load(lambda e, cb=cb: e.dma_start(out=wglu[:, cb, :], in_=WGLU[cb * 128:(cb + 1) * 128, :]), w=[f"wglu{cb}"])
                          S.load(lambda e: e.dma_start(out=bglu[:], in_=I["b_glu"].rearrange("(j p) -> p j", p=128), allow_slow_non_contiguous=True), w=["bglu"])
                          lr = T("lr", [64, 64], F32); li = T("li", [64, 64], F32); ls = T("ls", [64, 64], F32)
                          Br = T("Br", [64, 64, 16], F32); Bi = T("Bi", [64, 64, 16], F32)
                          Cn = [T("Cnr", [128, 8, 64], F32), T("Cni", [128, 8, 64], F32)]
                          Cp = [T("Cpr", [64, 64, 16], F32), T("Cpi", [64, 64, 16], F32)]
                          dvec = T("dvec", [128, 64], F32)
                          mask01 = T("mask01", [128, 128], F32)
                          S.load(lambda e: e.dma_start(out=lr[:], in_=I["ssm_lambda_re"].rearrange("g p -> p g"), allow_slow_non_contiguous=True), w=["lr"])
                          S.load(lambda e: e.dma_start(out=li[:], in_=I["ssm_lambda_im"].rearrange("g p -> p g"), allow_slow_non_contiguous=True), w=["li"])
                          S.load(lambda e: e.dma_start(out=ls[:], in_=I["ssm_log_step"].rearrange("(o g) -> o g", o=1).to_broadcast([64, 64])), w=["ls"])
                          S.load(lambda e: e.dma_start(out=Br[:], in_=I["ssm_b_re"].rearrange("g p h -> p g h")), w=["Br"])
                          S.load(lambda e: e.dma_start(out=Bi[:], in_=I["ssm_b_im"].rearrange("g p h -> p g h")), w=["Bi"])
                          for ri, nme in enumerate(["ssm_c_re", "ssm_c_im"]):
                              S.load(lambda e, ri=ri, nme=nme: e.dma_start(out=Cn[ri][:], in_=I[nme].rearrange("(gb g8) q p -> (g8 q) gb p", g8=8)), w=[f"Cn{ri}"])
                          for s in range(8):
                              S.load(lambda e, s=s: e.dma_start(out=dvec[16 * s:16 * s + 16, :], in_=I["ssm_d"].rearrange("g h -> h g"), allow_slow_non_contiguous=True), w=[f"dvec{s}"])
                          DVR = [f"dvec{s}" for s in range(8)]
                          S.pool(lambda e: e.affine_select(out=mask01[:], in_=onesf[:], pattern=[[16, 8], [0, 16]], compare_op=ALU.is_ge, fill=0.0, base=15, channel_multiplier=-1), w=["mask01"])
                          pc = [P(f"pc{i}", [64, 128], F32) for i in range(2)]
                          n = 0
                          for ri in range(2):
                              for gb in range(8):
                                  b = n % 2
                                  n += 1
                                  S.pe(lambda e, ri=ri, gb=gb, b=b: e.transpose(out=pc[b][:], in_=Cn[ri][:, gb, :], identity=identf[:]), r=[f"Cn{ri}"], w=[f"pc{b}"])
                                  S.act(lambda e, ri=ri, gb=gb, b=b: e.activation(out=Cp[ri][:, gb * 8:(gb + 1) * 8, :].rearrange("p g q -> p (g q)"), in_=pc[b][:], func=AF.Copy), r=[f"pc{b}"], w=[f"Cp{ri}"])
                          stage(2)
                          NK = 25
                          kvi = T("kvi", [64, NK], I32); kv = T("kv", [64, NK], F32)
                          S.pool(lambda e: e.iota(kvi[:, 0:8], pattern=[[-1, 8]], base=0, channel_multiplier=0), w=["kvi"])
                          S.pool(lambda e: e.iota(kvi[:, 8:16], pattern=[[-1, 8]], base=7, channel_multiplier=0), w=["kvi"])
                          S.pool(lambda e: e.iota(kvi[:, 16:25], pattern=[[1, 9]], base=0, channel_multiplier=0), w=["kvi"])
                          S.dve(lambda e: e.tensor_copy(out=kv[:], in_=kvi[:]), r=["kvi"], w=["kv"])
                          dt_ = T("dt_", [64, 64], F32); mu = T("mu", [64, 64], F32); th = T("th", [64, 64], F32)
                          S.act(lambda e: e.activation(out=dt_[:], in_=ls[:], func=AF.Exp), r=["ls"], w=["dt"])
                          S.dve(lambda e: e.tensor_tensor(out=mu[:], in0=dt_[:], in1=lr[:], op=ALU.mult), r=["dt", "lr"], w=["mu"])
                          S.dve(lambda e: e.tensor_tensor(out=th[:], in0=dt_[:], in1=li[:], op=ALU.mult), r=["dt", "li"], w=["th"])
                          shp = [64, 64, NK]
                          ANG = T("ANG", shp, F32); MAG = T("MAG", shp, F32); V0 = T("V0", shp, F32); V1 = T("V1", shp, F32)
                          VI = T("VI", shp, I32); AR = T("AR", shp, F32); AI = T("AI", shp, F32)
                          kvb = lambda: kv[:].unsqueeze(1).to_broadcast(shp)
                          S.dve(lambda e: e.tensor_tensor(out=ANG[:], in0=th[:].unsqueeze(2).to_broadcast(shp), in1=kvb(), op=ALU.mult), r=["th", "kv"], w=["ANG"])
                          S.dve(lambda e: e.tensor_tensor(out=MAG[:], in0=mu[:].unsqueeze(2).to_broadcast(shp), in1=kvb(), op=ALU.mult), r=["mu", "kv"], w=["MAG"])
                          S.act(lambda e: e.activation(out=MAG[:], in_=MAG[:], func=AF.Exp), r=["MAG"], w=["MAG"])
                          for which, off, dst in (("s", 64.0, AI), ("c", 64.25, AR)):
                              S.dve(lambda e, off=off: e.tensor_scalar(out=V0[:], in0=ANG[:], scalar1=1.0 / (2 * math.pi), scalar2=off, op0=ALU.mult, op1=ALU.add), r=["ANG"], w=["V0"])
                              S.dve(lambda e: e.tensor_copy(out=VI[:], in_=V0[:]), r=["V0"], w=["VI"])
                              S.dve(lambda e: e.tensor_copy(out=V1[:], in_=VI[:]), r=["VI"], w=["V1"])
                              S.dve(lambda e: e.tensor_tensor(out=V0[:], in0=V0[:], in1=V1[:], op=ALU.subtract), r=["V0", "V1"], w=["V0"])
                              S.dve(lambda e: e.tensor_scalar(out=V1[:], in0=V0[:], scalar1=0.5, scalar2=None, op0=ALU.is_gt), r=["V0"], w=["V1"])
                              S.dve(lambda e: e.tensor_tensor(out=V0[:], in0=V0[:], in1=V1[:], op=ALU.subtract), r=["V0", "V1"], w=["V0"])
                              S.dve(lambda e: e.tensor_scalar(out=V1[:], in0=V0[:], scalar1=-0.5, scalar2=None, op0=ALU.is_lt), r=["V0"], w=["V1"])
                              S.dve(lambda e: e.tensor_tensor(out=V0[:], in0=V0[:], in1=V1[:], op=ALU.add), r=["V0", "V1"], w=["V0"])
                              S.act(lambda e, dst=dst: e.activation(out=dst[:], in_=V0[:], func=AF.Sin, scale=6.283184), r=["V0"], w=[which + "in"])
                              S.dve(lambda e, dst=dst: e.tensor_tensor(out=dst[:], in0=dst[:], in1=MAG[:], op=ALU.mult), r=[which + "in", "MAG"], w=["A" + which])
                          AW = ["As", "Ac"]
                          S.act(lambda e: e.activation(out=A8r[:], in_=AR[:, :, 24], func=AF.Copy), r=AW, w=["A8r"])
                          S.act(lambda e: e.activation(out=A8i[:], in_=AI[:, :, 24], func=AF.Copy), r=AW, w=["A8i"])
                          stage(3)
                          zr = T("zr", [64, 64], F32); den = T("den", [64, 64], F32); t0 = T("t0", [64, 64], F32); t1 = T("t1", [64, 64], F32)
                          kr = T("kr", [64, 64], F32); ki = T("ki", [64, 64], F32)
                          S.dve(lambda e: e.tensor_scalar(out=zr[:], in0=AR[:, :, 17], scalar1=-1.0, scalar2=None, op0=ALU.add), r=AW, w=["zr"])
                          S.dve(lambda e: e.tensor_tensor(out=den[:], in0=lr[:], in1=lr[:], op=ALU.mult), r=["lr"], w=["den"])
                          S.dve(lambda e: e.tensor_tensor(out=t0[:], in0=li[:], in1=li[:], op=ALU.mult), r=["li"], w=["t0"])
                          S.dve(lambda e: e.tensor_tensor(out=den[:], in0=den[:], in1=t0[:], op=ALU.add), r=["den", "t0"], w=["den"])
                          S.dve(lambda e: e.reciprocal(out=den[:], in_=den[:]), r=["den"], w=["den"])
                          S.dve(lambda e: e.tensor_tensor(out=t0[:], in0=zr[:], in1=lr[:], op=ALU.mult), r=["zr", "lr"], w=["t0"])
                          S.dve(lambda e: e.tensor_tensor(out=t1[:], in0=AI[:, :, 17], in1=li[:], op=ALU.mult), r=AW + ["li"], w=["t1"])
                          S.dve(lambda e: e.tensor_tensor(out=t0[:], in0=t0[:], in1=t1[:], op=ALU.add), r=["t0", "t1"], w=["t0"])
                          S.dve(lambda e: e.tensor_tensor(out=kr[:], in0=t0[:], in1=den[:], op=ALU.mult), r=["t0", "den"], w=["kr"])
                          S.dve(lambda e: e.tensor_tensor(out=t0[:], in0=AI[:, :, 17], in1=lr[:], op=ALU.mult), r=AW + ["lr"], w=["t0"])
                          S.dve(lambda e: e.tensor_tensor(out=t1[:], in0=zr[:], in1=li[:], op=ALU.mult), r=["zr", "li"], w=["t1"])
                          S.dve(lambda e: e.tensor_tensor(out=t0[:], in0=t0[:], in1=t1[:], op=ALU.subtract), r=["t0", "t1"], w=["t0"])
                          S.dve(lambda e: e.tensor_tensor(out=ki[:], in0=t0[:], in1=den[:], op=ALU.mult), r=["t0", "den"], w=["ki"])
                          cr_ = T("cr_", [64, 64, 16], F32); ci_ = T("ci_", [64, 64, 16], F32); c0 = T("c0", [64, 64, 16], F32)
                          s16 = [64, 64, 16]
                          krb = lambda: kr[:].unsqueeze(2).to_broadcast(s16)
                          kib = lambda: ki[:].unsqueeze(2).to_broadcast(s16)
                          S.dve(lambda e: e.tensor_tensor(out=cr_[:], in0=AR[:, :, 0:16], in1=krb(), op=ALU.mult), r=AW + ["kr"], w=["cr"])
                          S.dve(lambda e: e.tensor_tensor(out=c0[:], in0=AI[:, :, 0:16], in1=kib(), op=ALU.mult), r=AW + ["ki"], w=["c0"])
                          S.dve(lambda e: e.tensor_tensor(out=cr_[:], in0=cr_[:], in1=c0[:], op=ALU.subtract), r=["cr", "c0"], w=["cr"])
                          S.dve(lambda e: e.tensor_tensor(out=ci_[:], in0=AR[:, :, 0:16], in1=kib(), op=ALU.mult), r=AW + ["ki"], w=["ci"])
                          S.dve(lambda e: e.tensor_tensor(out=c0[:], in0=AI[:, :, 0:16], in1=krb(), op=ALU.mult), r=AW + ["kr"], w=["c0"])
                          S.dve(lambda e: e.tensor_tensor(out=ci_[:], in0=ci_[:], in1=c0[:], op=ALU.add), r=["ci", "c0"], w=["ci"])
                          stage(4)
                          GC = 4
                          se = [64, GC, 8, 16]
                          Er = T("Er", se, F32); Ei = T("Ei", se, F32); Xr_ = T("Xr_", se, F32); Xi_ = T("Xi_", se, F32)
                          u0 = T("u0", se, F32); u1 = T("u1", se, F32)
                          sg9 = [64, GC, 9, 16]
                          Gr = T("Gr", sg9, F32); Gm = T("Gm", sg9, F32); w0 = T("w0", sg9, F32); w1 = T("w1", sg9, F32)
                          tmk = [T(f"tmk{i}", [128, 128], F32) for i in range(2)]
                          pT = [P(f"pT{i}", [128, 128], F32) for i in range(2)]
                          pX = [P(f"pX{i}", [128, 64], BF16) for i in range(2)]
                          Xrb = T("Xrb", se, BF16); Xib = T("Xib", se, BF16)
                          nT = 0
                          nX = 0
                          for gc in range(64 // GC):
                              g0 = gc * GC
                              gs = slice(g0, g0 + GC)
                              for (o0, dr, di, tag) in ((0, Er, Ei, "E"), (8, Xr_, Xi_, "X")):
                                  cb_ = lambda t_, o0=o0, gs=gs: t_[:, gs, o0:o0 + 8].unsqueeze(3).to_broadcast(se)
                                  bb_ = lambda t_, gs=gs: t_[:, gs, :].unsqueeze(2).to_broadcast(se)
                                  S.dve(lambda e, cb_=cb_, bb_=bb_: e.tensor_tensor(out=u0[:], in0=cb_(cr_), in1=bb_(Br), op=ALU.mult), r=["cr", "Br"], w=["u0"])
                                  S.dve(lambda e, cb_=cb_, bb_=bb_: e.tensor_tensor(out=u1[:], in0=cb_(ci_), in1=bb_(Bi), op=ALU.mult), r=["ci", "Bi"], w=["u1"])
                                  S.dve(lambda e, dr=dr: e.tensor_tensor(out=dr[:], in0=u0[:], in1=u1[:], op=ALU.subtract), r=["u0", "u1"], w=[tag + "r"])
                                  S.dve(lambda e, cb_=cb_, bb_=bb_: e.tensor_tensor(out=u0[:], in0=cb_(cr_), in1=bb_(Bi), op=ALU.mult), r=["cr", "Bi"], w=["u0"])
                                  S.dve(lambda e, cb_=cb_, bb_=bb_: e.tensor_tensor(out=u1[:], in0=cb_(ci_), in1=bb_(Br), op=ALU.mult), r=["ci", "Br"], w=["u1"])
                                  S.dve(lambda e, di=di: e.tensor_tensor(out=di[:], in0=u0[:], in1=u1[:], op=ALU.add), r=["u0", "u1"], w=[tag + "i"])
                              S.act(lambda e: e.activation(out=Xrb[:], in_=Xr_[:], func=AF.Copy), r=["Xr"], w=["Xrb"])
                              S.act(lambda e: e.activation(out=Xib[:], in_=Xi_[:], func=AF.Copy), r=["Xi"], w=["Xib"])
                              stage(5)
                              ab_ = lambda t_, gs=gs: t_[:, gs, 16:25].unsqueeze(3).to_broadcast(sg9)
                              cc_ = lambda t_, gs=gs: t_[:, gs, :].unsqueeze(2).to_broadcast(sg9)
                              S.dve(lambda e, ab_=ab_, cc_=cc_: e.tensor_tensor(out=w0[:], in0=ab_(AR), in1=cc_(Cp[0]), op=ALU.mult), r=AW + ["Cp0"], w=["w0"])
                              S.dve(lambda e, ab_=ab_, cc_=cc_: e.tensor_tensor(out=w1[:], in0=ab_(AI), in1=cc_(Cp[1]), op=ALU.mult), r=AW + ["Cp1"], w=["w1"])
                              S.dve(lambda e: e.tensor_tensor(out=Gr[:], in0=w0[:], in1=w1[:], op=ALU.subtract), r=["w0", "w1"], w=["Gr"])
                              S.dve(lambda e, ab_=ab_, cc_=cc_: e.tensor_tensor(out=w0[:], in0=ab_(AI), in1=cc_(Cp[0]), op=ALU.mult), r=AW + ["Cp0"], w=["w0"])
                              S.dve(lambda e, ab_=ab_, cc_=cc_: e.tensor_tensor(out=w1[:], in0=ab_(AR), in1=cc_(Cp[1]), op=ALU.mult), r=AW + ["Cp1"], w=["w1"])
                              S.dve(lambda e: e.scalar_tensor_tensor(out=Gm[:], in0=w0[:], scalar=-1.0, in1=w1[:], op0=ALU.mult, op1=ALU.subtract), r=["w0", "w1"], w=["Gm"])
                              stage(6)
                              S.act(lambda e, gs=gs: e.activation(out=WYr[:, gs, :].rearrange("p g (t q) -> p g t q", q=16), in_=Gr[:, :, 1:9, :], func=AF.Copy), r=["Gr"], w=["WYr"])
                              S.act(lambda e, gs=gs: e.activation(out=WYi[:, gs, :].rearrange("p g (t q) -> p g t q", q=16), in_=Gm[:, :, 1:9, :], func=AF.Copy), r=["Gm"], w=["WYi"])
                              stage(7)
                              for gl in range(GC):
                                  g = g0 + gl
                                  b = nT % 2
                                  nT += 1
                                  S.pe(lambda e, gl=gl, b=b: e.matmul(pT[b][:], lhsT=Er[:, gl, :, :].rearrange("p s h -> p (s h)"), rhs=Gr[:, gl, 0:8, :].rearrange("p t q -> p (t q)"), start=True, stop=False),
                                       r=["Er", "Gr"], w=[f"pT{b}"])
                                  S.pe(lambda e, gl=gl, b=b: e.matmul(pT[b][:], lhsT=Ei[:, gl, :, :].rearrange("p s h -> p (s h)"), rhs=Gm[:, gl, 0:8, :].rearrange("p t q -> p (t q)"), start=False, stop=True),
                                       r=["Ei", "Gm"], w=[f"pT{b}"])
                                  S.dve(lambda e, b=b: e.tensor_tensor(out=tmk[b][:], in0=pT[b][:], in1=mask01[:], op=ALU.mult), r=[f"pT{b}", "mask01"], w=[f"tmk{b}"])
                                  S.dve(lambda e, b=b, g=g: e.scalar_tensor_tensor(out=Tm[:, g, :], in0=identf[:], scalar=dvec[:, g:g + 1], in1=tmk[b][:], op0=ALU.mult, op1=ALU.add),
                                        r=[f"tmk{b}"] + DVR, w=["Tm"])
                                  stage(8)
                                  for (src, dstw, tag) in ((Xrb, WXr, "Xrb"), (Xib, WXi, "Xib")):
                                      bx = nX % 2
                                      nX += 1
                                      S.pe(lambda e, gl=gl, bx=bx, src=src: e.transpose(out=pX[bx][:], in_=src[:, gl, :, :].rearrange("p s h -> p (s h)"), identity=ident[0:64, 0:64]),
                                           r=[tag], w=[f"pX{bx}"])
                                      S.act(lambda e, bx=bx, g=g, dstw=dstw: e.activation(out=dstw[:, g, :], in_=pX[bx][:], func=AF.Copy), r=[f"pX{bx}"], w=["W" + tag])

                      except _Stop:
                          pass
                      S.run()

                with ExitStack() as es:
                  if "2" in phases:
                      T = lambda name, shape, dt: es.enter_context(nc.sbuf_tensor(name, shape, dt))
                      P = lambda name, shape, dt: es.enter_context(nc.psum_tensor(name, shape, dt))
                      S = Sched(nc, es, "p2m")
                      WN = 8
                      Ucm = T("Ucm", [128, 8, 1024], BF16)
                      Ug = T("Ug", [128, 64, 128], BF16)
                      Xr = T("Xr", [64, 64, 128], BF16)
                      Xi = T("Xi", [64, 64, 128], BF16)
                      Hw = {(ri, w): T(f"Hw{ri}{w}", [64, 64, WN + 1], F32) for ri in range(2) for w in range(2)}
                      Hb = [T(f"Hb{ri}", [64, 64, 64], BF16) for ri in range(2)]
                      Uc2 = T("Uc2", [128, 32, 8, 16], BF16)
                      sc = [T(f"sc{i}", [64, 64], F32) for i in range(4)]
                      Yg = Ucm[0:64]
                      Yfm = T("Yfm", [128, 8, 512], BF16)
                      sgm = [T(f"sgm{i}", [128, 512], BF16) for i in range(2)]
                      yo = [T(f"yo{i}", [128, 512], BF16) for i in range(2)]
                      ptr = [P(f"ptr{i}", [128, 4, 128], BF16) for i in range(2)]
                      pXr = P("pXr", [64, 4, 128], F32)
                      pXi = P("pXi", [64, 4, 128], F32)
                      pY = [P(f"pY{i}", [64, 4, 128], F32) for i in range(2)]
                      pZ = [P(f"pZ{i}", [128, 512], F32) for i in range(2)]
                      for ri in range(2):
                          S.dve(lambda e, ri=ri: e.memset(Hw[(ri, 1)][:, :, WN], 0.0), w=[("hw", ri, 1)])
                      nq = 0
                      for ct in range(NCT):
                          Uv = Ud.rearrange("(ct tt hf cc s) d -> ct hf tt cc s d", tt=8, hf=2, cc=8, s=8)
                          for hf in range(2):
                              for tt in range(8):
                                  S.load(lambda e, ct=ct, hf=hf, tt=tt: e.dma_start(out=Ucm[hf * 64 + tt * 8:hf * 64 + tt * 8 + 8, :, :], in_=Uv[ct, hf, tt]), w=[("Ucm", hf, tt)])
                          UCM = [("Ucm", hf, tt) for hf in range(2) for tt in range(8)]
                          for gh in range(2):
                              S.pool(lambda e, gh=gh: e.tensor_copy(out=Uc2[:], in_=Ucm[:, :, gh * 512:(gh + 1) * 512].rearrange("c s (g h) -> c g s h", h=16)), r=UCM, w=["Uc2"])
                              for gq in range(gh * 8, gh * 8 + 8):
                                  b = gq % 2
                                  for j in range(4):
                                      gl = (gq - gh * 8) * 4 + j
                                      S.pe(lambda e, gl=gl, b=b, j=j: e.transpose(out=ptr[b][:, j, :], in_=Uc2[:, gl, :, :].rearrange("c s h -> c (s h)"), identity=ident[:]), r=["Uc2"], w=[f"ptr{b}"])
                                  if gq % 2 == 0:
                                      S.act(lambda e, gq=gq, b=b: e.activation(out=Ug[:, gq * 4:gq * 4 + 4, :], in_=ptr[b][:], func=AF.Copy), r=[f"ptr{b}"], w=[("Ug", gq)])
                                  else:
                                      S.dve(lambda e, gq=gq, b=b: e.tensor_copy(out=Ug[:, gq * 4:gq * 4 + 4, :], in_=ptr[b][:]), r=[f"ptr{b}"], w=[("Ug", gq)])
                          for gq in range(16):
                              for j in range(4):
                                  g = gq * 4 + j
                                  S.pe(lambda e, g=g, j=j: e.matmul(pXr[:, j, :], lhsT=WXr[:, g, :], rhs=Ug[:, g, :], start=True, stop=True), r=[("Ug", gq)], w=["pXr"])
                                  S.pe(lambda e, g=g, j=j: e.matmul(pXi[:, j, :], lhsT=WXi[:, g, :], rhs=Ug[:, g, :], start=True, stop=True), r=[("Ug", gq)], w=["pXi"])
                              S.act(lambda e, gq=gq: e.activation(out=Xr[:, gq * 4:gq * 4 + 4, :], in_=pXr[:], func=AF.Copy), r=["pXr"], w=["Xr"])
                              S.act(lambda e, gq=gq: e.activation(out=Xi[:, gq * 4:gq * 4 + 4, :], in_=pXi[:], func=AF.Copy), r=["pXi"], w=["Xi"])
                          for c in range(128):
                              w = (c // WN) % 2
                              k = c % WN
                              if k == 0:
                                  prv = lambda ri, w=w: Hw[(ri, 1 - w)][:, :, WN]
                                  rk = lambda ri, w=w: ("hw", ri, 1 - w)
                              else:
                                  prv = lambda ri, w=w, k=k: Hw[(ri, w)][:, :, k]
                                  rk = lambda ri, w=w: ("hw", ri, w)
                              S.dve(lambda e, prv=prv: e.tensor_tensor(out=sc[0][:], in0=A8r[:], in1=prv(0), op=ALU.mult), r=[rk(0)], w=["sc0"])
                              S.dve(lambda e, prv=prv: e.tensor_tensor(out=sc[1][:], in0=A8i[:], in1=prv(1), op=ALU.mult), r=[rk(1)], w=["sc1"])
                              S.dve(lambda e, prv=prv: e.tensor_tensor(out=sc[2][:], in0=A8r[:], in1=prv(1), op=ALU.mult), r=[rk(1)], w=["sc2"])
                              S.dve(lambda e, prv=prv: e.tensor_tensor(out=sc[3][:], in0=A8i[:], in1=prv(0), op=ALU.mult), r=[rk(0)], w=["sc3"])
                              S.dve(lambda e: e.tensor_tensor(out=sc[0][:], in0=sc[0][:], in1=sc[1][:], op=ALU.subtract), r=["sc0", "sc1"], w=["sc0"])
                              S.dve(lambda e: e.tensor_tensor(out=sc[2][:], in0=sc[2][:], in1=sc[3][:], op=ALU.add), r=["sc2", "sc3"], w=["sc2"])
                              cp = (c // 16) * 8 + (c % 8) + (64 if (c % 16) >= 8 else 0)
                              S.dve(lambda e, w=w, k=k, cp=cp: e.tensor_tensor(out=Hw[(0, w)][:, :, k + 1], in0=sc[0][:], in1=Xr[:, :, cp], op=ALU.add), r=["sc0", "Xr"], w=[("hw", 0, w)])
                              S.dve(lambda e, w=w, k=k, cp=cp: e.tensor_tensor(out=Hw[(1, w)][:, :, k + 1], in0=sc[2][:], in1=Xi[:, :, cp], op=ALU.add), r=["sc2", "Xi"], w=[("hw", 1, w)])
                              if k == WN - 1:
                                  tt = c // 16
                                  for ri in range(2):
                                      if w == 0:
                                          S.act(lambda e, ri=ri, tt=tt: e.activation(out=Hb[ri][:, :, tt * 8], in_=Hw[(ri, 0)][:, :, WN], func=AF.Copy), r=[("hw", ri, 0)], w=[("hb", ri)])
                                      else:
                                          S.act(lambda e, ri=ri, tt=tt: e.activation(out=Hb[ri][:, :, tt * 8 + 1:tt * 8 + 8], in_=Hw[(ri, 1)][:, :, 1:WN], func=AF.Copy), r=[("hw", ri, 1)], w=[("hb", ri)])
                          for gq in range(16):
                              b = gq % 2
                              for j in range(4):
                                  g = gq * 4 + j
                                  S.pe(lambda e, g=g, b=b, j=j: e.matmul(pY[b][:, j, :], lhsT=Ug[:, g, 64:128], rhs=Tm[:, g, :], start=True, stop=False), r=[("Ug", gq)], w=[f"pY{b}"])
                                  S.pe(lambda e, g=g, b=b, j=j: e.matmul(pY[b][:, j, :], lhsT=Hb[0][:, g, :], rhs=WYr[:, g, :], start=False, stop=False), r=[("hb", 0)], w=[f"pY{b}"])
                                  S.pe(lambda e, g=g, b=b, j=j: e.matmul(pY[b][:, j, :], lhsT=Hb[1][:, g, :], rhs=WYi[:, g, :], start=False, stop=True), r=[("hb", 1)], w=[f"pY{b}"])
                              S.act(lambda e, gq=gq, b=b: e.activation(out=Yg[:, :, gq * 64:(gq + 1) * 64].rearrange("c t (gl q) -> c gl t q", q=16), in_=pY[b][:].rearrange("c gl (t q) -> c gl t q", q=16), func=AF.Gelu),
                                    r=[f"pY{b}"], w=UCM)
                          for cb in range(8):
                              for th_ in range(2):
                                  b = nq % 2
                                  nq += 1
                                  for j in range(4):
                                      t = th_ * 4 + j
                                      S.pe(lambda e, cb=cb, t=t, b=b, j=j: e.transpose(out=ptr[b][:, j, 0:64], in_=Yg[:, t, cb * 128:(cb + 1) * 128], identity=ident[0:64, 0:64]), r=UCM, w=[f"ptr{b}"])
                                  S.dve(lambda e, cb=cb, th_=th_, b=b: e.tensor_copy(out=Yfm[:, cb, :].rearrange("p (c t) -> p t c", t=8)[:, th_ * 4:th_ * 4 + 4, :], in_=ptr[b][:, :, 0:64]),
                                        r=[f"ptr{b}"], w=[("Yfm", cb)])
                          YF = [("Yfm", cb) for cb in range(8)]
                          for nb in range(8):
                              b = nb % 2
                              for cb in range(8):
                                  S.pe(lambda e, nb=nb, cb=cb, b=b: e.matmul(pZ[b][:], lhsT=wglu[:, cb, nb * 128:(nb + 1) * 128], rhs=Yfm[:, cb, :], start=(cb == 0), stop=(cb == 7)), r=YF, w=[f"pZ{b}"])
                              S.act(lambda e, nb=nb, b=b: e.activation(out=sgm[b][:], in_=pZ[b][:], func=AF.Sigmoid, bias=bglu[:, nb:nb + 1]), r=[f"pZ{b}"], w=[f"sgm{b}"])
                              S.dve(lambda e, nb=nb, b=b: e.tensor_tensor(out=yo[b][:], in0=Yfm[:, nb, :], in1=sgm[b][:], op=ALU.mult), r=[f"sgm{b}"] + YF, w=[f"yo{b}"])
                              S.store(lambda e, nb=nb, b=b, ct=ct: e.dma_start(out=YC[nb * 128:(nb + 1) * 128, ct * 512:(ct + 1) * 512], in_=yo[b][:]), r=[f"yo{b}"], w=[("YCs", nb, ct)])
                      S.run()

        if "3" in phases:
            with ExitStack() as es:
                T = lambda name, shape, dt: es.enter_context(nc.sbuf_tensor(name, shape, dt))
                P = lambda name, shape, dt: es.enter_context(nc.psum_tensor(name, shape, dt))
                S = Sched(nc, es, "p3")
                kt = [T(f"kt{i}", [128, NTOK], BF16) for i in range(2)]
                qt = [T(f"qt{i}", [128, NOWN], BF16) for i in range(2)]
                vh = [T(f"vh{i}", [128, NT, 128], BF16) for i in range(2)]
                pb = {(c, i): T(f"pb{c}{i}", [128, 512], BF16) for c in range(2) for i in range(2)}
                Lr = [T(f"Lr{c}", [128, 512], F32) for c in range(2)]
                Aa = [T(f"Aa{c}", [128, 512], F32) for c in range(2)]
                Oc = T("Oc", [128, 512], F32)
                sq = T("sq3", [128, 512], BF16)
                rr = T("rr3", [128, 512], F32)
                ya = [T(f"ya{i}", [128, 512], BF16) for i in range(2)]
                pS = {(c, i): P(f"pS{c}{i}", [128, 512], F32) for c in range(2) for i in range(2)}
                pO = [P(f"pO{c}", [128, 512], F32) for c in range(2)]
                pD = [P(f"pD{c}", [128, 512], F32) for c in range(2)]
                nn = 0
                nya = 0
                for h in range(NH):
                    hb = h % 2
                    S.load(lambda e, h=h, hb=hb: e.dma_start(out=kt[hb][:], in_=KT[h]), w=[f"kt{hb}"])
                    S.load(lambda e, h=h, hb=hb: e.dma_start(out=qt[hb][:], in_=QT[h]), w=[f"qt{hb}"])
                    S.load(lambda e, h=h, hb=hb: e.dma_start(out=vh[hb][:], in_=Vd.rearrange("(t p) c -> p t c", p=128)[:, :, h * 128:(h + 1) * 128]), w=[f"vh{hb}"])
                    for j in range(NQB):
                        last = 8 * j + 7
                        for t in range(last + 1):
                            c0 = 0 if t < 8 * j else 64 * (t - 8 * j)
                            b = nn % 2
                            nn += 1
                            for c in range(2):
                                S.pe(lambda e, c=c, b=b, t=t, c0=c0, j=j, hb=hb: e.matmul(pS[(c, b)][:, c0:512], lhsT=kt[hb][64 * c:64 * c + 64, t * 128:(t + 1) * 128], rhs=qt[hb][64 * c:64 * c + 64, j * 512 + c0:(j + 1) * 512], start=True, stop=True),
                                     r=[f"kt{hb}", f"qt{hb}"], w=[f"pS{c}{b}"])
                            for c in range(2):
                                if t == 0:
                                    S.act(lambda e, c=c, b=b, c0=c0: e.activation(out=pb[(c, b)][:, c0:512], in_=pS[(c, b)][:, c0:512], func=AF.Exp, bias=kbias0[:, 0:1]), r=[f"pS{c}{b}"], w=[f"pb{c}{b}"])
                                else:
                                    S.act(lambda e, c=c, b=b, c0=c0: e.activation(out=pb[(c, b)][:, c0:512], in_=pS[(c, b)][:, c0:512], func=AF.Exp), r=[f"pS{c}{b}"], w=[f"pb{c}{b}"])
                            for c in range(2):
                                S.pe(lambda e, c=c, b=b, t=t, c0=c0, hb=hb, last=last: e.matmul(pO[c][:, c0:512], lhsT=vh[hb][:, t, :], rhs=pb[(c, b)][:, c0:512], start=(t == 0), stop=(t == last)),
                                     r=[f"vh{hb}", f"pb{c}{b}"], w=[f"pO{c}"])
                                S.pe(lambda e, c=c, b=b, t=t, c0=c0, last=last: e.matmul(pD[c][:, c0:512], lhsT=onesb[:], rhs=pb[(c, b)][:, c0:512], start=(t == 0), stop=(t == last)),
                                     r=[f"pb{c}{b}"], w=[f"pD{c}"])
                        for c in range(2):
                            S.act(lambda e, c=c: e.activation(out=Lr[c][:], in_=pD[c][:], func=AF.Ln), r=[f"pD{c}"], w=[f"Lr{c}"])
                            S.act(lambda e, c=c: e.activation(out=Lr[c][:], in_=Lr[c][:], func=AF.Exp, scale=-1.0), r=[f"Lr{c}"], w=[f"Lr{c}"])
                            S.dve(lambda e, c=c: e.tensor_tensor(out=Aa[c][:], in0=pO[c][:], in1=Lr[c][:], op=ALU.mult), r=[f"pO{c}", f"Lr{c}"], w=[f"Aa{c}"])
                        S.dve(lambda e: e.scalar_tensor_tensor(out=Oc[:], in0=Aa[1][:], scalar=nlam[:, 0:1], in1=Aa[0][:], op0=ALU.mult, op1=ALU.add), r=["Aa0", "Aa1"], w=["Oc"])
                        S.act(lambda e: e.activation(out=sq[:], in_=Oc[:], func=AF.Square), r=["Oc"], w=["sq"])
                        S.pe(lambda e: e.matmul(pS[(0, 0)][:], lhsT=onesb[:], rhs=sq[:], start=True, stop=True), r=["sq"], w=["pS00"])
                        S.act(lambda e: e.activation(out=rr[:], in_=pS[(0, 0)][:], func=AF.Ln, scale=1.0 / 128, bias=EPS), r=["pS00"], w=["rr"])
                        S.act(lambda e: e.activation(out=rr[:], in_=rr[:], func=AF.Exp, scale=-0.5), r=["rr"], w=["rr"])
                        yb = nya % 2
                        nya += 1
                        S.dve(lambda e, yb=yb: e.scalar_tensor_tensor(out=ya[yb][:], in0=Oc[:], scalar=gsub[:, 0:1], in1=rr[:], op0=ALU.mult, op1=ALU.mult), r=["Oc", "rr"], w=[f"ya{yb}"])
                        S.store(lambda e, yb=yb, h=h, j=j: e.dma_start(out=YC[1024 + h * 128:1024 + (h + 1) * 128, j * 512:(j + 1) * 512], in_=ya[yb][:]), r=[f"ya{yb}"], w=[("YCa", h, j)])
                S.run()

        if "4" in phases:
            with ExitStack() as es:
                T = lambda name, shape, dt: es.enter_context(nc.sbuf_tensor(name, shape, dt))
                P = lambda name, shape, dt: es.enter_context(nc.psum_tensor(name, shape, dt))
                S = Sched(nc, es, "p4")
                NWP = 3
                wp = [T(f"wp{i}", [128, 16, 512], BF16) for i in range(NWP)]
                aT = T("aT", [128, 64, 512], BF16)
                a16 = T("a16", [128, 16, 512], BF16)
                x1 = [T(f"x1{i}", [128, D], F32) for i in range(4)]
                xn2 = T("xn2", [128, D], BF16)
                junk = T("junk4", [128, D], BF16)
                gbc = [T(f"gbc{i}", [128, D], F32) for i in range(2)]
                stmp = [T(f"stmp{i}", [128, 512], BF16) for i in range(2)]
                ftmp = [T(f"ftmp{i}", [128, 512], F32) for i in range(2)]
                ot = [T(f"ot{i}", [128, 512], F32) for i in range(2)]
                ss = T("ss4", [128, 4], F32)
                rs = T("rs4", [128, 4], F32)
                pM = [P(f"pM{i}", [128, 512], F32) for i in range(4)]
                pA = [P(f"pA{i}", [128, 512], F32) for i in range(2)]
                pt = [P(f"pt4{i}", [128, 4, 128], BF16) for i in range(2)]
                S.load(lambda e: e.dma_start(out=gbc[0][:], in_=MODR[2:3, :].to_broadcast([128, D])), w=["gbc0"])
                S.load(lambda e: e.dma_start(out=gbc[1][:], in_=MODR[5:6, :].to_broadcast([128, D])), w=["gbc1"])
                xown = I["xs"].rearrange("(t two i) d -> t two i d", two=2, i=64)
                nwp = 0
                nf = 0
                na = 0
                no = 0
                for bk in range(NOWN // 512):
                    S.load(lambda e, bk=bk: e.dma_start(out=a16[:], in_=YC[:, bk * 512:(bk + 1) * 512].rearrange("(kb p) t -> p kb t", p=128)), w=["a16"])
                    for i in range(4):
                        for hh in range(2):
                            tl = bk * 8 + i * 2 + hh
                            S.load(lambda e, i=i, hh=hh, tl=tl: e.dma_start(out=x1[i][64 * hh:64 * hh + 64, :], in_=xown[tl, 1]), w=[f"x1{i}"])
                    for db in range(4):
                        wb = nwp % NWP
                        nwp += 1
                        S.load(lambda e, wb=wb, db=db: e.dma_start(out=wp[wb][:], in_=WOUT[:, db * 512:(db + 1) * 512].rearrange("(kb p) n -> p kb n", p=128)), w=[f"wp{wb}"])
                        for i in range(4):
                            for kb in range(NKB):
                                S.pe(lambda e, i=i, kb=kb, wb=wb: e.matmul(pM[i][:], lhsT=a16[:, kb, i * 128:(i + 1) * 128], rhs=wp[wb][:, kb, :], start=(kb == 0), stop=(kb == NKB - 1)),
                                     r=["a16", f"wp{wb}"], w=[f"pM{i}"])
                            fb = nf % 2
                            nf += 1
                            S.dve(lambda e, i=i, fb=fb, db=db: e.tensor_tensor(out=ftmp[fb][:], in0=pM[i][:], in1=gbc[0][:, db * 512:(db + 1) * 512], op=ALU.mult), r=[f"pM{i}", "gbc0"], w=[f"ftmp{fb}"])
                            S.pool(lambda e, i=i, fb=fb, db=db: e.tensor_tensor(out=x1[i][:, db * 512:(db + 1) * 512], in0=x1[i][:, db * 512:(db + 1) * 512], in1=ftmp[fb][:], op=ALU.add), r=[f"ftmp{fb}", f"x1{i}"], w=[f"x1{i}"])
                    for i in range(4):
                        S.act(lambda e, i=i: e.activation(out=junk[:], in_=x1[i][:], func=AF.Square, accum_out=ss[:, i:i + 1]), r=[f"x1{i}"], w=["junk", f"ss{i}"])
                        S.act(lambda e, i=i: e.activation(out=rs[:, i:i + 1], in_=ss[:, i:i + 1], func=AF.Ln, scale=1.0 / D, bias=EPS), r=[f"ss{i}"], w=[f"rs{i}"])
                        S.act(lambda e, i=i: e.activation(out=rs[:, i:i + 1], in_=rs[:, i:i + 1], func=AF.Exp, scale=-0.5), r=[f"rs{i}"], w=[f"rs{i}"])
                        S.act(lambda e, i=i: e.activation(out=xn2[:], in_=x1[i][:], func=AF.Copy, scale=rs[:, i:i + 1]), r=[f"x1{i}", f"rs{i}"], w=["xn2"])
                        for q in range(4):
                            pb_ = q % 2
                            for j in range(4):
                                kb = q * 4 + j
                                S.pe(lambda e, kb=kb, pb_=pb_, j=j: e.transpose(out=pt[pb_][:, j, :], in_=xn2[:, kb * 128:(kb + 1) * 128], identity=ident[:]), r=["xn2"], w=[f"pt{pb_}"])
                            for j in range(4):
                                kb = q * 4 + j
                                S.dve(lambda e, i=i, kb=kb, pb_=pb_, j=j: e.tensor_scalar(out=a16[:, kb, i * 128:(i + 1) * 128], in0=pt[pb_][:, j, :], scalar1=g2s[:, kb:kb + 1], scalar2=modv[:, 48 + kb:49 + kb], op0=ALU.mult, op1=ALU.add),
                                      r=[f"pt{pb_}"], w=["a16"])
                    for hp in range(16):
                        wb = nwp % NWP
                        nwp += 1
                        S.load(lambda e, wb=wb, hp=hp: e.dma_start(out=wp[wb][:], in_=W1[:, hp * 512:(hp + 1) * 512].rearrange("(kb p) n -> p kb n", p=128)), w=[f"wp{wb}"])
                        for hl in range(4):
                            hbk = hp * 4 + hl
                            ab = na % 2
                            na += 1
                            for kb in range(NKB):
                                S.pe(lambda e, hl=hl, kb=kb, wb=wb, ab=ab: e.matmul(pA[ab][:], lhsT=wp[wb][:, kb, hl * 128:(hl + 1) * 128], rhs=a16[:, kb, :], start=(kb == 0), stop=(kb == NKB - 1)),
                                     r=["a16", f"wp{wb}"], w=[f"pA{ab}"])
                            S.act(lambda e, ab=ab: e.activation(out=stmp[ab][:], in_=pA[ab][:], func=AF.Square), r=[f"pA{ab}"], w=[f"stmp{ab}"])
                            S.dve(lambda e, ab=ab, hbk=hbk: e.scalar_tensor_tensor(out=aT[:, hbk, :], in0=pA[ab][:], scalar=0.0, in1=stmp[ab][:], op0=ALU.is_gt, op1=ALU.mult), r=[f"pA{ab}", f"stmp{ab}"], w=[("aT", hbk)])
                    for db in range(4):
                        for hq in range(4):
                            wb = nwp % NWP
                            nwp += 1
                            S.load(lambda e, wb=wb, db=db, hq=hq: e.dma_start(out=wp[wb][:], in_=W2[hq * 2048:(hq + 1) * 2048, db * 512:(db + 1) * 512].rearrange("(hb p) n -> p hb n", p=128)), w=[f"wp{wb}"])
                            for i in range(4):
                                for hl in range(16):
                                    hbk = hq * 16 + hl
                                    S.pe(lambda e, i=i, hl=hl, hbk=hbk, wb=wb, hq=hq: e.matmul(pM[i][:], lhsT=aT[:, hbk, i * 128:(i + 1) * 128], rhs=wp[wb][:, hl, :], start=(hq == 0 and hl == 0), stop=(hq == 3 and hl == 15)),
                                         r=[("aT", hbk), f"wp{wb}"], w=[f"pM{i}"])
                        for i in range(4):
                            fb = nf % 2
                            nf += 1
                            ob = no % 2
                            no += 1
                            S.dve(lambda e, i=i, fb=fb, db=db: e.tensor_tensor(out=ftmp[fb][:], in0=pM[i][:], in1=gbc[1][:, db * 512:(db + 1) * 512], op=ALU.mult), r=[f"pM{i}", "gbc1"], w=[f"ftmp{fb}"])
                            S.pool(lambda e, i=i, fb=fb, db=db, ob=ob: e.tensor_tensor(out=ot[ob][:], in0=x1[i][:, db * 512:(db + 1) * 512], in1=ftmp[fb][:], op=ALU.add), r=[f"ftmp{fb}", f"x1{i}"], w=[f"ot{ob}"])
                            S.store(lambda e, i=i, db=db, ob=ob, bk=bk: e.dma_start(out=out[bk * 512 + i * 128:bk * 512 + (i + 1) * 128, db * 512:(db + 1) * 512], in_=ot[ob][:]), r=[f"ot{ob}"], w=[("out", bk, i, db)], eng="sp")
                S.run()
    return nc


def make_in_maps(inputs, nb=None, seq=None):
    x = np.asarray(inputs["x"], dtype=np.float32)
    B, Sq, _ = x.shape
    maps = []
    for b in range(B):
        for par in range(2):
            m = {}
            if par == 1:
                m["xs"] = np.ascontiguousarray(x[b])
            else:
                m["xs"] = np.ascontiguousarray(np.concatenate([np.zeros((64, D), np.float32), x[b][:-64]], axis=0))
            m["c"] = np.ascontiguousarray(np.asarray(inputs["c"], np.float32)[b])
            for n, shp in PARAMS:
                if n in ("c", "valid0", "kbias0"):
                    continue
                m[n] = np.ascontiguousarray(np.asarray(inputs[n], np.float32)[0])
            v0 = np.ones((128, 1), np.float32)
            k0 = np.zeros((128, 1), np.float32)
            if par == 0:
                v0[:64] = 0.0
                k0[:64] = -30000.0
            m["valid0"] = v0
            m["kbias0"] = k0
            maps.append(m)
    return maps


def assemble(results, B, Sq):
    NT = Sq // 128
    out = np.empty((B, Sq, D), np.float32)
    ov = out.reshape(B, NT, 2, 64, D)
    for b in range(B):
        for par in range(2):
            ov[b, :, par] = np.asarray(results[2 * b + par]["out"], np.float32).reshape(NT, 64, D)
    return out


def kernel(**inputs):
    x = inputs["x"]
    B, Sq, _ = x.shape
    nc = build(NT=Sq // 128)
    maps = make_in_maps(inputs)
    res = run_bass_kernel_spmd(nc, maps, core_ids=list(range(len(maps))))
    return assemble(res.results, B, Sq)
```

```python
import math
import numpy as np
from contextlib import ExitStack
import concourse.bass as bass
import concourse.mybir as mybir
from concourse.bass_utils import run_bass_kernel_spmd

F32 = mybir.dt.float32
BF16 = mybir.dt.bfloat16
I32 = mybir.dt.int32
AF = mybir.ActivationFunctionType
ALU = mybir.AluOpType
AX = mybir.AxisListType

D = 2048
NKB = 16
NH = 8
EPS = 1e-6
LAMBDA_INIT = 0.8 - 0.6 * math.exp(-0.3 * 0)
ENGS = ["pe", "act", "dve", "pool", "sp"]
SAME_ENGINE_SYNC = True
NSTORE = 6


class Sched:
    def __init__(self, nc, es, name):
        self.nc, self.es, self.name = nc, es, name
        self.ops, self.last_w, self.readers = [], {}, {}
        self.allsems = []
        self.esem = {e: self._sem(f"{name}_{e}") for e in ENGS}
        self.dsem = {}
        self.psem = [self._sem(f"{name}_st{i}") for i in range(NSTORE)]

    def _sem(self, name):
        h = self.nc.alloc_semaphore(name=name)
        self.allsems.append(h)
        return h

    def op(self, eng, fn, r=(), w=(), dma=False, dkey=None):
        deps = set()
        for k in r:
            if k in self.last_w:
                deps.add(self.last_w[k])
        for k in w:
            if k in self.last_w:
                deps.add(self.last_w[k])
            deps.update(self.readers.get(k, ()))
        idx = len(self.ops)
        self.ops.append(dict(eng=eng, fn=fn, deps=sorted(deps), dma=dma, dkey=dkey, sig=None, need=False, idx=idx))
        for k in r:
            self.readers.setdefault(k, []).append(idx)
        for k in w:
            self.last_w[k] = idx
            self.readers[k] = []
        return idx

    def pe(self, fn, r=(), w=()): return self.op("pe", fn, r, w)
    def act(self, fn, r=(), w=()): return self.op("act", fn, r, w)
    def dve(self, fn, r=(), w=()): return self.op("dve", fn, r, w)
    def pool(self, fn, r=(), w=()): return self.op("pool", fn, r, w)

    def load(self, fn, r=(), w=(), eng="sp"):
        return self.op(eng, fn, r, w, dma=True, dkey=("L", w[0]))

    def store(self, fn, r=(), w=(), eng="pool"):
        return self.op(eng, fn, r, w, dma=True, dkey=None)

    def run(self):
        nc, ops = self.nc, self.ops
        for o in ops:
            keep = []
            for d in o["deps"]:
                p = ops[d]
                if not p["dma"] and not o["dma"] and p["eng"] == o["eng"]:
                    if o["eng"] == "pe" or not SAME_ENGINE_SYNC:
                        continue
                keep.append(d)
                p["need"] = True
            o["deps"] = keep
        last = {}
        for o in ops:
            if not o["dma"]:
                last[o["eng"]] = o
        for o in last.values():
            o["need"] = True
        cnt = {e: 0 for e in ENGS}
        dcnt = {}
        pcnt = [0] * NSTORE
        plast = [None] * NSTORE
        nst = 0
        for o in ops:
            if o["dma"]:
                if o["dkey"] is None:
                    s = nst % NSTORE
                    nst += 1
                    if plast[s] is not None:
                        o["deps"].append(plast[s])
                    pcnt[s] += 16
                    o["sig"] = (self.psem[s], pcnt[s])
                    plast[s] = o["idx"]
                else:
                    k = o["dkey"]
                    if k not in self.dsem:
                        self.dsem[k] = self._sem(f"{self.name}_d{len(self.dsem)}")
                        dcnt[k] = 0
                    dcnt[k] += 16
                    o["sig"] = (self.dsem[k], dcnt[k])
            elif o["need"]:
                cnt[o["eng"]] += 1
                o["sig"] = (self.esem[o["eng"]], cnt[o["eng"]])
        finals = [(self.esem[e], cnt[e]) for e in ENGS if cnt[e] > 0]
        finals += [(self.dsem[k], dcnt[k]) for k in self.dsem]
        finals += [(self.psem[i], pcnt[i]) for i in range(NSTORE) if pcnt[i] > 0]
        by_eng = {e: [o for o in ops if o["eng"] == e] for e in ENGS}

        def body(e):
            def f(eng):
                waited = {}
                for o in by_eng[e]:
                    for d in o["deps"]:
                        sem, val = ops[d]["sig"]
                        if waited.get(id(sem), 0) < val:
                            eng.wait_ge(sem, val)
                            waited[id(sem)] = val
                    ins = o["fn"](eng)
                    if o["sig"] is not None:
                        ins.then_inc(o["sig"][0], 16 if o["dma"] else 1)
                for sem, val in finals:
                    if waited.get(id(sem), 0) < val:
                        eng.wait_ge(sem, val)
            return f

        with nc.Block() as block:
            block.tensor(body("pe"))
            block.scalar(body("act"))
            block.vector(body("dve"))
            block.gpsimd(body("pool"))
            block.sync(body("sp"))
        nc.all_engine_barrier()
        nc.clear_and_free_semaphores(self.allsems)
        nc.all_engine_barrier()


PARAMS = [
    ("c", [D]), ("w_ada", [D, 6 * D]), ("b_ada", [6 * D]), ("g_norm_mix", [D]), ("g_norm_mlp", [D]),
    ("w_in", [D, 4096]), ("ssm_lambda_re", [64, 64]), ("ssm_lambda_im", [64, 64]),
    ("ssm_b_re", [64, 64, 16]), ("ssm_b_im", [64, 64, 16]), ("ssm_c_re", [64, 16, 64]), ("ssm_c_im", [64, 16, 64]),
    ("ssm_d", [64, 16]), ("ssm_log_step", [64]), ("w_glu", [1024, 1024]), ("b_glu", [1024]),
    ("g_q", [64]), ("g_k", [64]), ("lambda_q1", [64]), ("lambda_k1", [64]), ("lambda_q2", [64]), ("lambda_k2", [64]),
    ("g_subln", [128]), ("w_out", [D, D]), ("w_mlp1", [D, 8192]), ("w_mlp2", [8192, D]),
    ("valid0", [128, 1]), ("kbias0", [128, 1]),
]


def build(NT=64, debug=False, phases="01234"):
    NTOK = NT * 128
    NOWN = NTOK // 2
    NQB = NT // 8
    NCT = NT // 8
    NB1 = NT // 4
    nc = bass.Bass("TRN2", target_bir_lowering=False)
    I = {}
    I["xs"] = nc.dram_tensor("xs", [NTOK, D], F32, kind="ExternalInput").ap()
    for n, shp in PARAMS:
        I[n] = nc.dram_tensor(n, shp, F32, kind="ExternalInput").ap()
    out = nc.dram_tensor("out", [NOWN, D], F32, kind="ExternalOutput").ap()
    sk = "ExternalOutput" if debug else "Internal"
    WIN = nc.dram_tensor("WIN", [D, 4096], BF16, kind="Internal").ap()
    WGLU = nc.dram_tensor("WGLU", [1024, 1024], BF16, kind="Internal").ap()
    WOUT = nc.dram_tensor("WOUT", [D, D], BF16, kind="Internal").ap()
    W1 = nc.dram_tensor("W1", [D, 8192], BF16, kind="Internal").ap()
    W2 = nc.dram_tensor("W2", [8192, D], BF16, kind="Internal").ap()
    KT = nc.dram_tensor("KT", [NH, 128, NTOK], BF16, kind=sk).ap()
    QT = nc.dram_tensor("QT", [NH, 128, NOWN], BF16, kind=sk).ap()
    Vd = nc.dram_tensor("Vd", [NTOK, 1024], BF16, kind=sk).ap()
    Ud = nc.dram_tensor("Ud", [NTOK, 1024], BF16, kind=sk).ap()
    YC = nc.dram_tensor("YC", [D, NOWN], BF16, kind=sk).ap()
    MODR = nc.dram_tensor("MODR", [6, D], F32, kind=sk).ap()

    with ExitStack() as pes:
        PT = lambda name, shape, dt: pes.enter_context(nc.sbuf_tensor(name, shape, dt))
        ident = PT("ident", [128, 128], BF16)
        identf = PT("identf", [128, 128], F32)
        onesf = PT("onesf", [128, 128], F32)
        onesb = PT("onesb", [128, 128], BF16)
        bd64 = PT("bd64", [128, 128], BF16)
        cact = PT("cact", [128, 16], F32)
        modv = PT("modv", [128, 96], F32)
        bada = PT("bada", [128, 96], F32)
        g1s = PT("g1s", [128, 16], F32)
        g2s = PT("g2s", [128, 16], F32)
        gtmp = PT("gtmp", [128, 16], F32)
        valid0 = PT("valid0s", [128, 1], F32)
        kbias0 = PT("kbias0s", [128, 1], F32)
        gq2 = PT("gq2", [128, 1], F32)
        gk2 = PT("gk2", [128, 1], F32)
        gsub = PT("gsub", [128, 1], F32)
        nlam = PT("nlam", [128, 1], F32)
        lamt = PT("lamt", [128, 4, 64], F32)
        lamr = PT("lamr", [128, 4], F32)

        def mod_vectors(S, T, P, vecs, tag):
            wt = [T(f"wada{tag}{i}", [128, 16, 256], F32) for i in range(2)]
            pm = P(f"pmod{tag}", [128, 96], F32)
            n = 0
            for v in vecs:
                for cb in range(8):
                    b = n % 2
                    n += 1
                    col0 = v * D + cb * 256
                    S.load(lambda e, b=b, col0=col0: e.dma_start(out=wt[b][:], in_=I["w_ada"][:, col0:col0 + 256].rearrange("(kb p) n -> p kb n", p=128)),
                           w=[f"wada{b}"])
                    for j in range(2):
                        col = v * 16 + cb * 2 + j
                        for kb in range(NKB):
                            S.pe(lambda e, b=b, j=j, kb=kb, col=col: e.matmul(pm[:, col:col + 1], lhsT=wt[b][:, kb, j * 128:(j + 1) * 128], rhs=cact[:, kb:kb + 1], start=(kb == 0), stop=(kb == NKB - 1)),
                                 r=[f"wada{b}", "cact"], w=["pmod"])
                S.dve(lambda e, v=v: e.tensor_tensor(out=modv[:, v * 16:(v + 1) * 16], in0=pm[:, v * 16:(v + 1) * 16], in1=bada[:, v * 16:(v + 1) * 16], op=ALU.add),
                      r=["pmod", "bada"], w=[f"modv{v}"])
                S.store(lambda e, v=v: e.dma_start(out=MODR[v].rearrange("(kb p) -> p kb", p=128), in_=modv[:, v * 16:(v + 1) * 16], allow_slow_non_contiguous=True),
                        r=[f"modv{v}"], w=[f"MODR{v}"])

        def convert(S, src, dst, rows, step=128):
            for r0 in range(0, rows, step):
                S.store(lambda e, r0=r0: e.dma_start(out=dst[r0:r0 + step, :], in_=src[r0:r0 + step, :]), w=[("cv", id(dst), r0)])

        if "0" in phases:
            with ExitStack() as es:
                T = lambda name, shape, dt: es.enter_context(nc.sbuf_tensor(name, shape, dt))
                P = lambda name, shape, dt: es.enter_context(nc.psum_tensor(name, shape, dt))
                S = Sched(nc, es, "p0")
                convert(S, I["w_in"], WIN, D)
                S.pool(lambda e: e.memset(onesf[:], 1.0), w=["onesf"])
                S.pool(lambda e: e.memset(onesb[:], 1.0), w=["onesb"])
                S.pool(lambda e: e.affine_select(out=identf[:], in_=onesf[:], pattern=[[1, 128]], compare_op=ALU.is_equal, fill=0.0, base=0, channel_multiplier=-1), r=["onesf"], w=["identf"])
                S.pool(lambda e: e.tensor_copy(out=ident[:], in_=identf[:]), r=["identf"], w=["ident"])
                S.pool(lambda e: e.memset(bd64[:], 0.0), w=["bd64"])
                S.pool(lambda e: e.memset(bd64[0:64, 0:64], 1.0), w=["bd64"])
                S.pool(lambda e: e.memset(bd64[64:128, 64:128], 1.0), w=["bd64"])
                S.load(lambda e: e.dma_start(out=cact[:], in_=I["c"].rearrange("(kb p) -> p kb", p=128), allow_slow_non_contiguous=True), w=["cact"])
                S.load(lambda e: e.dma_start(out=bada[:], in_=I["b_ada"].rearrange("(j p) -> p j", p=128), allow_slow_non_contiguous=True), w=["bada"])
                S.load(lambda e: e.dma_start(out=g1s[:], in_=I["g_norm_mix"].rearrange("(kb p) -> p kb", p=128), allow_slow_non_contiguous=True), w=["g1s"])
                S.load(lambda e: e.dma_start(out=g2s[:], in_=I["g_norm_mlp"].rearrange("(kb p) -> p kb", p=128), allow_slow_non_contiguous=True), w=["g2s"])
                S.load(lambda e: e.dma_start(out=valid0[:], in_=I["valid0"]), w=["valid0"])
                S.load(lambda e: e.dma_start(out=kbias0[:], in_=I["kbias0"]), w=["kbias0"])
                for hh in range(2):
                    S.load(lambda e, hh=hh: e.dma_start(out=gq2[64 * hh:64 * hh + 64, :], in_=I["g_q"].rearrange("(p o) -> p o", o=1)), w=[f"gq2{hh}"])
                    S.load(lambda e, hh=hh: e.dma_start(out=gk2[64 * hh:64 * hh + 64, :], in_=I["g_k"].rearrange("(p o) -> p o", o=1)), w=[f"gk2{hh}"])
                S.load(lambda e: e.dma_start(out=gsub[:], in_=I["g_subln"].rearrange("(p o) -> p o", o=1)), w=["gsub"])
                for i, nme in enumerate(["lambda_q1", "lambda_k1", "lambda_q2", "lambda_k2"]):
                    S.load(lambda e, i=i, nme=nme: e.dma_start(out=lamt[:, i, :], in_=I[nme].rearrange("(o n) -> o n", o=1).to_broadcast([128, 64])), w=[f"lamt{i}"])
                S.act(lambda e: e.activation(out=cact[:], in_=cact[:], func=AF.Silu), r=["cact"], w=["cact"])
                S.dve(lambda e: e.tensor_scalar(out=gq2[:], in0=gq2[:], scalar1=0.125, scalar2=None, op0=ALU.mult), r=["gq20", "gq21"], w=["gq2"])
                S.dve(lambda e: e.tensor_scalar(out=gsub[:], in0=gsub[:], scalar1=1.0 - LAMBDA_INIT, scalar2=None, op0=ALU.mult), r=["gsub"], w=["gsub"])
                S.dve(lambda e: e.tensor_tensor(out=lamt[:, 0, :], in0=lamt[:, 0, :], in1=lamt[:, 1, :], op=ALU.mult), r=["lamt0", "lamt1"], w=["lamt0"])
                S.dve(lambda e: e.tensor_tensor(out=lamt[:, 2, :], in0=lamt[:, 2, :], in1=lamt[:, 3, :], op=ALU.mult), r=["lamt2", "lamt3"], w=["lamt2"])
                S.dve(lambda e: e.tensor_reduce(out=lamr[:, 0:1], in_=lamt[:, 0, :], axis=AX.X, op=ALU.add), r=["lamt0"], w=["lamr0"])
                S.dve(lambda e: e.tensor_reduce(out=lamr[:, 1:2], in_=lamt[:, 2, :], axis=AX.X, op=ALU.add), r=["lamt2"], w=["lamr1"])
                S.act(lambda e: e.activation(out=lamr[:, 2:4], in_=lamr[:, 0:2], func=AF.Exp), r=["lamr0", "lamr1"], w=["lamr2"])
                S.dve(lambda e: e.tensor_tensor(out=nlam[:], in0=lamr[:, 3:4], in1=lamr[:, 2:3], op=ALU.subtract), r=["lamr2"], w=["nlam"])
                S.dve(lambda e: e.tensor_scalar(out=nlam[:], in0=nlam[:], scalar1=-LAMBDA_INIT, scalar2=None, op0=ALU.add), r=["nlam"], w=["nlam"])
                mod_vectors(S, T, P, [0, 1], "a")
                S.dve(lambda e: e.tensor_scalar(out=gtmp[:], in0=modv[:, 16:32], scalar1=1.0, scalar2=None, op0=ALU.add), r=["modv1"], w=["gtmp"])
                S.dve(lambda e: e.tensor_tensor(out=g1s[:], in0=g1s[:], in1=gtmp[:], op=ALU.mult), r=["gtmp", "g1s"], w=["g1s"])
                S.run()

        if "1" in phases:
            with ExitStack() as es:
                T = lambda name, shape, dt: es.enter_context(nc.sbuf_tensor(name, shape, dt))
                P = lambda name, shape, dt: es.enter_context(nc.psum_tensor(name, shape, dt))
                S = Sched(nc, es, "p1")
                win = T("win", [128, NKB, 4096], BF16)
                xt = [T(f"xt{i}", [128, D], F32) for i in range(2)]
                junk = T("junk", [128, D], BF16)
                xn = [T(f"xn{i}", [128, D], BF16) for i in range(2)]
                ss = T("ss", [128, 2], F32)
                rs = T("rs", [128, 2], F32)
                hn = [T(f"hn{i}", [128, NKB, 512], BF16) for i in range(2)]
                sqb = [T(f"sqb{i}", [128, 512], BF16) for i in range(2)]
                lf = [T(f"lf{i}", [128, 512], F32) for i in range(2)]
                ko = [T(f"ko{i}", [128, 512], BF16) for i in range(2)]
                vo = [T(f"vo{i}", [128, 1024], BF16) for i in range(2)]
                pt = [P(f"pt{i}", [128, 4, 128], BF16) for i in range(2)]
                pk = [P(f"pk{i}", [128, 512], F32) for i in range(3)]
                pn = [P(f"pn{i}", [128, 512], F32) for i in range(2)]
                for kb in range(NKB):
                    S.load(lambda e, kb=kb: e.dma_start(out=win[:, kb, :], in_=WIN[kb * 128:(kb + 1) * 128, :]), w=[f"win{kb}"])
                WINR = [f"win{kb}" for kb in range(NKB)]
                convert(S, I["w_glu"], WGLU, 1024)
                convert(S, I["w_out"], WOUT, D)
                convert(S, I["w_mlp1"], W1, D)
                convert(S, I["w_mlp2"], W2, 8192, step=512)
                npk = 0
                nep = 0
                nvo = 0
                for b in range(NB1):
                    hb = b % 2
                    for i in range(4):
                        tg = b * 4 + i
                        xb = tg % 2
                        S.load(lambda e, xb=xb, tg=tg: e.dma_start(out=xt[xb][:], in_=I["xs"][tg * 128:(tg + 1) * 128, :]), w=[f"xt{xb}"])
                        S.act(lambda e, xb=xb: e.activation(out=junk[:], in_=xt[xb][:], func=AF.Square, accum_out=ss[:, xb:xb + 1]), r=[f"xt{xb}"], w=["junk", f"ss{xb}"])
                        S.act(lambda e, xb=xb: e.activation(out=rs[:, xb:xb + 1], in_=ss[:, xb:xb + 1], func=AF.Ln, scale=1.0 / D, bias=EPS), r=[f"ss{xb}"], w=[f"rs{xb}"])
                        S.act(lambda e, xb=xb: e.activation(out=rs[:, xb:xb + 1], in_=rs[:, xb:xb + 1], func=AF.Exp, scale=-0.5), r=[f"rs{xb}"], w=[f"rs{xb}"])
                        S.act(lambda e, xb=xb: e.activation(out=xn[xb][:], in_=xt[xb][:], func=AF.Copy, scale=rs[:, xb:xb + 1]), r=[f"xt{xb}", f"rs{xb}"], w=[f"xn{xb}"])
                        for q in range(4):
                            pb = q % 2
                            for j in range(4):
                                kb = q * 4 + j
                                S.pe(lambda e, xb=xb, kb=kb, pb=pb, j=j: e.transpose(out=pt[pb][:, j, :], in_=xn[xb][:, kb * 128:(kb + 1) * 128], identity=ident[:]),
                                     r=[f"xn{xb}"], w=[f"pt{pb}"])
                            for j in range(4):
                                kb = q * 4 + j
                                S.dve(lambda e, hb=hb, i=i, kb=kb, pb=pb, j=j: e.tensor_scalar(out=hn[hb][:, kb, i * 128:(i + 1) * 128], in0=pt[pb][:, j, :], scalar1=g1s[:, kb:kb + 1], scalar2=modv[:, kb:kb + 1], op0=ALU.mult, op1=ALU.add),
                                      r=[f"pt{pb}"], w=[f"hn{hb}"])
                    for which in ("k", "q"):
                        for h in range(NH):
                            pkb = npk % 3
                            npk += 1
                            eb = nep % 2
                            nep += 1
                            if which == "k":
                                N = 512
                                col0 = 2048 + h * 128
                                rhs_of = lambda kb, hb=hb: hn[hb][:, kb, :]
                                gcol = gk2
                            else:
                                N = 256
                                col0 = 1024 + h * 128
                                rhs_of = lambda kb, hb=hb: hn[hb][:, kb, :].rearrange("p (t two i) -> p t two i", two=2, i=64)[:, :, 1, :]
                                gcol = gq2
                            for kb in range(NKB):
                                S.pe(lambda e, kb=kb, pkb=pkb, col0=col0, N=N, rhs_of=rhs_of: e.matmul(pk[pkb][:, 0:N], lhsT=win[:, kb, col0:col0 + 128], rhs=rhs_of(kb), start=(kb == 0), stop=(kb == NKB - 1)),
                                     r=[f"hn{hb}", f"win{kb}"], w=[f"pk{pkb}"])
                            S.act(lambda e, pkb=pkb, eb=eb, N=N: e.activation(out=sqb[eb][:, 0:N], in_=pk[pkb][:, 0:N], func=AF.Square), r=[f"pk{pkb}"], w=[f"sqb{eb}"])
                            S.pe(lambda e, eb=eb, N=N: e.matmul(pn[eb][:, 0:N], lhsT=bd64[:], rhs=sqb[eb][:, 0:N], start=True, stop=True), r=[f"sqb{eb}"], w=[f"pn{eb}"])
                            S.act(lambda e, eb=eb, N=N: e.activation(out=lf[eb][:, 0:N], in_=pn[eb][:, 0:N], func=AF.Ln, scale=1.0 / 64, bias=EPS), r=[f"pn{eb}"], w=[f"lf{eb}"])
                            S.act(lambda e, eb=eb, N=N: e.activation(out=lf[eb][:, 0:N], in_=lf[eb][:, 0:N], func=AF.Exp, scale=-0.5), r=[f"lf{eb}"], w=[f"lf{eb}"])
                            S.dve(lambda e, pkb=pkb, eb=eb, N=N, gcol=gcol: e.scalar_tensor_tensor(out=ko[eb][:, 0:N], in0=pk[pkb][:, 0:N], scalar=gcol[:, 0:1], in1=lf[eb][:, 0:N], op0=ALU.mult, op1=ALU.mult),
                                  r=[f"pk{pkb}", f"lf{eb}"], w=[f"ko{eb}"])
                            if which == "k":
                                S.store(lambda e, eb=eb, h=h, b=b: e.dma_start(out=KT[h][:, b * 512:(b + 1) * 512], in_=ko[eb][:, 0:512]), r=[f"ko{eb}"], w=[("KT", h, b)])
                            else:
                                S.store(lambda e, eb=eb, h=h, b=b: e.dma_start(out=QT[h][:, b * 256:(b + 1) * 256], in_=ko[eb][:, 0:256]), r=[f"ko{eb}"], w=[("QT", h, b)])
                    for which in ("v", "u"):
                        for i in range(4):
                            tg = b * 4 + i
                            vb = nvo % 2
                            nvo += 1
                            for nb in range(2):
                                pkb = npk % 3
                                npk += 1
                                col0 = (3072 if which == "v" else 0) + nb * 512
                                for kb in range(NKB):
                                    S.pe(lambda e, kb=kb, pkb=pkb, col0=col0, i=i, hb=hb: e.matmul(pk[pkb][:], lhsT=hn[hb][:, kb, i * 128:(i + 1) * 128], rhs=win[:, kb, col0:col0 + 512], start=(kb == 0), stop=(kb == NKB - 1)),
                                         r=[f"hn{hb}", f"win{kb}"], w=[f"pk{pkb}"])
                                if which == "u" and tg == 0:
                                    S.act(lambda e, pkb=pkb, vb=vb, nb=nb: e.activation(out=vo[vb][:, nb * 512:(nb + 1) * 512], in_=pk[pkb][:], func=AF.Copy, scale=valid0[:, 0:1]), r=[f"pk{pkb}"], w=[f"vo{vb}"])
                                else:
                                    S.act(lambda e, pkb=pkb, vb=vb, nb=nb: e.activation(out=vo[vb][:, nb * 512:(nb + 1) * 512], in_=pk[pkb][:], func=AF.Copy), r=[f"pk{pkb}"], w=[f"vo{vb}"])
                            dst = Vd if which == "v" else Ud
                            S.store(lambda e, vb=vb, tg=tg, dst=dst: e.dma_start(out=dst[tg * 128:(tg + 1) * 128, :], in_=vo[vb][:]), r=[f"vo{vb}"], w=[(which, tg)])
                S.run()

        if "2" in phases or "a" in phases or "b" in phases:
            with ExitStack() as wes:
                WT = lambda name, shape, dt: wes.enter_context(nc.sbuf_tensor(name, shape, dt))
                Tm = WT("Tm", [128, 64, 128], BF16)
                WXr = WT("WXr", [128, 64, 64], BF16)
                WXi = WT("WXi", [128, 64, 64], BF16)
                WYr = WT("WYr", [64, 64, 128], BF16)
                WYi = WT("WYi", [64, 64, 128], BF16)
                A8r = WT("A8r", [64, 64], F32)
                A8i = WT("A8i", [64, 64], F32)
                wglu = WT("wglu", [128, 8, 1024], BF16)
                bglu = WT("bglu", [128, 8], F32)
                with ExitStack() as es:
                    T = lambda name, shape, dt: es.enter_context(nc.sbuf_tensor(name, shape, dt))
                    P = lambda name, shape, dt: es.enter_context(nc.psum_tensor(name, shape, dt))
                    S = Sched(nc, es, "p2v")
                    mod_vectors(S, T, P, [2, 3, 4, 5], "b")
                    S.dve(lambda e: e.tensor_scalar(out=gtmp[:], in0=modv[:, 64:80], scalar1=1.0, scalar2=None, op0=ALU.add), r=["modv4"], w=["gtmp"])
                    S.dve(lambda e: e.tensor_tensor(out=g2s[:], in0=g2s[:], in1=gtmp[:], op=ALU.mult), r=["gtmp"], w=["g2s"])
                    S.run()
                with ExitStack() as es:
                  if "2" in phases or "b" in phases:
                      T = lambda name, shape, dt: es.enter_context(nc.sbuf_tensor(name, shape, dt))
                      P = lambda name, shape, dt: es.enter_context(nc.psum_tensor(name, shape, dt))
                      S = Sched(nc, es, "p2s")
                      import os as _os
                      CUT = int(_os.environ.get("P2S_CUT", "99"))
                      class _Stop(Exception): pass
                      def stage(n):
                          if CUT < n: raise _Stop()
                      try:
                          for cb in range(8):
                              S.load(lambda e, cb=cb: e.dma_start(out=wglu[:, cb, :], in_=WGLU[cb * 128:(cb + 1) * 128, :]), w=[f"wglu{cb}"])
                          S.load(lambda e: e.dma_start(out=bglu[:], in_=I["b_glu"].rearrange("(j p) -> p j", p=128), allow_slow_non_contiguous=True), w=["bglu"])
                          lr = T("lr", [64, 64], F32); li = T("li", [64, 64], F32); ls = T("ls", [64, 64], F32)
                          Br = T("Br", [64, 64, 16], F32); Bi = T("Bi", [64, 64, 16], F32)
                          Cn = [T("Cnr", [128, 8, 64], F32), T("Cni", [128, 8, 64], F32)]
                          Cp = [T("Cpr", [64, 64, 16], F32), T("Cpi", [64, 64, 16], F32)]
                          dvec = T("dvec", [128, 64], F32)
                          mask01 = T("mask01", [128, 128], F32)
                          S.load(lambda e: e.dma_start(out=lr[:], in_=I["ssm_lambda_re"].rearrange("g p -> p g"), allow_slow_non_contiguous=True), w=["lr"])
                          S.load(lambda e: e.dma_start(out=li[:], in_=I["ssm_lambda_im"].rearrange("g p -> p g"), allow_slow_non_contiguous=True), w=["li"])
                          S.load(lambda e: e.dma_start(out=ls[:], in_=I["ssm_log_step"].rearrange("(o g) -> o g", o=1).to_broadcast([64, 64])), w=["ls"])
                          S.load(lambda e: e.dma_start(out=Br[:], in_=I["ssm_b_re"].rearrange("g p h -> p g h")), w=["Br"])
                          S.load(lambda e: e.dma_start(out=Bi[:], in_=I["ssm_b_im"].rearrange("g p h -> p g h")), w=["Bi"])
                          for ri, nme in enumerate(["ssm_c_re", "ssm_c_im"]):
                              S.load(lambda e, ri=ri, nme=nme: e.dma_start(out=Cn[ri][:], in_=I[nme].rearrange("(gb g8) q p -> (g8 q) gb p", g8=8)), w=[f"Cn{ri}"])
                          for s in range(8):
                              S.load(lambda e, s=s: e.dma_start(out=dvec[16 * s:16 * s + 16, :], in_=I["ssm_d"].rearrange("g h -> h g"), allow_slow_non_contiguous=True), w=[f"dvec{s}"])
                          DVR = [f"dvec{s}" for s in range(8)]
                          S.pool(lambda e: e.affine_select(out=mask01[:], in_=onesf[:], pattern=[[16, 8], [0, 16]], compare_op=ALU.is_ge, fill=0.0, base=15, channel_multiplier=-1), w=["mask01"])
                          pc = [P(f"pc{i}", [64, 128], F32) for i in range(2)]
                          n = 0
                          for ri in range(2):
                              for gb in range(8):
                                  b = n % 2
                                  n += 1
                                  S.pe(lambda e, ri=ri, gb=gb, b=b: e.transpose(out=pc[b][:], in_=Cn[ri][:, gb, :], identity=identf[:]), r=[f"Cn{ri}"], w=[f"pc{b}"])
                                  S.act(lambda e, ri=ri, gb=gb, b=b: e.activation(out=Cp[ri][:, gb * 8:(gb + 1) * 8, :].rearrange("p g q -> p (g q)"), in_=pc[b][:], func=AF.Copy), r=[f"pc{b}"], w=[f"Cp{ri}"])
                          stage(2)
                          NK = 25
                          kvi = T("kvi", [64, NK], I32); kv = T("kv", [64, NK], F32)
                          S.pool(lambda e: e.iota(kvi[:, 0:8], pattern=[[-1, 8]], base=0, channel_multiplier=0), w=["kvi"])
                          S.pool(lambda e: e.iota(kvi[:, 8:16], pattern=[[-1, 8]], base=7, channel_multiplier=0), w=["kvi"])
                          S.pool(lambda e: e.iota(kvi[:, 16:25], pattern=[[1, 9]], base=0, channel_multiplier=0), w=["kvi"])
                          S.dve(lambda e: e.tensor_copy(out=kv[:], in_=kvi[:]), r=["kvi"], w=["kv"])
                          dt_ = T("dt_", [64, 64], F32); mu = T("mu", [64, 64], F32); th = T("th", [64, 64], F32)
                          S.act(lambda e: e.activation(out=dt_[:], in_=ls[:], func=AF.Exp), r=["ls"], w=["dt"])
                          S.dve(lambda e: e.tensor_tensor(out=mu[:], in0=dt_[:], in1=lr[:], op=ALU.mult), r=["dt", "lr"], w=["mu"])
                          S.dve(lambda e: e.tensor_tensor(out=th[:], in0=dt_[:], in1=li[:], op=ALU.mult), r=["dt", "li"], w=["th"])
                          shp = [64, 64, NK]
                          ANG = T("ANG", shp, F32); MAG = T("MAG", shp, F32); V0 = T("V0", shp, F32); V1 = T("V1", shp, F32)
                          VI = T("VI", shp, I32); AR = T("AR", shp, F32); AI = T("AI", shp, F32)
                          kvb = lambda: kv[:].unsqueeze(1).to_broadcast(shp)
                          S.dve(lambda e: e.tensor_tensor(out=ANG[:], in0=th[:].unsqueeze(2).to_broadcast(shp), in1=kvb(), op=ALU.mult), r=["th", "kv"], w=["ANG"])
                          S.dve(lambda e: e.tensor_tensor(out=MAG[:], in0=mu[:].unsqueeze(2).to_broadcast(shp), in1=kvb(), op=ALU.mult), r=["mu", "kv"], w=["MAG"])
                          S.act(lambda e: e.activation(out=MAG[:], in_=MAG[:], func=AF.Exp), r=["MAG"], w=["MAG"])
                          for which, off, dst in (("s", 64.0, AI), ("c", 64.25, AR)):
                              S.dve(lambda e, off=off: e.tensor_scalar(out=V0[:], in0=ANG[:], scalar1=1.0 / (2 * math.pi), scalar2=off, op0=ALU.mult, op1=ALU.add), r=["ANG"], w=["V0"])
                              S.dve(lambda e: e.tensor_copy(out=VI[:], in_=V0[:]), r=["V0"], w=["VI"])
                              S.dve(lambda e: e.tensor_copy(out=V1[:], in_=VI[:]), r=["VI"], w=["V1"])
                              S.dve(lambda e: e.tensor_tensor(out=V0[:], in0=V0[:], in1=V1[:], op=ALU.subtract), r=["V0", "V1"], w=["V0"])
                              S.dve(lambda e: e.tensor_scalar(out=V1[:], in0=V0[:], scalar1=0.5, scalar2=None, op0=ALU.is_gt), r=["V0"], w=["V1"])
                              S.dve(lambda e: e.tensor_tensor(out=V0[:], in0=V0[:], in1=V1[:], op=ALU.subtract), r=["V0", "V1"], w=["V0"])
                              S.dve(lambda e: e.tensor_scalar(out=V1[:], in0=V0[:], scalar1=-0.5, scalar2=None, op0=ALU.is_lt), r=["V0"], w=["V1"])
                              S.dve(lambda e: e.tensor_tensor(out=V0[:], in0=V0[:], in1=V1[:], op=ALU.add), r=["V0", "V1"], w=["V0"])
                              S.act(lambda e, dst=dst: e.activation(out=dst[:], in_=V0[:], func=AF.Sin, scale=6.283184), r=["V0"], w=[which + "in"])
                              S.dve(lambda e, dst=dst: e.tensor_tensor(out=dst[:], in0=dst[:], in1=MAG[:], op=ALU.mult), r=[which + "in", "MAG"], w=["A" + which])
                          AW = ["As", "Ac"]
                          S.act(lambda e: e.activation(out=A8r[:], in_=AR[:, :, 24], func=AF.Copy), r=AW, w=["A8r"])
                          S.act(lambda e: e.activation(out=A8i[:], in_=AI[:, :, 24], func=AF.Copy), r=AW, w=["A8i"])
                          stage(3)
                          zr = T("zr", [64, 64], F32); den = T("den", [64, 64], F32); t0 = T("t0", [64, 64], F32); t1 = T("t1", [64, 64], F32)
                          kr = T("kr", [64, 64], F32); ki = T("ki", [64, 64], F32)
                          S.dve(lambda e: e.tensor_scalar(out=zr[:], in0=AR[:, :, 17], scalar1=-1.0, scalar2=None, op0=ALU.add), r=AW, w=["zr"])
                          S.dve(lambda e: e.tensor_tensor(out=den[:], in0=lr[:], in1=lr[:], op=ALU.mult), r=["lr"], w=["den"])
                          S.dve(lambda e: e.tensor_tensor(out=t0[:], in0=li[:], in1=li[:], op=ALU.mult), r=["li"], w=["t0"])
                          S.dve(lambda e: e.tensor_tensor(out=den[:], in0=den[:], in1=t0[:], op=ALU.add), r=["den", "t0"], w=["den"])
                          S.dve(lambda e: e.reciprocal(out=den[:], in_=den[:]), r=["den"], w=["den"])
                          S.dve(lambda e: e.tensor_tensor(out=t0[:], in0=zr[:], in1=lr[:], op=ALU.mult), r=["zr", "lr"], w=["t0"])
                          S.dve(lambda e: e.tensor_tensor(out=t1[:], in0=AI[:, :, 17], in1=li[:], op=ALU.mult), r=AW + ["li"], w=["t1"])
                          S.dve(lambda e: e.tensor_tensor(out=t0[:], in0=t0[:], in1=t1[:], op=ALU.add), r=["t0", "t1"], w=["t0"])
                          S.dve(lambda e: e.tensor_tensor(out=kr[:], in0=t0[:], in1=den[:], op=ALU.mult), r=["t0", "den"], w=["kr"])
                          S.dve(lambda e: e.tensor_tensor(out=t0[:], in0=AI[:, :, 17], in1=lr[:], op=ALU.mult), r=AW + ["lr"], w=["t0"])
                          S.dve(lambda e: e.tensor_tensor(out=t1[:], in0=zr[:], in1=li[:], op=ALU.mult), r=["zr", "li"], w=["t1"])
                          S.dve(lambda e: e.tensor_tensor(out=t0[:], in0=t0[:], in1=t1[:], op=ALU.subtract), r=["t0", "t1"], w=["t0"])
                          S.dve(lambda e: e.tensor_tensor(out=ki[:], in0=t0[:], in1=den[:], op=ALU.mult), r=["t0", "den"], w=["ki"])
                          cr_ = T("cr_", [64, 64, 16], F32); ci_ = T("ci_", [64, 64, 16], F32); c0 = T("c0", [64, 64, 16], F32)
                          s16 = [64, 64, 16]
                          krb = lambda: kr[:].unsqueeze(2).to_broadcast(s16)
                          kib = lambda: ki[:].unsqueeze(2).to_broadcast(s16)
                          S.dve(lambda e: e.tensor_tensor(out=cr_[:], in0=AR[:, :, 0:16], in1=krb(), op=ALU.mult), r=AW + ["kr"], w=["cr"])
                          S.dve(lambda e: e.tensor_tensor(out=c0[:], in0=AI[:, :, 0:16], in1=kib(), op=ALU.mult), r=AW + ["ki"], w=["c0"])
                          S.dve(lambda e: e.tensor_tensor(out=cr_[:], in0=cr_[:], in1=c0[:], op=ALU.subtract), r=["cr", "c0"], w=["cr"])
                          S.dve(lambda e: e.tensor_tensor(out=ci_[:], in0=AR[:, :, 0:16], in1=kib(), op=ALU.mult), r=AW + ["ki"], w=["ci"])
                          S.dve(lambda e: e.tensor_tensor(out=c0[:], in0=AI[:, :, 0:16], in1=krb(), op=ALU.mult), r=AW + ["kr"], w=["c0"])
                          S.dve(lambda e: e.tensor_tensor(out=ci_[:], in0=ci_[:], in1=c0[:], op=ALU.add), r=["ci", "c0"], w=["ci"])
                          stage(4)
                          GC = 4
                          se = [64, GC, 8, 16]
                          Er = T("Er", se, F32); Ei = T("Ei", se, F32); Xr_ = T("Xr_", se, F32); Xi_ = T("Xi_", se, F32)
                          u0 = T("u0", se, F32); u1 = T("u1", se, F32)
                          sg9 = [64, GC, 9, 16]
                          Gr = T("Gr", sg9, F32); Gm = T("Gm", sg9, F32); w0 = T("w0", sg9, F32); w1 = T("w1", sg9, F32)
                          tmk = [T(f"tmk{i}", [128, 128], F32) for i in range(2)]
                          pT = [P(f"pT{i}", [128, 128], F32) for i in range(2)]
                          pX = [P(f"pX{i}", [128, 64], BF16) for i in range(2)]
                          Xrb = T("Xrb", se, BF16); Xib = T("Xib", se, BF16)
                          nT = 0
                          nX = 0
                          for gc in range(64 // GC):
                              g0 = gc * GC
                              gs = slice(g0, g0 + GC)
                              for (o0, dr, di, tag) in ((0, Er, Ei, "E"), (8, Xr_, Xi_, "X")):
                                  cb_ = lambda t_, o0=o0, gs=gs: t_[:, gs, o0:o0 + 8].unsqueeze(3).to_broadcast(se)
                                  bb_ = lambda t_, gs=gs: t_[:, gs, :].unsqueeze(2).to_broadcast(se)
                                  S.dve(lambda e, cb_=cb_, bb_=bb_: e.tensor_tensor(out=u0[:], in0=cb_(cr_), in1=bb_(Br), op=ALU.mult), r=["cr", "Br"], w=["u0"])
                                  S.dve(lambda e, cb_=cb_, bb_=bb_: e.tensor_tensor(out=u1[:], in0=cb_(ci_), in1=bb_(Bi), op=ALU.mult), r=["ci", "Bi"], w=["u1"])
                                  S.dve(lambda e, dr=dr: e.tensor_tensor(out=dr[:], in0=u0[:], in1=u1[:], op=ALU.subtract), r=["u0", "u1"], w=[tag + "r"])
                                  S.dve(lambda e, cb_=cb_, bb_=bb_: e.tensor_tensor(out=u0[:], in0=cb_(cr_), in1=bb_(Bi), op=ALU.mult), r=["cr", "Bi"], w=["u0"])
                                  S.dve(lambda e, cb_=cb_, bb_=bb_: e.tensor_tensor(out=u1[:], in0=cb_(ci_), in1=bb_(Br), op=ALU.mult), r=["ci", "Br"], w=["u1"])
                                  S.dve(lambda e, di=di: e.tensor_tensor(out=di[:], in0=u0[:], in1=u1[:], op=ALU.add), r=["u0", "u1"], w=[tag + "i"])
                              S.act(lambda e: e.activation(out=Xrb[:], in_=Xr_[:], func=AF.Copy), r=["Xr"], w=["Xrb"])
                              S.act(lambda e: e.activation(out=Xib[:], in_=Xi_[:], func=AF.Copy), r=["Xi"], w=["Xib"])
                              stage(5)
                              ab_ = lambda t_, gs=gs: t_[:, gs, 16:25].unsqueeze(3).to_broadcast(sg9)
                              cc_ = lambda t_, gs=gs: t_[:, gs, :].unsqueeze(2).to_broadcast(sg9)
                              S.dve(lambda e, ab_=ab_, cc_=cc_: e.tensor_tensor(out=w0[:], in0=ab_(AR), in1=cc_(Cp[0]), op=ALU.mult), r=AW + ["Cp0"], w=["w0"])
                              S.dve(lambda e, ab_=ab_, cc_=cc_: e.tensor_tensor(out=w1[:], in0=ab_(AI), in1=cc_(Cp[1]), op=ALU.mult), r=AW + ["Cp1"], w=["w1"])
                              S.dve(lambda e: e.tensor_tensor(out=Gr[:], in0=w0[:], in1=w1[:], op=ALU.subtract), r=["w0", "w1"], w=["Gr"])
                              S.dve(lambda e, ab_=ab_, cc_=cc_: e.tensor_tensor(out=w0[:], in0=ab_(AI), in1=cc_(Cp[0]), op=ALU.mult), r=AW + ["Cp0"], w=["w0"])
                              S.dve(lambda e, ab_=ab_, cc_=cc_: e.tensor_tensor(out=w1[:], in0=ab_(AR), in1=cc_(Cp[1]), op=ALU.mult), r=AW + ["Cp1"], w=["w1"])
                              S.dve(lambda e: e.scalar_tensor_tensor(out=Gm[:], in0=w0[:], scalar=-1.0, in1=w1[:], op0=ALU.mult, op1=ALU.subtract), r=["w0", "w1"], w=["Gm"])
                              stage(6)
                              S.act(lambda e, gs=gs: e.activation(out=WYr[:, gs, :].rearrange("p g (t q) -> p g t q", q=16), in_=Gr[:, :, 1:9, :], func=AF.Copy), r=["Gr"], w=["WYr"])
                              S.act(lambda e, gs=gs: e.activation(out=WYi[:, gs, :].rearrange("p g (t q) -> p g t q", q=16), in_=Gm[:, :, 1:9, :], func=AF.Copy), r=["Gm"], w=["WYi"])
                              stage(7)
                              for gl in range(GC):
                                  g = g0 + gl
                                  b = nT % 2
                                  nT += 1
                                  S.pe(lambda e, gl=gl, b=b: e.matmul(pT[b][:], lhsT=Er[:, gl, :, :].rearrange("p s h -> p (s h)"), rhs=Gr[:, gl, 0:8, :].rearrange("p t q -> p (t q)"), start=True, stop=False),
                                       r=["Er", "Gr"], w=[f"pT{b}"])
                                  S.pe(lambda e, gl=gl, b=b: e.matmul(pT[b][:], lhsT=Ei[:, gl, :, :].rearrange("p s h -> p (s h)"), rhs=Gm[:, gl, 0:8, :].rearrange("p t q -> p (t q)"), start=False, stop=True),
                                       r=["Ei", "Gm"], w=[f"pT{b}"])
                                  S.dve(lambda e, b=b: e.tensor_tensor(out=tmk[b][:], in0=pT[b][:], in1=mask01[:], op=ALU.mult), r=[f"pT{b}", "mask01"], w=[f"tmk{b}"])
                                  S.dve(lambda e, b=b, g=g: e.scalar_tensor_tensor(out=Tm[:, g, :], in0=identf[:], scalar=dvec[:, g:g + 1], in1=tmk[b][:], op0=ALU.mult, op1=ALU.add),
                                        r=[f"tmk{b}"] + DVR, w=["Tm"])
                                  stage(8)
                                  for (src, dstw, tag) in ((Xrb, WXr, "Xrb"), (Xib, WXi, "Xib")):
                                      bx = nX % 2
                                      nX += 1
                                      S.pe(lambda e, gl=gl, bx=bx, src=src: e.transpose(out=pX[bx][:], in_=src[:, gl, :, :].rearrange("p s h -> p (s h)"), identity=ident[0:64, 0:64]),
                                           r=[tag], w=[f"pX{bx}"])
                                      S.act(lambda e, bx=bx, g=g, dstw=dstw: e.activation(out=dstw[:, g, :], in_=pX[bx][:], func=AF.Copy), r=[f"pX{bx}"], w=["W" + tag])

                      except _Stop:
                          pass
                      S.run()

                with ExitStack() as es:
                  if "2" in phases:
                      T = lambda name, shape, dt: es.enter_context(nc.sbuf_tensor(name, shape, dt))
                      P = lambda name, shape, dt: es.enter_context(nc.psum_tensor(name, shape, dt))
                      S = Sched(nc, es, "p2m")
                      WN = 8
                      Ucm = T("Ucm", [128, 8, 1024], BF16)
                      Ug = T("Ug", [128, 64, 128], BF16)
                      Xr = T("Xr", [64, 64, 128], BF16)
                      Xi = T("Xi", [64, 64, 128], BF16)
                      Hw = {(ri, w): T(f"Hw{ri}{w}", [64, 64, WN + 1], F32) for ri in range(2) for w in range(2)}
                      Hb = [T(f"Hb{ri}", [64, 64, 64], BF16) for ri in range(2)]
                      Uc2 = T("Uc2", [128, 32, 8, 16], BF16)
                      sc = [T(f"sc{i}", [64, 64], F32) for i in range(4)]
                      Yg = Ucm[0:64]
                      Yfm = T("Yfm", [128, 8, 512], BF16)
                      sgm = [T(f"sgm{i}", [128, 512], BF16) for i in range(2)]
                      yo = [T(f"yo{i}", [128, 512], BF16) for i in range(2)]
                      ptr = [P(f"ptr{i}", [128, 4, 128], BF16) for i in range(2)]
                      pXr = P("pXr", [64, 4, 128], F32)
                      pXi = P("pXi", [64, 4, 128], F32)
                      pY = [P(f"pY{i}", [64, 4, 128], F32) for i in range(2)]
                      pZ = [P(f"pZ{i}", [128, 512], F32) for i in range(2)]
                      for ri in range(2):
                          S.dve(lambda e, ri=ri: e.memset(Hw[(ri, 1)][:, :, WN], 0.0), w=[("hw", ri, 1)])
                      nq = 0
                      for ct in range(NCT):
                          Uv = Ud.rearrange("(ct tt hf cc s) d -> ct hf tt cc s d", tt=8, hf=2, cc=8, s=8)
                          for hf in range(2):
                              for tt in range(8):
                                  S.load(lambda e, ct=ct, hf=hf, tt=tt: e.dma_start(out=Ucm[hf * 64 + tt * 8:hf * 64 + tt * 8 + 8, :, :], in_=Uv[ct, hf, tt]), w=[("Ucm", hf, tt)])
                          UCM = [("Ucm", hf, tt) for hf in range(2) for tt in range(8)]
                          for gh in range(2):
                              S.pool(lambda e, gh=gh: e.tensor_copy(out=Uc2[:], in_=Ucm[:, :, gh * 512:(gh + 1) * 512].rearrange("c s (g h) -> c g s h", h=16)), r=UCM, w=["Uc2"])
                              for gq in range(gh * 8, gh * 8 + 8):
                                  b = gq % 2
                                  for j in range(4):
                                      gl = (gq - gh * 8) * 4 + j
                                      S.pe(lambda e, gl=gl, b=b, j=j: e.transpose(out=ptr[b][:, j, :], in_=Uc2[:, gl, :, :].rearrange("c s h -> c (s h)"), identity=ident[:]), r=["Uc2"], w=[f"ptr{b}"])
                                  if gq % 2 == 0:
                                      S.act(lambda e, gq=gq, b=b: e.activation(out=Ug[:, gq * 4:gq * 4 + 4, :], in_=ptr[b][:], func=AF.Copy), r=[f"ptr{b}"], w=[("Ug", gq)])
                                  else:
                                      S.dve(lambda e, gq=gq, b=b: e.tensor_copy(out=Ug[:, gq * 4:gq * 4 + 4, :], in_=ptr[b][:]), r=[f"ptr{b}"], w=[("Ug", gq)])
                          for gq in range(16):
                              for j in range(4):
                                  g = gq * 4 + j
                                  S.pe(lambda e, g=g, j=j: e.matmul(pXr[:, j, :], lhsT=WXr[:, g, :], rhs=Ug[:, g, :], start=True, stop=True), r=[("Ug", gq)], w=["pXr"])
                                  S.pe(lambda e, g=g, j=j: e.matmul(pXi[:, j, :], lhsT=WXi[:, g, :], rhs=Ug[:, g, :], start=True, stop=True), r=[("Ug", gq)], w=["pXi"])
                              S.act(lambda e, gq=gq: e.activation(out=Xr[:, gq * 4:gq * 4 + 4, :], in_=pXr[:], func=AF.Copy), r=["pXr"], w=["Xr"])
                              S.act(lambda e, gq=gq: e.activation(out=Xi[:, gq * 4:gq * 4 + 4, :], in_=pXi[:], func=AF.Copy), r=["pXi"], w=["Xi"])
                          for c in range(128):
                              w = (c // WN) % 2
                              k = c % WN
                              if k == 0:
                                  prv = lambda ri, w=w: Hw[(ri, 1 - w)][:, :, WN]
                                  rk = lambda ri, w=w: ("hw", ri, 1 - w)
                              else:
                                  prv = lambda ri, w=w, k=k: Hw[(ri, w)][:, :, k]
                                  rk = lambda ri, w=w: ("hw", ri, w)
                              S.dve(lambda e, prv=prv: e.tensor_tensor(out=sc[0][:], in0=A8r[:], in1=prv(0), op=ALU.mult), r=[rk(0)], w=["sc0"])
                              S.dve(lambda e, prv=prv: e.tensor_tensor(out=sc[1][:], in0=A8i[:], in1=prv(1), op=ALU.mult), r=[rk(1)], w=["sc1"])
                              S.pool(lambda e, prv=prv: e.tensor_tensor(out=sc[2][:], in0=A8r[:], in1=prv(1), op=ALU.mult), r=[rk(1)], w=["sc2"])
                              S.pool(lambda e, prv=prv: e.tensor_tensor(out=sc[3][:], in0=A8i[:], in1=prv(0), op=ALU.mult), r=[rk(0)], w=["sc3"])
                              S.dve(lambda e: e.tensor_tensor(out=sc[0][:], in0=sc[0][:], in1=sc[1][:], op=ALU.subtract), r=["sc0", "sc1"], w=["sc0"])
                              S.pool(lambda e: e.tensor_tensor(out=sc[2][:], in0=sc[2][:], in1=sc[3][:], op=ALU.add), r=["sc2", "sc3"], w=["sc2"])
                              cp = (c // 16) * 8 + (c % 8) + (64 if (c % 16) >= 8 else 0)
                              S.dve(lambda e, w=w, k=k, cp=cp: e.tensor_tensor(out=Hw[(0, w)][:, :, k + 1], in0=sc[0][:], in1=Xr[:, :, cp], op=ALU.add), r=["sc0", "Xr"], w=[("hw", 0, w)])
                              S.pool(lambda e, w=w, k=k, cp=cp: e.tensor_tensor(out=Hw[(1, w)][:, :, k + 1], in0=sc[2][:], in1=Xi[:, :, cp], op=ALU.add), r=["sc2", "Xi"], w=[("hw", 1, w)])
                              if k == WN - 1:
                                  tt = c // 16
                                  for ri in range(2):
                                      if w == 0:
                                          S.act(lambda e, ri=ri, tt=tt: e.activation(out=Hb[ri][:, :, tt * 8], in_=Hw[(ri, 0)][:, :, WN], func=AF.Copy), r=[("hw", ri, 0)], w=[("hb", ri)])
                                      else:
                                          S.act(lambda e, ri=ri, tt=tt: e.activation(out=Hb[ri][:, :, tt * 8 + 1:tt * 8 + 8], in_=Hw[(ri, 1)][:, :, 1:WN], func=AF.Copy), r=[("hw", ri, 1)], w=[("hb", ri)])
                          for gq in range(16):
                              b = gq % 2
                              for j in range(4):
                                  g = gq * 4 + j
                                  S.pe(lambda e, g=g, b=b, j=j: e.matmul(pY[b][:, j, :], lhsT=Ug[:, g, 64:128], rhs=Tm[:, g, :], start=True, stop=False), r=[("Ug", gq)], w=[f"pY{b}"])
                                  S.pe(lambda e, g=g, b=b, j=j: e.matmul(pY[b][:, j, :], lhsT=Hb[0][:, g, :], rhs=WYr[:, g, :], start=False, stop=False), r=[("hb", 0)], w=[f"pY{b}"])
                                  S.pe(lambda e, g=g, b=b, j=j: e.matmul(pY[b][:, j, :], lhsT=Hb[1][:, g, :], rhs=WYi[:, g, :], start=False, stop=True), r=[("hb", 1)], w=[f"pY{b}"])
                              S.act(lambda e, gq=gq, b=b: e.activation(out=Yg[:, :, gq * 64:(gq + 1) * 64].rearrange("c t (gl q) -> c gl t q", q=16), in_=pY[b][:].rearrange("c gl (t q) -> c gl t q", q=16), func=AF.Gelu),
                                    r=[f"pY{b}"], w=UCM)
                          for cb in range(8):
                              for th_ in range(2):
                                  b = nq % 2
                                  nq += 1
                                  for j in range(4):
                                      t = th_ * 4 + j
                                      S.pe(lambda e, cb=cb, t=t, b=b, j=j: e.transpose(out=ptr[b][:, j, 0:64], in_=Yg[:, t, cb * 128:(cb + 1) * 128], identity=ident[0:64, 0:64]), r=UCM, w=[f"ptr{b}"])
                                  S.dve(lambda e, cb=cb, th_=th_, b=b: e.tensor_copy(out=Yfm[:, cb, :].rearrange("p (c t) -> p t c", t=8)[:, th_ * 4:th_ * 4 + 4, :], in_=ptr[b][:, :, 0:64]),
                                        r=[f"ptr{b}"], w=[("Yfm", cb)])
                          YF = [("Yfm", cb) for cb in range(8)]
                          for nb in range(8):
                              b = nb % 2
                              for cb in range(8):
                                  S.pe(lambda e, nb=nb, cb=cb, b=b: e.matmul(pZ[b][:], lhsT=wglu[:, cb, nb * 128:(nb + 1) * 128], rhs=Yfm[:, cb, :], start=(cb == 0), stop=(cb == 7)), r=YF, w=[f"pZ{b}"])
                              S.act(lambda e, nb=nb, b=b: e.activation(out=sgm[b][:], in_=pZ[b][:], func=AF.Sigmoid, bias=bglu[:, nb:nb + 1]), r=[f"pZ{b}"], w=[f"sgm{b}"])
                              S.dve(lambda e, nb=nb, b=b: e.tensor_tensor(out=yo[b][:], in0=Yfm[:, nb, :], in1=sgm[b][:], op=ALU.mult), r=[f"sgm{b}"] + YF, w=[f"yo{b}"])
                              S.store(lambda e, nb=nb, b=b, ct=ct: e.dma_start(out=YC[nb * 128:(nb + 1) * 128, ct * 512:(ct + 1) * 512], in_=yo[b][:]), r=[f"yo{b}"], w=[("YCs", nb, ct)])
                      S.run()

        if "3" in phases:
            with ExitStack() as es:
                T = lambda name, shape, dt: es.enter_context(nc.sbuf_tensor(name, shape, dt))
                P = lambda name, shape, dt: es.enter_context(nc.psum_tensor(name, shape, dt))
                S = Sched(nc, es, "p3")
                kt = [T(f"kt{i}", [128, NTOK], BF16) for i in range(2)]
                qt = [T(f"qt{i}", [128, NOWN], BF16) for i in range(2)]
                vh = [T(f"vh{i}", [128, NT, 128], BF16) for i in range(2)]
                pb = {(c, i): T(f"pb{c}{i}", [128, 512], BF16) for c in range(2) for i in range(2)}
                Lr = [T(f"Lr{c}", [128, 512], F32) for c in range(2)]
                Aa = [T(f"Aa{c}", [128, 512], F32) for c in range(2)]
                Oc = T("Oc", [128, 512], F32)
                sq = T("sq3", [128, 512], BF16)
                rr = T("rr3", [128, 512], F32)
                ya = [T(f"ya{i}", [128, 512], BF16) for i in range(2)]
                pS = {(c, i): P(f"pS{c}{i}", [128, 512], F32) for c in range(2) for i in range(2)}
                pO = [P(f"pO{c}", [128, 512], F32) for c in range(2)]
                pD = [P(f"pD{c}", [128, 512], F32) for c in range(2)]
                nya = [0]

                def emit_loads(h):
                    hb = h % 2
                    S.load(lambda e, h=h, hb=hb: e.dma_start(out=kt[hb][:], in_=KT[h]), w=[f"kt{hb}"])
                    S.load(lambda e, h=h, hb=hb: e.dma_start(out=qt[hb][:], in_=QT[h]), w=[f"qt{hb}"])
                    S.load(lambda e, h=h, hb=hb: e.dma_start(out=vh[hb][:], in_=Vd.rearrange("(t p) c -> p t c", p=128)[:, :, h * 128:(h + 1) * 128]), w=[f"vh{hb}"])

                def emit_S(job):
                    h, j, t, b = job
                    hb = h % 2
                    c0 = 0 if t < 8 * j else 64 * (t - 8 * j)
                    for c in range(2):
                        S.pe(lambda e, c=c, b=b, t=t, c0=c0, j=j, hb=hb: e.matmul(pS[(c, b)][:, c0:512], lhsT=kt[hb][64 * c:64 * c + 64, t * 128:(t + 1) * 128], rhs=qt[hb][64 * c:64 * c + 64, j * 512 + c0:(j + 1) * 512], start=True, stop=True),
                             r=[f"kt{hb}", f"qt{hb}"], w=[f"pS{c}{b}"])
                    for c in range(2):
                        if t == 0:
                            S.act(lambda e, c=c, b=b, c0=c0: e.activation(out=pb[(c, b)][:, c0:512], in_=pS[(c, b)][:, c0:512], func=AF.Exp, bias=kbias0[:, 0:1]), r=[f"pS{c}{b}"], w=[f"pb{c}{b}"])
                        else:
                            S.act(lambda e, c=c, b=b, c0=c0: e.activation(out=pb[(c, b)][:, c0:512], in_=pS[(c, b)][:, c0:512], func=AF.Exp), r=[f"pS{c}{b}"], w=[f"pb{c}{b}"])

                def emit_PV(job):
                    h, j, t, b = job
                    hb = h % 2
                    last = 8 * j + 7
                    c0 = 0 if t < 8 * j else 64 * (t - 8 * j)
                    for c in range(2):
                        S.pe(lambda e, c=c, b=b, t=t, c0=c0, hb=hb, last=last: e.matmul(pO[c][:, c0:512], lhsT=vh[hb][:, t, :], rhs=pb[(c, b)][:, c0:512], start=(t == 0), stop=(t == last)),
                             r=[f"vh{hb}", f"pb{c}{b}"], w=[f"pO{c}"])
                        S.pe(lambda e, c=c, b=b, t=t, c0=c0, last=last: e.matmul(pD[c][:, c0:512], lhsT=onesb[:], rhs=pb[(c, b)][:, c0:512], start=(t == 0), stop=(t == last)),
                             r=[f"pb{c}{b}"], w=[f"pD{c}"])
                    if t != last:
                        return
                    for c in range(2):
                        S.act(lambda e, c=c: e.activation(out=Lr[c][:], in_=pD[c][:], func=AF.Ln), r=[f"pD{c}"], w=[f"Lr{c}"])
                        S.act(lambda e, c=c: e.activation(out=Lr[c][:], in_=Lr[c][:], func=AF.Exp, scale=-1.0), r=[f"Lr{c}"], w=[f"Lr{c}"])
                        S.dve(lambda e, c=c: e.tensor_tensor(out=Aa[c][:], in0=pO[c][:], in1=Lr[c][:], op=ALU.mult), r=[f"pO{c}", f"Lr{c}"], w=[f"Aa{c}"])
                    S.dve(lambda e: e.scalar_tensor_tensor(out=Oc[:], in0=Aa[1][:], scalar=nlam[:, 0:1], in1=Aa[0][:], op0=ALU.mult, op1=ALU.add), r=["Aa0", "Aa1"], w=["Oc"])
                    S.act(lambda e: e.activation(out=sq[:], in_=Oc[:], func=AF.Square), r=["Oc"], w=["sq"])
                    S.pe(lambda e: e.matmul(pD[0][:], lhsT=onesb[:], rhs=sq[:], start=True, stop=True), r=["sq"], w=["pD0"])
                    S.act(lambda e: e.activation(out=rr[:], in_=pD[0][:], func=AF.Ln, scale=1.0 / 128, bias=EPS), r=["pD0"], w=["rr"])
                    S.act(lambda e: e.activation(out=rr[:], in_=rr[:], func=AF.Exp, scale=-0.5), r=["rr"], w=["rr"])
                    yb = nya[0] % 2
                    nya[0] += 1
                    S.dve(lambda e, yb=yb: e.scalar_tensor_tensor(out=ya[yb][:], in0=Oc[:], scalar=gsub[:, 0:1], in1=rr[:], op0=ALU.mult, op1=ALU.mult), r=["Oc", "rr"], w=[f"ya{yb}"])
                    S.store(lambda e, yb=yb, h=h, j=j: e.dma_start(out=YC[1024 + h * 128:1024 + (h + 1) * 128, j * 512:(j + 1) * 512], in_=ya[yb][:]), r=[f"ya{yb}"], w=[("YCa", h, j)])

                jobs = []
                for h in range(NH):
                    for j in range(NQB):
                        for t in range(8 * j + 8):
                            jobs.append((h, j, t, len(jobs) % 2))
                prev = None
                curh = -1
                for job in jobs:
                    if job[0] != curh:
                        curh = job[0]
                        emit_loads(curh)
                    emit_S(job)
                    if prev is not None:
                        emit_PV(prev)
                    prev = job
                emit_PV(prev)
                S.run()

        if "4" in phases:
            with ExitStack() as es:
                T = lambda name, shape, dt: es.enter_context(nc.sbuf_tensor(name, shape, dt))
                P = lambda name, shape, dt: es.enter_context(nc.psum_tensor(name, shape, dt))
                S = Sched(nc, es, "p4")
                NWP = 3
                wp = [T(f"wp{i}", [128, 16, 512], BF16) for i in range(NWP)]
                aT = T("aT", [128, 64, 512], BF16)
                a16 = T("a16", [128, 16, 512], BF16)
                x1 = [T(f"x1{i}", [128, D], F32) for i in range(4)]
                xn2 = T("xn2", [128, D], BF16)
                junk = T("junk4", [128, D], BF16)
                gbc = [T(f"gbc{i}", [128, D], F32) for i in range(2)]
                stmp = [T(f"stmp{i}", [128, 512], BF16) for i in range(2)]
                ftmp = [T(f"ftmp{i}", [128, 512], F32) for i in range(2)]
                ot = [T(f"ot{i}", [128, 512], F32) for i in range(2)]
                ss = T("ss4", [128, 4], F32)
                rs = T("rs4", [128, 4], F32)
                pM = [P(f"pM{i}", [128, 512], F32) for i in range(4)]
                pA = [P(f"pA{i}", [128, 512], F32) for i in range(2)]
                pt = [P(f"pt4{i}", [128, 4, 128], BF16) for i in range(2)]
                S.load(lambda e: e.dma_start(out=gbc[0][:], in_=MODR[2:3, :].to_broadcast([128, D])), w=["gbc0"])
                S.load(lambda e: e.dma_start(out=gbc[1][:], in_=MODR[5:6, :].to_broadcast([128, D])), w=["gbc1"])
                xown = I["xs"].rearrange("(t two i) d -> t two i d", two=2, i=64)
                nwp = 0
                nf = 0
                na = 0
                no = 0
                for bk in range(NOWN // 512):
                    S.load(lambda e, bk=bk: e.dma_start(out=a16[:], in_=YC[:, bk * 512:(bk + 1) * 512].rearrange("(kb p) t -> p kb t", p=128)), w=["a16"])
                    for i in range(4):
                        for hh in range(2):
                            tl = bk * 8 + i * 2 + hh
                            S.load(lambda e, i=i, hh=hh, tl=tl: e.dma_start(out=x1[i][64 * hh:64 * hh + 64, :], in_=xown[tl, 1]), w=[f"x1{i}"])
                    for db in range(4):
                        wb = nwp % NWP
                        nwp += 1
                        S.load(lambda e, wb=wb, db=db: e.dma_start(out=wp[wb][:], in_=WOUT[:, db * 512:(db + 1) * 512].rearrange("(kb p) n -> p kb n", p=128)), w=[f"wp{wb}"])
                        for i in range(4):
                            for kb in range(NKB):
                                S.pe(lambda e, i=i, kb=kb, wb=wb: e.matmul(pM[i][:], lhsT=a16[:, kb, i * 128:(i + 1) * 128], rhs=wp[wb][:, kb, :], start=(kb == 0), stop=(kb == NKB - 1)),
                                     r=["a16", f"wp{wb}"], w=[f"pM{i}"])
                            fb = nf % 2
                            nf += 1
                            S.dve(lambda e, i=i, fb=fb, db=db: e.tensor_tensor(out=ftmp[fb][:], in0=pM[i][:], in1=gbc[0][:, db * 512:(db + 1) * 512], op=ALU.mult), r=[f"pM{i}", "gbc0"], w=[f"ftmp{fb}"])
                            S.pool(lambda e, i=i, fb=fb, db=db: e.tensor_tensor(out=x1[i][:, db * 512:(db + 1) * 512], in0=x1[i][:, db * 512:(db + 1) * 512], in1=ftmp[fb][:], op=ALU.add), r=[f"ftmp{fb}", f"x1{i}"], w=[f"x1{i}"])
                    for i in range(4):
                        S.act(lambda e, i=i: e.activation(out=junk[:], in_=x1[i][:], func=AF.Square, accum_out=ss[:, i:i + 1]), r=[f"x1{i}"], w=["junk", f"ss{i}"])
                        S.act(lambda e, i=i: e.activation(out=rs[:, i:i + 1], in_=ss[:, i:i + 1], func=AF.Ln, scale=1.0 / D, bias=EPS), r=[f"ss{i}"], w=[f"rs{i}"])
                        S.act(lambda e, i=i: e.activation(out=rs[:, i:i + 1], in_=rs[:, i:i + 1], func=AF.Exp, scale=-0.5), r=[f"rs{i}"], w=[f"rs{i}"])
                        S.act(lambda e, i=i: e.activation(out=xn2[:], in_=x1[i][:], func=AF.Copy, scale=rs[:, i:i + 1]), r=[f"x1{i}", f"rs{i}"], w=["xn2"])
                        for q in range(4):
                            pb_ = q % 2
                            for j in range(4):
                                kb = q * 4 + j
                                S.pe(lambda e, kb=kb, pb_=pb_, j=j: e.transpose(out=pt[pb_][:, j, :], in_=xn2[:, kb * 128:(kb + 1) * 128], identity=ident[:]), r=["xn2"], w=[f"pt{pb_}"])
                            for j in range(4):
                                kb = q * 4 + j
                                S.dve(lambda e, i=i, kb=kb, pb_=pb_, j=j: e.tensor_scalar(out=a16[:, kb, i * 128:(i + 1) * 128], in0=pt[pb_][:, j, :], scalar1=g2s[:, kb:kb + 1], scalar2=modv[:, 48 + kb:49 + kb], op0=ALU.mult, op1=ALU.add),
                                      r=[f"pt{pb_}"], w=["a16"])
                    for hp in range(16):
                        wb = nwp % NWP
                        nwp += 1
                        S.load(lambda e, wb=wb, hp=hp: e.dma_start(out=wp[wb][:], in_=W1[:, hp * 512:(hp + 1) * 512].rearrange("(kb p) n -> p kb n", p=128)), w=[f"wp{wb}"])
                        for hl in range(4):
                            hbk = hp * 4 + hl
                            ab = na % 2
                            na += 1
                            for kb in range(NKB):
                                S.pe(lambda e, hl=hl, kb=kb, wb=wb, ab=ab: e.matmul(pA[ab][:], lhsT=wp[wb][:, kb, hl * 128:(hl + 1) * 128], rhs=a16[:, kb, :], start=(kb == 0), stop=(kb == NKB - 1)),
                                     r=["a16", f"wp{wb}"], w=[f"pA{ab}"])
                            S.act(lambda e, ab=ab: e.activation(out=stmp[ab][:], in_=pA[ab][:], func=AF.Square), r=[f"pA{ab}"], w=[f"stmp{ab}"])
                            S.dve(lambda e, ab=ab, hbk=hbk: e.scalar_tensor_tensor(out=aT[:, hbk, :], in0=pA[ab][:], scalar=0.0, in1=stmp[ab][:], op0=ALU.is_gt, op1=ALU.mult), r=[f"pA{ab}", f"stmp{ab}"], w=[("aT", hbk)])
                    for db in range(4):
                        for hq in range(4):
                            wb = nwp % NWP
                            nwp += 1
                            S.load(lambda e, wb=wb, db=db, hq=hq: e.dma_start(out=wp[wb][:], in_=W2[hq * 2048:(hq + 1) * 2048, db * 512:(db + 1) * 512].rearrange("(hb p) n -> p hb n", p=128)), w=[f"wp{wb}"])
                            for i in range(4):
                                for hl in range(16):
                                    hbk = hq * 16 + hl
                                    S.pe(lambda e, i=i, hl=hl, hbk=hbk, wb=wb, hq=hq: e.matmul(pM[i][:], lhsT=aT[:, hbk, i * 128:(i + 1) * 128], rhs=wp[wb][:, hl, :], start=(hq == 0 and hl == 0), stop=(hq == 3 and hl == 15)),
                                         r=[("aT", hbk), f"wp{wb}"], w=[f"pM{i}"])
                        for i in range(4):
                            fb = nf % 2
                            nf += 1
                            ob = no % 2
                            no += 1
                            S.dve(lambda e, i=i, fb=fb, db=db: e.tensor_tensor(out=ftmp[fb][:], in0=pM[i][:], in1=gbc[1][:, db * 512:(db + 1) * 512], op=ALU.mult), r=[f"pM{i}", "gbc1"], w=[f"ftmp{fb}"])
                            S.pool(lambda e, i=i, fb=fb, db=db, ob=ob: e.tensor_tensor(out=ot[ob][:], in0=x1[i][:, db * 512:(db + 1) * 512], in1=ftmp[fb][:], op=ALU.add), r=[f"ftmp{fb}", f"x1{i}"], w=[f"ot{ob}"])
                            S.store(lambda e, i=i, db=db, ob=ob, bk=bk: e.dma_start(out=out[bk * 512 + i * 128:bk * 512 + (i + 1) * 128, db * 512:(db + 1) * 512], in_=ot[ob][:]), r=[f"ot{ob}"], w=[("out", bk, i, db)], eng="sp")
                S.run()
    return nc


def make_in_maps(inputs, nb=None, seq=None):
    x = np.asarray(inputs["x"], dtype=np.float32)
    B, Sq, _ = x.shape
    maps = []
    for b in range(B):
        for par in range(2):
            m = {}
            if par == 1:
                m["xs"] = np.ascontiguousarray(x[b])
            else:
                m["xs"] = np.ascontiguousarray(np.concatenate([np.zeros((64, D), np.float32), x[b][:-64]], axis=0))
            m["c"] = np.ascontiguousarray(np.asarray(inputs["c"], np.float32)[b])
            for n, shp in PARAMS:
                if n in ("c", "valid0", "kbias0"):
                    continue
                m[n] = np.ascontiguousarray(np.asarray(inputs[n], np.float32)[0])
            v0 = np.ones((128, 1), np.float32)
            k0 = np.zeros((128, 1), np.float32)
            if par == 0:
                v0[:64] = 0.0
                k0[:64] = -30000.0
            m["valid0"] = v0
            m["kbias0"] = k0
            maps.append(m)
    return maps


def assemble(results, B, Sq):
    NT = Sq // 128
    out = np.empty((B, Sq, D), np.float32)
    ov = out.reshape(B, NT, 2, 64, D)
    for b in range(B):
        for par in range(2):
            ov[b, :, par] = np.asarray(results[2 * b + par]["out"], np.float32).reshape(NT, 64, D)
    return out


def kernel(**inputs):
    x = inputs["x"]
    B, Sq, _ = x.shape
    nc = build(NT=Sq // 128)
    maps = make_in_maps(inputs)
    res = run_bass_kernel_spmd(nc, maps, core_ids=list(range(len(maps))))
    return assemble(res.results, B, Sq)
```

```python
import math
import numpy as np
from contextlib import ExitStack
import concourse.bass as bass
import concourse.mybir as mybir
from concourse.bass_utils import run_bass_kernel_spmd

F32 = mybir.dt.float32
BF16 = mybir.dt.bfloat16
I32 = mybir.dt.int32
AF = mybir.ActivationFunctionType
ALU = mybir.AluOpType
AX = mybir.AxisListType

D = 2048
NKB = 16
NH = 8
EPS = 1e-6
LAMBDA_INIT = 0.8 - 0.6 * math.exp(-0.3 * 0)
ENGS = ["pe", "act", "dve", "pool", "sp"]
SAME_ENGINE_SYNC = True
NSTORE = 6


class Sched:
    def __init__(self, nc, es, name):
        self.nc, self.es, self.name = nc, es, name
        self.ops, self.last_w, self.readers = [], {}, {}
        self.allsems = []
        self.esem = {e: self._sem(f"{name}_{e}") for e in ENGS}
        self.dsem = {}
        self.psem = [self._sem(f"{name}_st{i}") for i in range(NSTORE)]

    def _sem(self, name):
        h = self.nc.alloc_semaphore(name=name)
        self.allsems.append(h)
        return h

    def op(self, eng, fn, r=(), w=(), dma=False, dkey=None):
        deps = set()
        for k in r:
            if k in self.last_w:
                deps.add(self.last_w[k])
        for k in w:
            if k in self.last_w:
                deps.add(self.last_w[k])
            deps.update(self.readers.get(k, ()))
        idx = len(self.ops)
        self.ops.append(dict(eng=eng, fn=fn, deps=sorted(deps), dma=dma, dkey=dkey, sig=None, need=False, idx=idx))
        for k in r:
            self.readers.setdefault(k, []).append(idx)
        for k in w:
            self.last_w[k] = idx
            self.readers[k] = []
        return idx

    def pe(self, fn, r=(), w=()): return self.op("pe", fn, r, w)
    def act(self, fn, r=(), w=()): return self.op("act", fn, r, w)
    def dve(self, fn, r=(), w=()): return self.op("dve", fn, r, w)
    def pool(self, fn, r=(), w=()): return self.op("pool", fn, r, w)

    def load(self, fn, r=(), w=(), eng="sp"):
        return self.op(eng, fn, r, w, dma=True, dkey=("L", w[0]))

    def store(self, fn, r=(), w=(), eng="pool"):
        return self.op(eng, fn, r, w, dma=True, dkey=None)

    def run(self):
        nc, ops = self.nc, self.ops
        for o in ops:
            keep = []
            for d in o["deps"]:
                p = ops[d]
                if not p["dma"] and not o["dma"] and p["eng"] == o["eng"]:
                    if o["eng"] == "pe" or not SAME_ENGINE_SYNC:
                        continue
                keep.append(d)
                p["need"] = True
            o["deps"] = keep
        last = {}
        for o in ops:
            if not o["dma"]:
                last[o["eng"]] = o
        for o in last.values():
            o["need"] = True
        cnt = {e: 0 for e in ENGS}
        dcnt = {}
        pcnt = [0] * NSTORE
        plast = [None] * NSTORE
        nst = 0
        for o in ops:
            if o["dma"]:
                if o["dkey"] is None:
                    s = nst % NSTORE
                    nst += 1
                    if plast[s] is not None:
                        o["deps"].append(plast[s])
                    pcnt[s] += 16
                    o["sig"] = (self.psem[s], pcnt[s])
                    plast[s] = o["idx"]
                else:
                    k = o["dkey"]
                    if k not in self.dsem:
                        self.dsem[k] = self._sem(f"{self.name}_d{len(self.dsem)}")
                        dcnt[k] = 0
                    dcnt[k] += 16
                    o["sig"] = (self.dsem[k], dcnt[k])
            elif o["need"]:
                cnt[o["eng"]] += 1
                o["sig"] = (self.esem[o["eng"]], cnt[o["eng"]])
        finals = [(self.esem[e], cnt[e]) for e in ENGS if cnt[e] > 0]
        finals += [(self.dsem[k], dcnt[k]) for k in self.dsem]
        finals += [(self.psem[i], pcnt[i]) for i in range(NSTORE) if pcnt[i] > 0]
        by_eng = {e: [o for o in ops if o["eng"] == e] for e in ENGS}

        def body(e):
            def f(eng):
                waited = {}
                for o in by_eng[e]:
                    for d in o["deps"]:
                        sem, val = ops[d]["sig"]
                        if waited.get(id(sem), 0) < val:
                            eng.wait_ge(sem, val)
                            waited[id(sem)] = val
                    ins = o["fn"](eng)
                    if o["sig"] is not None:
                        ins.then_inc(o["sig"][0], 16 if o["dma"] else 1)
                for sem, val in finals:
                    if waited.get(id(sem), 0) < val:
                        eng.wait_ge(sem, val)
            return f

        with nc.Block() as block:
            block.tensor(body("pe"))
            block.scalar(body("act"))
            block.vector(body("dve"))
            block.gpsimd(body("pool"))
            block.sync(body("sp"))
        nc.all_engine_barrier()
        nc.clear_and_free_semaphores(self.allsems)
        nc.all_engine_barrier()


PARAMS = [
    ("c", [D]), ("w_ada", [D, 6 * D]), ("b_ada", [6 * D]), ("g_norm_mix", [D]), ("g_norm_mlp", [D]),
    ("w_in", [D, 4096]), ("ssm_lambda_re", [64, 64]), ("ssm_lambda_im", [64, 64]),
    ("ssm_b_re", [64, 64, 16]), ("ssm_b_im", [64, 64, 16]), ("ssm_c_re", [64, 16, 64]), ("ssm_c_im", [64, 16, 64]),
    ("ssm_d", [64, 16]), ("ssm_log_step", [64]), ("w_glu", [1024, 1024]), ("b_glu", [1024]),
    ("g_q", [64]), ("g_k", [64]), ("lambda_q1", [64]), ("lambda_k1", [64]), ("lambda_q2", [64]), ("lambda_k2", [64]),
    ("g_subln", [128]), ("w_out", [D, D]), ("w_mlp1", [D, 8192]), ("w_mlp2", [8192, D]),
    ("valid0", [128, 1]), ("kbias0", [128, 1]),
]


def build(NT=64, debug=False, phases="01234"):
    NTOK = NT * 128
    NOWN = NTOK // 2
    NQB = NT // 8
    NCT = NT // 8
    NB1 = NT // 4
    nc = bass.Bass("TRN2", target_bir_lowering=False)
    I = {}
    I["xs"] = nc.dram_tensor("xs", [NTOK, D], F32, kind="ExternalInput").ap()
    for n, shp in PARAMS:
        I[n] = nc.dram_tensor(n, shp, F32, kind="ExternalInput").ap()
    out = nc.dram_tensor("out", [NOWN, D], F32, kind="ExternalOutput").ap()
    sk = "ExternalOutput" if debug else "Internal"
    WIN = nc.dram_tensor("WIN", [D, 4096], BF16, kind="Internal").ap()
    WGLU = nc.dram_tensor("WGLU", [1024, 1024], BF16, kind="Internal").ap()
    WOUT = nc.dram_tensor("WOUT", [D, D], BF16, kind="Internal").ap()
    W1 = nc.dram_tensor("W1", [D, 8192], BF16, kind="Internal").ap()
    W2 = nc.dram_tensor("W2", [8192, D], BF16, kind="Internal").ap()
    KT = nc.dram_tensor("KT", [NH, 128, NTOK], BF16, kind=sk).ap()
    QT = nc.dram_tensor("QT", [NH, 128, NOWN], BF16, kind=sk).ap()
    Vd = nc.dram_tensor("Vd", [NTOK, 1024], BF16, kind=sk).ap()
    Ud = nc.dram_tensor("Ud", [NTOK, 1024], BF16, kind=sk).ap()
    YC = nc.dram_tensor("YC", [D, NOWN], BF16, kind=sk).ap()
    MODR = nc.dram_tensor("MODR", [6, D], F32, kind=sk).ap()

    with ExitStack() as pes:
        PT = lambda name, shape, dt: pes.enter_context(nc.sbuf_tensor(name, shape, dt))
        ident = PT("ident", [128, 128], BF16)
        identf = PT("identf", [128, 128], F32)
        onesf = PT("onesf", [128, 128], F32)
        onesb = PT("onesb", [128, 128], BF16)
        bd64 = PT("bd64", [128, 128], BF16)
        cact = PT("cact", [128, 16], F32)
        modv = PT("modv", [128, 96], F32)
        bada = PT("bada", [128, 96], F32)
        g1s = PT("g1s", [128, 16], F32)
        g2s = PT("g2s", [128, 16], F32)
        gtmp = PT("gtmp", [128, 16], F32)
        valid0 = PT("valid0s", [128, 1], F32)
        kbias0 = PT("kbias0s", [128, 1], F32)
        gq2 = PT("gq2", [128, 1], F32)
        gk2 = PT("gk2", [128, 1], F32)
        gsub = PT("gsub", [128, 1], F32)
        nlam = PT("nlam", [128, 1], F32)
        lamt = PT("lamt", [128, 4, 64], F32)
        lamr = PT("lamr", [128, 4], F32)

        def mod_vectors(S, T, P, vecs, tag):
            wt = [T(f"wada{tag}{i}", [128, 16, 256], F32) for i in range(2)]
            pm = P(f"pmod{tag}", [128, 96], F32)
            n = 0
            for v in vecs:
                for cb in range(8):
                    b = n % 2
                    n += 1
                    col0 = v * D + cb * 256
                    S.load(lambda e, b=b, col0=col0: e.dma_start(out=wt[b][:], in_=I["w_ada"][:, col0:col0 + 256].rearrange("(kb p) n -> p kb n", p=128)),
                           w=[f"wada{b}"])
                    for j in range(2):
                        col = v * 16 + cb * 2 + j
                        for kb in range(NKB):
                            S.pe(lambda e, b=b, j=j, kb=kb, col=col: e.matmul(pm[:, col:col + 1], lhsT=wt[b][:, kb, j * 128:(j + 1) * 128], rhs=cact[:, kb:kb + 1], start=(kb == 0), stop=(kb == NKB - 1)),
                                 r=[f"wada{b}", "cact"], w=["pmod"])
                S.dve(lambda e, v=v: e.tensor_tensor(out=modv[:, v * 16:(v + 1) * 16], in0=pm[:, v * 16:(v + 1) * 16], in1=bada[:, v * 16:(v + 1) * 16], op=ALU.add),
                      r=["pmod", "bada"], w=[f"modv{v}"])
                S.store(lambda e, v=v: e.dma_start(out=MODR[v].rearrange("(kb p) -> p kb", p=128), in_=modv[:, v * 16:(v + 1) * 16], allow_slow_non_contiguous=True),
                        r=[f"modv{v}"], w=[f"MODR{v}"])

        def convert(S, src, dst, rows, step=128):
            for r0 in range(0, rows, step):
                S.store(lambda e, r0=r0: e.dma_start(out=dst[r0:r0 + step, :], in_=src[r0:r0 + step, :]), w=[("cv", id(dst), r0)])

        if "0" in phases:
            with ExitStack() as es:
                T = lambda name, shape, dt: es.enter_context(nc.sbuf_tensor(name, shape, dt))
                P = lambda name, shape, dt: es.enter_context(nc.psum_tensor(name, shape, dt))
                S = Sched(nc, es, "p0")
                convert(S, I["w_in"], WIN, D)
                S.pool(lambda e: e.memset(onesf[:], 1.0), w=["onesf"])
                S.pool(lambda e: e.memset(onesb[:], 1.0), w=["onesb"])
                S.pool(lambda e: e.affine_select(out=identf[:], in_=onesf[:], pattern=[[1, 128]], compare_op=ALU.is_equal, fill=0.0, base=0, channel_multiplier=-1), r=["onesf"], w=["identf"])
                S.pool(lambda e: e.tensor_copy(out=ident[:], in_=identf[:]), r=["identf"], w=["ident"])
                S.pool(lambda e: e.memset(bd64[:], 0.0), w=["bd64"])
                S.pool(lambda e: e.memset(bd64[0:64, 0:64], 1.0), w=["bd64"])
                S.pool(lambda e: e.memset(bd64[64:128, 64:128], 1.0), w=["bd64"])
                S.load(lambda e: e.dma_start(out=cact[:], in_=I["c"].rearrange("(kb p) -> p kb", p=128), allow_slow_non_contiguous=True), w=["cact"])
                S.load(lambda e: e.dma_start(out=bada[:], in_=I["b_ada"].rearrange("(j p) -> p j", p=128), allow_slow_non_contiguous=True), w=["bada"])
                S.load(lambda e: e.dma_start(out=g1s[:], in_=I["g_norm_mix"].rearrange("(kb p) -> p kb", p=128), allow_slow_non_contiguous=True), w=["g1s"])
                S.load(lambda e: e.dma_start(out=g2s[:], in_=I["g_norm_mlp"].rearrange("(kb p) -> p kb", p=128), allow_slow_non_contiguous=True), w=["g2s"])
                S.load(lambda e: e.dma_start(out=valid0[:], in_=I["valid0"]), w=["valid0"])
                S.load(lambda e: e.dma_start(out=kbias0[:], in_=I["kbias0"]), w=["kbias0"])
                for hh in range(2):
                    S.load(lambda e, hh=hh: e.dma_start(out=gq2[64 * hh:64 * hh + 64, :], in_=I["g_q"].rearrange("(p o) -> p o", o=1)), w=[f"gq2{hh}"])
                    S.load(lambda e, hh=hh: e.dma_start(out=gk2[64 * hh:64 * hh + 64, :], in_=I["g_k"].rearrange("(p o) -> p o", o=1)), w=[f"gk2{hh}"])
                S.load(lambda e: e.dma_start(out=gsub[:], in_=I["g_subln"].rearrange("(p o) -> p o", o=1)), w=["gsub"])
                for i, nme in enumerate(["lambda_q1", "lambda_k1", "lambda_q2", "lambda_k2"]):
                    S.load(lambda e, i=i, nme=nme: e.dma_start(out=lamt[:, i, :], in_=I[nme].rearrange("(o n) -> o n", o=1).to_broadcast([128, 64])), w=[f"lamt{i}"])
                S.act(lambda e: e.activation(out=cact[:], in_=cact[:], func=AF.Silu), r=["cact"], w=["cact"])
                S.dve(lambda e: e.tensor_scalar(out=gq2[:], in0=gq2[:], scalar1=0.125, scalar2=None, op0=ALU.mult), r=["gq20", "gq21"], w=["gq2"])
                S.dve(lambda e: e.tensor_scalar(out=gsub[:], in0=gsub[:], scalar1=1.0 - LAMBDA_INIT, scalar2=None, op0=ALU.mult), r=["gsub"], w=["gsub"])
                S.dve(lambda e: e.tensor_tensor(out=lamt[:, 0, :], in0=lamt[:, 0, :], in1=lamt[:, 1, :], op=ALU.mult), r=["lamt0", "lamt1"], w=["lamt0"])
                S.dve(lambda e: e.tensor_tensor(out=lamt[:, 2, :], in0=lamt[:, 2, :], in1=lamt[:, 3, :], op=ALU.mult), r=["lamt2", "lamt3"], w=["lamt2"])
                S.dve(lambda e: e.tensor_reduce(out=lamr[:, 0:1], in_=lamt[:, 0, :], axis=AX.X, op=ALU.add), r=["lamt0"], w=["lamr0"])
                S.dve(lambda e: e.tensor_reduce(out=lamr[:, 1:2], in_=lamt[:, 2, :], axis=AX.X, op=ALU.add), r=["lamt2"], w=["lamr1"])
                S.act(lambda e: e.activation(out=lamr[:, 2:4], in_=lamr[:, 0:2], func=AF.Exp), r=["lamr0", "lamr1"], w=["lamr2"])
                S.dve(lambda e: e.tensor_tensor(out=nlam[:], in0=lamr[:, 3:4], in1=lamr[:, 2:3], op=ALU.subtract), r=["lamr2"], w=["nlam"])
                S.dve(lambda e: e.tensor_scalar(out=nlam[:], in0=nlam[:], scalar1=-LAMBDA_INIT, scalar2=None, op0=ALU.add), r=["nlam"], w=["nlam"])
                mod_vectors(S, T, P, [0, 1], "a")
                S.dve(lambda e: e.tensor_scalar(out=gtmp[:], in0=modv[:, 16:32], scalar1=1.0, scalar2=None, op0=ALU.add), r=["modv1"], w=["gtmp"])
                S.dve(lambda e: e.tensor_tensor(out=g1s[:], in0=g1s[:], in1=gtmp[:], op=ALU.mult), r=["gtmp", "g1s"], w=["g1s"])
                S.run()

        if "1" in phases:
            with ExitStack() as es:
                T = lambda name, shape, dt: es.enter_context(nc.sbuf_tensor(name, shape, dt))
                P = lambda name, shape, dt: es.enter_context(nc.psum_tensor(name, shape, dt))
                S = Sched(nc, es, "p1")
                win = T("win", [128, NKB, 4096], BF16)
                xt = [T(f"xt{i}", [128, D], F32) for i in range(2)]
                junk = T("junk", [128, D], BF16)
                xn = [T(f"xn{i}", [128, D], BF16) for i in range(2)]
                ss = T("ss", [128, 2], F32)
                rs = T("rs", [128, 2], F32)
                hn = [T(f"hn{i}", [128, NKB, 512], BF16) for i in range(2)]
                sqb = [T(f"sqb{i}", [128, 512], BF16) for i in range(2)]
                lf = [T(f"lf{i}", [128, 512], F32) for i in range(2)]
                ko = [T(f"ko{i}", [128, 512], BF16) for i in range(2)]
                vo = [T(f"vo{i}", [128, 1024], BF16) for i in range(2)]
                pt = [P(f"pt{i}", [128, 4, 128], BF16) for i in range(2)]
                pk = [P(f"pk{i}", [128, 512], F32) for i in range(3)]
                pn = [P(f"pn{i}", [128, 512], F32) for i in range(2)]
                for kb in range(NKB):
                    S.load(lambda e, kb=kb: e.dma_start(out=win[:, kb, :], in_=WIN[kb * 128:(kb + 1) * 128, :]), w=[f"win{kb}"])
                WINR = [f"win{kb}" for kb in range(NKB)]
                convert(S, I["w_glu"], WGLU, 1024)
                convert(S, I["w_out"], WOUT, D)
                convert(S, I["w_mlp1"], W1, D)
                convert(S, I["w_mlp2"], W2, 8192, step=512)
                npk = 0
                nep = 0
                nvo = 0
                for b in range(NB1):
                    hb = b % 2
                    for i in range(4):
                        tg = b * 4 + i
                        xb = tg % 2
                        S.load(lambda e, xb=xb, tg=tg: e.dma_start(out=xt[xb][:], in_=I["xs"][tg * 128:(tg + 1) * 128, :]), w=[f"xt{xb}"])
                        S.act(lambda e, xb=xb: e.activation(out=junk[:], in_=xt[xb][:], func=AF.Square, accum_out=ss[:, xb:xb + 1]), r=[f"xt{xb}"], w=["junk", f"ss{xb}"])
                        S.act(lambda e, xb=xb: e.activation(out=rs[:, xb:xb + 1], in_=ss[:, xb:xb + 1], func=AF.Ln, scale=1.0 / D, bias=EPS), r=[f"ss{xb}"], w=[f"rs{xb}"])
                        S.act(lambda e, xb=xb: e.activation(out=rs[:, xb:xb + 1], in_=rs[:, xb:xb + 1], func=AF.Exp, scale=-0.5), r=[f"rs{xb}"], w=[f"rs{xb}"])
                        S.act(lambda e, xb=xb: e.activation(out=xn[xb][:], in_=xt[xb][:], func=AF.Copy, scale=rs[:, xb:xb + 1]), r=[f"xt{xb}", f"rs{xb}"], w=[f"xn{xb}"])
                        for q in range(4):
                            pb = q % 2
                            for j in range(4):
                                kb = q * 4 + j
                                S.pe(lambda e, xb=xb, kb=kb, pb=pb, j=j: e.transpose(out=pt[pb][:, j, :], in_=xn[xb][:, kb * 128:(kb + 1) * 128], identity=ident[:]),
                                     r=[f"xn{xb}"], w=[f"pt{pb}"])
                            for j in range(4):
                                kb = q * 4 + j
                                S.dve(lambda e, hb=hb, i=i, kb=kb, pb=pb, j=j: e.tensor_scalar(out=hn[hb][:, kb, i * 128:(i + 1) * 128], in0=pt[pb][:, j, :], scalar1=g1s[:, kb:kb + 1], scalar2=modv[:, kb:kb + 1], op0=ALU.mult, op1=ALU.add),
                                      r=[f"pt{pb}"], w=[f"hn{hb}"])
                    pend = [None]
                    for which in ("k", "q"):
                        for h in range(NH):
                            pkb = npk % 3
                            npk += 1
                            eb = nep % 2
                            nep += 1
                            if which == "k":
                                N = 512
                                col0 = 2048 + h * 128
                                rhs_of = lambda kb, hb=hb: hn[hb][:, kb, :]
                                gcol = gk2
                            else:
                                N = 256
                                col0 = 1024 + h * 128
                                rhs_of = lambda kb, hb=hb: hn[hb][:, kb, :].rearrange("p (t two i) -> p t two i", two=2, i=64)[:, :, 1, :]
                                gcol = gq2
                            for kb in range(NKB):
                                S.pe(lambda e, kb=kb, pkb=pkb, col0=col0, N=N, rhs_of=rhs_of: e.matmul(pk[pkb][:, 0:N], lhsT=win[:, kb, col0:col0 + 128], rhs=rhs_of(kb), start=(kb == 0), stop=(kb == NKB - 1)),
                                     r=[f"hn{hb}", f"win{kb}"], w=[f"pk{pkb}"])
                            if pend[0] is not None:
                                pend[0]()
                            def _epi(pkb=pkb, eb=eb, N=N, gcol=gcol, which=which, h=h, b=b):
                                S.act(lambda e, pkb=pkb, eb=eb, N=N: e.activation(out=sqb[eb][:, 0:N], in_=pk[pkb][:, 0:N], func=AF.Square), r=[f"pk{pkb}"], w=[f"sqb{eb}"])
                                S.pe(lambda e, eb=eb, N=N: e.matmul(pn[eb][:, 0:N], lhsT=bd64[:], rhs=sqb[eb][:, 0:N], start=True, stop=True), r=[f"sqb{eb}"], w=[f"pn{eb}"])
                                S.act(lambda e, eb=eb, N=N: e.activation(out=lf[eb][:, 0:N], in_=pn[eb][:, 0:N], func=AF.Ln, scale=1.0 / 64, bias=EPS), r=[f"pn{eb}"], w=[f"lf{eb}"])
                                S.act(lambda e, eb=eb, N=N: e.activation(out=lf[eb][:, 0:N], in_=lf[eb][:, 0:N], func=AF.Exp, scale=-0.5), r=[f"lf{eb}"], w=[f"lf{eb}"])
                                S.dve(lambda e, pkb=pkb, eb=eb, N=N, gcol=gcol: e.scalar_tensor_tensor(out=ko[eb][:, 0:N], in0=pk[pkb][:, 0:N], scalar=gcol[:, 0:1], in1=lf[eb][:, 0:N], op0=ALU.mult, op1=ALU.mult),
                                      r=[f"pk{pkb}", f"lf{eb}"], w=[f"ko{eb}"])
                                if which == "k":
                                    S.store(lambda e, eb=eb, h=h, b=b: e.dma_start(out=KT[h][:, b * 512:(b + 1) * 512], in_=ko[eb][:, 0:512]), r=[f"ko{eb}"], w=[("KT", h, b)])
                                else:
                                    S.store(lambda e, eb=eb, h=h, b=b: e.dma_start(out=QT[h][:, b * 256:(b + 1) * 256], in_=ko[eb][:, 0:256]), r=[f"ko{eb}"], w=[("QT", h, b)])
                            pend[0] = _epi
                    if pend[0] is not None:
                        pend[0]()
                        pend[0] = None
                    for which in ("v", "u"):
                        for i in range(4):
                            tg = b * 4 + i
                            vb = nvo % 2
                            nvo += 1
                            for nb in range(2):
                                pkb = npk % 3
                                npk += 1
                                col0 = (3072 if which == "v" else 0) + nb * 512
                                for kb in range(NKB):
                                    S.pe(lambda e, kb=kb, pkb=pkb, col0=col0, i=i, hb=hb: e.matmul(pk[pkb][:], lhsT=hn[hb][:, kb, i * 128:(i + 1) * 128], rhs=win[:, kb, col0:col0 + 512], start=(kb == 0), stop=(kb == NKB - 1)),
                                         r=[f"hn{hb}", f"win{kb}"], w=[f"pk{pkb}"])
                                if which == "u" and tg == 0:
                                    S.act(lambda e, pkb=pkb, vb=vb, nb=nb: e.activation(out=vo[vb][:, nb * 512:(nb + 1) * 512], in_=pk[pkb][:], func=AF.Copy, scale=valid0[:, 0:1]), r=[f"pk{pkb}"], w=[f"vo{vb}"])
                                else:
                                    S.act(lambda e, pkb=pkb, vb=vb, nb=nb: e.activation(out=vo[vb][:, nb * 512:(nb + 1) * 512], in_=pk[pkb][:], func=AF.Copy), r=[f"pk{pkb}"], w=[f"vo{vb}"])
                            dst = Vd if which == "v" else Ud
                            S.store(lambda e, vb=vb, tg=tg, dst=dst: e.dma_start(out=dst[tg * 128:(tg + 1) * 128, :], in_=vo[vb][:]), r=[f"vo{vb}"], w=[(which, tg)])
                S.run()

        if "2" in phases or "a" in phases or "b" in phases:
            with ExitStack() as wes:
                WT = lambda name, shape, dt: wes.enter_context(nc.sbuf_tensor(name, shape, dt))
                Tm = WT("Tm", [128, 64, 128], BF16)
                WXr = WT("WXr", [128, 64, 64], BF16)
                WXi = WT("WXi", [128, 64, 64], BF16)
                WYr = WT("WYr", [64, 64, 128], BF16)
                WYi = WT("WYi", [64, 64, 128], BF16)
                A8r = WT("A8r", [64, 64], F32)
                A8i = WT("A8i", [64, 64], F32)
                wglu = WT("wglu", [128, 8, 1024], BF16)
                bglu = WT("bglu", [128, 8], F32)
                with ExitStack() as es:
                    T = lambda name, shape, dt: es.enter_context(nc.sbuf_tensor(name, shape, dt))
                    P = lambda name, shape, dt: es.enter_context(nc.psum_tensor(name, shape, dt))
                    S = Sched(nc, es, "p2v")
                    mod_vectors(S, T, P, [2, 3, 4, 5], "b")
                    S.dve(lambda e: e.tensor_scalar(out=gtmp[:], in0=modv[:, 64:80], scalar1=1.0, scalar2=None, op0=ALU.add), r=["modv4"], w=["gtmp"])
                    S.dve(lambda e: e.tensor_tensor(out=g2s[:], in0=g2s[:], in1=gtmp[:], op=ALU.mult), r=["gtmp"], w=["g2s"])
                    S.run()
                with ExitStack() as es:
                  if "2" in phases or "b" in phases:
                      T = lambda name, shape, dt: es.enter_context(nc.sbuf_tensor(name, shape, dt))
                      P = lambda name, shape, dt: es.enter_context(nc.psum_tensor(name, shape, dt))
                      S = Sched(nc, es, "p2s")
                      import os as _os
                      CUT = int(_os.environ.get("P2S_CUT", "99"))
                      class _Stop(Exception): pass
                      def stage(n):
                          if CUT < n: raise _Stop()
                      try:
                          for cb in range(8):
                              S.load(lambda e, cb=cb: e.dma_start(out=wglu[:, cb, :], in_=WGLU[cb * 128:(cb + 1) * 128, :]), w=[f"wglu{cb}"])
                          S.load(lambda e: e.dma_start(out=bglu[:], in_=I["b_glu"].rearrange("(j p) -> p j", p=128), allow_slow_non_contiguous=True), w=["bglu"])
                          lr = T("lr", [64, 64], F32); li = T("li", [64, 64], F32); ls = T("ls", [64, 64], F32)
                          Br = T("Br", [64, 64, 16], F32); Bi = T("Bi", [64, 64, 16], F32)
                          Cn = [T("Cnr", [128, 8, 64], F32), T("Cni", [128, 8, 64], F32)]
                          Cp = [T("Cpr", [64, 64, 16], F32), T("Cpi", [64, 64, 16], F32)]
                          dvec = T("dvec", [128, 64], F32)
                          mask01 = T("mask01", [128, 128], F32)
                          S.load(lambda e: e.dma_start(out=lr[:], in_=I["ssm_lambda_re"].rearrange("g p -> p g"), allow_slow_non_contiguous=True), w=["lr"])
                          S.load(lambda e: e.dma_start(out=li[:], in_=I["ssm_lambda_im"].rearrange("g p -> p g"), allow_slow_non_contiguous=True), w=["li"])
                          S.load(lambda e: e.dma_start(out=ls[:], in_=I["ssm_log_step"].rearrange("(o g) -> o g", o=1).to_broadcast([64, 64])), w=["ls"])
                          S.load(lambda e: e.dma_start(out=Br[:], in_=I["ssm_b_re"].rearrange("g p h -> p g h")), w=["Br"])
                          S.load(lambda e: e.dma_start(out=Bi[:], in_=I["ssm_b_im"].rearrange("g p h -> p g h")), w=["Bi"])
                          for ri, nme in enumerate(["ssm_c_re", "ssm_c_im"]):
                              S.load(lambda e, ri=ri, nme=nme: e.dma_start(out=Cn[ri][:], in_=I[nme].rearrange("(gb g8) q p -> (g8 q) gb p", g8=8)), w=[f"Cn{ri}"])
                          for s in range(8):
                              S.load(lambda e, s=s: e.dma_start(out=dvec[16 * s:16 * s + 16, :], in_=I["ssm_d"].rearrange("g h -> h g"), allow_slow_non_contiguous=True), w=[f"dvec{s}"])
                          DVR = [f"dvec{s}" for s in range(8)]
                          S.pool(lambda e: e.affine_select(out=mask01[:], in_=onesf[:], pattern=[[16, 8], [0, 16]], compare_op=ALU.is_ge, fill=0.0, base=15, channel_multiplier=-1), w=["mask01"])
                          pc = [P(f"pc{i}", [64, 128], F32) for i in range(2)]
                          n = 0
                          for ri in range(2):
                              for gb in range(8):
                                  b = n % 2
                                  n += 1
                                  S.pe(lambda e, ri=ri, gb=gb, b=b: e.transpose(out=pc[b][:], in_=Cn[ri][:, gb, :], identity=identf[:]), r=[f"Cn{ri}"], w=[f"pc{b}"])
                                  S.act(lambda e, ri=ri, gb=gb, b=b: e.activation(out=Cp[ri][:, gb * 8:(gb + 1) * 8, :].rearrange("p g q -> p (g q)"), in_=pc[b][:], func=AF.Copy), r=[f"pc{b}"], w=[f"Cp{ri}"])
                          stage(2)
                          NK = 25
                          kvi = T("kvi", [64, NK], I32); kv = T("kv", [64, NK], F32)
                          S.pool(lambda e: e.iota(kvi[:, 0:8], pattern=[[-1, 8]], base=0, channel_multiplier=0), w=["kvi"])
                          S.pool(lambda e: e.iota(kvi[:, 8:16], pattern=[[-1, 8]], base=7, channel_multiplier=0), w=["kvi"])
                          S.pool(lambda e: e.iota(kvi[:, 16:25], pattern=[[1, 9]], base=0, channel_multiplier=0), w=["kvi"])
                          S.dve(lambda e: e.tensor_copy(out=kv[:], in_=kvi[:]), r=["kvi"], w=["kv"])
                          dt_ = T("dt_", [64, 64], F32); mu = T("mu", [64, 64], F32); th = T("th", [64, 64], F32)
                          S.act(lambda e: e.activation(out=dt_[:], in_=ls[:], func=AF.Exp), r=["ls"], w=["dt"])
                          S.dve(lambda e: e.tensor_tensor(out=mu[:], in0=dt_[:], in1=lr[:], op=ALU.mult), r=["dt", "lr"], w=["mu"])
                          S.dve(lambda e: e.tensor_tensor(out=th[:], in0=dt_[:], in1=li[:], op=ALU.mult), r=["dt", "li"], w=["th"])
                          shp = [64, 64, NK]
                          ANG = T("ANG", shp, F32); MAG = T("MAG", shp, F32); V0 = T("V0", shp, F32); V1 = T("V1", shp, F32)
                          VI = T("VI", shp, I32); AR = T("AR", shp, F32); AI = T("AI", shp, F32)
                          kvb = lambda: kv[:].unsqueeze(1).to_broadcast(shp)
                          S.dve(lambda e: e.tensor_tensor(out=ANG[:], in0=th[:].unsqueeze(2).to_broadcast(shp), in1=kvb(), op=ALU.mult), r=["th", "kv"], w=["ANG"])
                          S.dve(lambda e: e.tensor_tensor(out=MAG[:], in0=mu[:].unsqueeze(2).to_broadcast(shp), in1=kvb(), op=ALU.mult), r=["mu", "kv"], w=["MAG"])
                          S.act(lambda e: e.activation(out=MAG[:], in_=MAG[:], func=AF.Exp), r=["MAG"], w=["MAG"])
                          for which, off, dst in (("s", 64.0, AI), ("c", 64.25, AR)):
                              S.dve(lambda e, off=off: e.tensor_scalar(out=V0[:], in0=ANG[:], scalar1=1.0 / (2 * math.pi), scalar2=off, op0=ALU.mult, op1=ALU.add), r=["ANG"], w=["V0"])
                              S.dve(lambda e: e.tensor_copy(out=VI[:], in_=V0[:]), r=["V0"], w=["VI"])
                              S.dve(lambda e: e.tensor_copy(out=V1[:], in_=VI[:]), r=["VI"], w=["V1"])
                              S.dve(lambda e: e.tensor_tensor(out=V0[:], in0=V0[:], in1=V1[:], op=ALU.subtract), r=["V0", "V1"], w=["V0"])
                              S.dve(lambda e: e.tensor_scalar(out=V1[:], in0=V0[:], scalar1=0.5, scalar2=None, op0=ALU.is_gt), r=["V0"], w=["V1"])
                              S.dve(lambda e: e.tensor_tensor(out=V0[:], in0=V0[:], in1=V1[:], op=ALU.subtract), r=["V0", "V1"], w=["V0"])
                              S.dve(lambda e: e.tensor_scalar(out=V1[:], in0=V0[:], scalar1=-0.5, scalar2=None, op0=ALU.is_lt), r=["V0"], w=["V1"])
                              S.dve(lambda e: e.tensor_tensor(out=V0[:], in0=V0[:], in1=V1[:], op=ALU.add), r=["V0", "V1"], w=["V0"])
                              S.act(lambda e, dst=dst: e.activation(out=dst[:], in_=V0[:], func=AF.Sin, scale=6.283184), r=["V0"], w=[which + "in"])
                              S.dve(lambda e, dst=dst: e.tensor_tensor(out=dst[:], in0=dst[:], in1=MAG[:], op=ALU.mult), r=[which + "in", "MAG"], w=["A" + which])
                          AW = ["As", "Ac"]
                          S.act(lambda e: e.activation(out=A8r[:], in_=AR[:, :, 24], func=AF.Copy), r=AW, w=["A8r"])
                          S.act(lambda e: e.activation(out=A8i[:], in_=AI[:, :, 24], func=AF.Copy), r=AW, w=["A8i"])
                          stage(3)
                          zr = T("zr", [64, 64], F32); den = T("den", [64, 64], F32); t0 = T("t0", [64, 64], F32); t1 = T("t1", [64, 64], F32)
                          kr = T("kr", [64, 64], F32); ki = T("ki", [64, 64], F32)
                          S.dve(lambda e: e.tensor_scalar(out=zr[:], in0=AR[:, :, 17], scalar1=-1.0, scalar2=None, op0=ALU.add), r=AW, w=["zr"])
                          S.dve(lambda e: e.tensor_tensor(out=den[:], in0=lr[:], in1=lr[:], op=ALU.mult), r=["lr"], w=["den"])
                          S.dve(lambda e: e.tensor_tensor(out=t0[:], in0=li[:], in1=li[:], op=ALU.mult), r=["li"], w=["t0"])
                          S.dve(lambda e: e.tensor_tensor(out=den[:], in0=den[:], in1=t0[:], op=ALU.add), r=["den", "t0"], w=["den"])
                          S.dve(lambda e: e.reciprocal(out=den[:], in_=den[:]), r=["den"], w=["den"])
                          S.dve(lambda e: e.tensor_tensor(out=t0[:], in0=zr[:], in1=lr[:], op=ALU.mult), r=["zr", "lr"], w=["t0"])
                          S.dve(lambda e: e.tensor_tensor(out=t1[:], in0=AI[:, :, 17], in1=li[:], op=ALU.mult), r=AW + ["li"], w=["t1"])
                          S.dve(lambda e: e.tensor_tensor(out=t0[:], in0=t0[:], in1=t1[:], op=ALU.add), r=["t0", "t1"], w=["t0"])
                          S.dve(lambda e: e.tensor_tensor(out=kr[:], in0=t0[:], in1=den[:], op=ALU.mult), r=["t0", "den"], w=["kr"])
                          S.dve(lambda e: e.tensor_tensor(out=t0[:], in0=AI[:, :, 17], in1=lr[:], op=ALU.mult), r=AW + ["lr"], w=["t0"])
                          S.dve(lambda e: e.tensor_tensor(out=t1[:], in0=zr[:], in1=li[:], op=ALU.mult), r=["zr", "li"], w=["t1"])
                          S.dve(lambda e: e.tensor_tensor(out=t0[:], in0=t0[:], in1=t1[:], op=ALU.subtract), r=["t0", "t1"], w=["t0"])
                          S.dve(lambda e: e.tensor_tensor(out=ki[:], in0=t0[:], in1=den[:], op=ALU.mult), r=["t0", "den"], w=["ki"])
                          cr_ = T("cr_", [64, 64, 16], F32); ci_ = T("ci_", [64, 64, 16], F32); c0 = T("c0", [64, 64, 16], F32)
                          s16 = [64, 64, 16]
                          krb = lambda: kr[:].unsqueeze(2).to_broadcast(s16)
                          kib = lambda: ki[:].unsqueeze(2).to_broadcast(s16)
                          S.dve(lambda e: e.tensor_tensor(out=cr_[:], in0=AR[:, :, 0:16], in1=krb(), op=ALU.mult), r=AW + ["kr"], w=["cr"])
                          S.dve(lambda e: e.tensor_tensor(out=c0[:], in0=AI[:, :, 0:16], in1=kib(), op=ALU.mult), r=AW + ["ki"], w=["c0"])
                          S.dve(lambda e: e.tensor_tensor(out=cr_[:], in0=cr_[:], in1=c0[:], op=ALU.subtract), r=["cr", "c0"], w=["cr"])
                          S.dve(lambda e: e.tensor_tensor(out=ci_[:], in0=AR[:, :, 0:16], in1=kib(), op=ALU.mult), r=AW + ["ki"], w=["ci"])
                          S.dve(lambda e: e.tensor_tensor(out=c0[:], in0=AI[:, :, 0:16], in1=krb(), op=ALU.mult), r=AW + ["kr"], w=["c0"])
                          S.dve(lambda e: e.tensor_tensor(out=ci_[:], in0=ci_[:], in1=c0[:], op=ALU.add), r=["ci", "c0"], w=["ci"])
                          stage(4)
                          GC = 4
                          se = [64, GC, 8, 16]
                          Er = T("Er", se, F32); Ei = T("Ei", se, F32); Xr_ = T("Xr_", se, F32); Xi_ = T("Xi_", se, F32)
                          u0 = T("u0", se, F32); u1 = T("u1", se, F32)
                          sg9 = [64, GC, 9, 16]
                          Gr = T("Gr", sg9, F32); Gm = T("Gm", sg9, F32); w0 = T("w0", sg9, F32); w1 = T("w1", sg9, F32)
                          tmk = [T(f"tmk{i}", [128, 128], F32) for i in range(2)]
                          pT = [P(f"pT{i}", [128, 128], F32) for i in range(2)]
                          pX = [P(f"pX{i}", [128, 64], BF16) for i in range(2)]
                          Xrb = T("Xrb", se, BF16); Xib = T("Xib", se, BF16)
                          nT = 0
                          nX = 0
                          for gc in range(64 // GC):
                              g0 = gc * GC
                              gs = slice(g0, g0 + GC)
                              for (o0, dr, di, tag) in ((0, Er, Ei, "E"), (8, Xr_, Xi_, "X")):
                                  cb_ = lambda t_, o0=o0, gs=gs: t_[:, gs, o0:o0 + 8].unsqueeze(3).to_broadcast(se)
                                  bb_ = lambda t_, gs=gs: t_[:, gs, :].unsqueeze(2).to_broadcast(se)
                                  S.dve(lambda e, cb_=cb_, bb_=bb_: e.tensor_tensor(out=u0[:], in0=cb_(cr_), in1=bb_(Br), op=ALU.mult), r=["cr", "Br"], w=["u0"])
                                  S.dve(lambda e, cb_=cb_, bb_=bb_: e.tensor_tensor(out=u1[:], in0=cb_(ci_), in1=bb_(Bi), op=ALU.mult), r=["ci", "Bi"], w=["u1"])
                                  S.dve(lambda e, dr=dr: e.tensor_tensor(out=dr[:], in0=u0[:], in1=u1[:], op=ALU.subtract), r=["u0", "u1"], w=[tag + "r"])
                                  S.dve(lambda e, cb_=cb_, bb_=bb_: e.tensor_tensor(out=u0[:], in0=cb_(cr_), in1=bb_(Bi), op=ALU.mult), r=["cr", "Bi"], w=["u0"])
                                  S.dve(lambda e, cb_=cb_, bb_=bb_: e.tensor_tensor(out=u1[:], in0=cb_(ci_), in1=bb_(Br), op=ALU.mult), r=["ci", "Br"], w=["u1"])
                                  S.dve(lambda e, di=di: e.tensor_tensor(out=di[:], in0=u0[:], in1=u1[:], op=ALU.add), r=["u0", "u1"], w=[tag + "i"])
                              S.act(lambda e: e.activation(out=Xrb[:], in_=Xr_[:], func=AF.Copy), r=["Xr"], w=["Xrb"])
                              S.act(lambda e: e.activation(out=Xib[:], in_=Xi_[:], func=AF.Copy), r=["Xi"], w=["Xib"])
                              stage(5)
                              ab_ = lambda t_, gs=gs: t_[:, gs, 16:25].unsqueeze(3).to_broadcast(sg9)
                              cc_ = lambda t_, gs=gs: t_[:, gs, :].unsqueeze(2).to_broadcast(sg9)
                              S.dve(lambda e, ab_=ab_, cc_=cc_: e.tensor_tensor(out=w0[:], in0=ab_(AR), in1=cc_(Cp[0]), op=ALU.mult), r=AW + ["Cp0"], w=["w0"])
                              S.dve(lambda e, ab_=ab_, cc_=cc_: e.tensor_tensor(out=w1[:], in0=ab_(AI), in1=cc_(Cp[1]), op=ALU.mult), r=AW + ["Cp1"], w=["w1"])
                              S.dve(lambda e: e.tensor_tensor(out=Gr[:], in0=w0[:], in1=w1[:], op=ALU.subtract), r=["w0", "w1"], w=["Gr"])
                              S.dve(lambda e, ab_=ab_, cc_=cc_: e.tensor_tensor(out=w0[:], in0=ab_(AI), in1=cc_(Cp[0]), op=ALU.mult), r=AW + ["Cp0"], w=["w0"])
                              S.dve(lambda e, ab_=ab_, cc_=cc_: e.tensor_tensor(out=w1[:], in0=ab_(AR), in1=cc_(Cp[1]), op=ALU.mult), r=AW + ["Cp1"], w=["w1"])
                              S.dve(lambda e: e.scalar_tensor_tensor(out=Gm[:], in0=w0[:], scalar=-1.0, in1=w1[:], op0=ALU.mult, op1=ALU.subtract), r=["w0", "w1"], w=["Gm"])
                              stage(6)
                              S.act(lambda e, gs=gs: e.activation(out=WYr[:, gs, :].rearrange("p g (t q) -> p g t q", q=16), in_=Gr[:, :, 1:9, :], func=AF.Copy), r=["Gr"], w=["WYr"])
                              S.act(lambda e, gs=gs: e.activation(out=WYi[:, gs, :].rearrange("p g (t q) -> p g t q", q=16), in_=Gm[:, :, 1:9, :], func=AF.Copy), r=["Gm"], w=["WYi"])
                              stage(7)
                              for gl in range(GC):
                                  g = g0 + gl
                                  b = nT % 2
                                  nT += 1
                                  S.pe(lambda e, gl=gl, b=b: e.matmul(pT[b][:], lhsT=Er[:, gl, :, :].rearrange("p s h -> p (s h)"), rhs=Gr[:, gl, 0:8, :].rearrange("p t q -> p (t q)"), start=True, stop=False),
                                       r=["Er", "Gr"], w=[f"pT{b}"])
                                  S.pe(lambda e, gl=gl, b=b: e.matmul(pT[b][:], lhsT=Ei[:, gl, :, :].rearrange("p s h -> p (s h)"), rhs=Gm[:, gl, 0:8, :].rearrange("p t q -> p (t q)"), start=False, stop=True),
                                       r=["Ei", "Gm"], w=[f"pT{b}"])
                                  S.dve(lambda e, b=b: e.tensor_tensor(out=tmk[b][:], in0=pT[b][:], in1=mask01[:], op=ALU.mult), r=[f"pT{b}", "mask01"], w=[f"tmk{b}"])
                                  S.dve(lambda e, b=b, g=g: e.scalar_tensor_tensor(out=Tm[:, g, :], in0=identf[:], scalar=dvec[:, g:g + 1], in1=tmk[b][:], op0=ALU.mult, op1=ALU.add),
                                        r=[f"tmk{b}"] + DVR, w=["Tm"])
                                  stage(8)
                                  for (src, dstw, tag) in ((Xrb, WXr, "Xrb"), (Xib, WXi, "Xib")):
                                      bx = nX % 2
                                      nX += 1
                                      S.pe(lambda e, gl=gl, bx=bx, src=src: e.transpose(out=pX[bx][:], in_=src[:, gl, :, :].rearrange("p s h -> p (s h)"), identity=ident[0:64, 0:64]),
                                           r=[tag], w=[f"pX{bx}"])
                                      S.act(lambda e, bx=bx, g=g, dstw=dstw: e.activation(out=dstw[:, g, :], in_=pX[bx][:], func=AF.Copy), r=[f"pX{bx}"], w=["W" + tag])

                      except _Stop:
                          pass
                      S.run()

                with ExitStack() as es:
                  if "2" in phases:
                      T = lambda name, shape, dt: es.enter_context(nc.sbuf_tensor(name, shape, dt))
                      P = lambda name, shape, dt: es.enter_context(nc.psum_tensor(name, shape, dt))
                      S = Sched(nc, es, "p2m")
                      WN = 8
                      Ucm = T("Ucm", [128, 8, 1024], BF16)
                      Ug = T("Ug", [128, 64, 128], BF16)
                      Xr = T("Xr", [64, 64, 128], BF16)
                      Xi = T("Xi", [64, 64, 128], BF16)
                      Hw = {(ri, w): T(f"Hw{ri}{w}", [64, 64, WN + 1], F32) for ri in range(2) for w in range(2)}
                      Hb = [T(f"Hb{ri}", [64, 64, 64], BF16) for ri in range(2)]
                      Uc2 = T("Uc2", [128, 32, 8, 16], BF16)
                      sc = [T(f"sc{i}", [64, 64], F32) for i in range(4)]
                      Yg = Ucm[0:64]
                      Yfm = T("Yfm", [128, 8, 512], BF16)
                      sgm = [T(f"sgm{i}", [128, 512], BF16) for i in range(2)]
                      yo = [T(f"yo{i}", [128, 512], BF16) for i in range(2)]
                      ptr = [P(f"ptr{i}", [128, 4, 128], BF16) for i in range(2)]
                      pXr = P("pXr", [64, 4, 128], F32)
                      pXi = P("pXi", [64, 4, 128], F32)
                      pY = [P(f"pY{i}", [64, 4, 128], F32) for i in range(2)]
                      pZ = [P(f"pZ{i}", [128, 512], F32) for i in range(2)]
                      for ri in range(2):
                          S.dve(lambda e, ri=ri: e.memset(Hw[(ri, 1)][:, :, WN], 0.0), w=[("hw", ri, 1)])
                      nq = 0
                      for ct in range(NCT):
                          Uv = Ud.rearrange("(ct tt hf cc s) d -> ct hf tt cc s d", tt=8, hf=2, cc=8, s=8)
                          for hf in range(2):
                              for tt in range(8):
                                  S.load(lambda e, ct=ct, hf=hf, tt=tt: e.dma_start(out=Ucm[hf * 64 + tt * 8:hf * 64 + tt * 8 + 8, :, :], in_=Uv[ct, hf, tt]), w=[("Ucm", hf, tt)])
                          UCM = [("Ucm", hf, tt) for hf in range(2) for tt in range(8)]
                          for gh in range(2):
                              S.pool(lambda e, gh=gh: e.tensor_copy(out=Uc2[:], in_=Ucm[:, :, gh * 512:(gh + 1) * 512].rearrange("c s (g h) -> c g s h", h=16)), r=UCM, w=["Uc2"])
                              for gq in range(gh * 8, gh * 8 + 8):
                                  b = gq % 2
                                  for j in range(4):
                                      gl = (gq - gh * 8) * 4 + j
                                      S.pe(lambda e, gl=gl, b=b, j=j: e.transpose(out=ptr[b][:, j, :], in_=Uc2[:, gl, :, :].rearrange("c s h -> c (s h)"), identity=ident[:]), r=["Uc2"], w=[f"ptr{b}"])
                                  if gq % 2 == 0:
                                      S.act(lambda e, gq=gq, b=b: e.activation(out=Ug[:, gq * 4:gq * 4 + 4, :], in_=ptr[b][:], func=AF.Copy), r=[f"ptr{b}"], w=[("Ug", gq)])
                                  else:
                                      S.dve(lambda e, gq=gq, b=b: e.tensor_copy(out=Ug[:, gq * 4:gq * 4 + 4, :], in_=ptr[b][:]), r=[f"ptr{b}"], w=[("Ug", gq)])
                          for gq in range(16):
                              for j in range(4):
                                  g = gq * 4 + j
                                  S.pe(lambda e, g=g, j=j: e.matmul(pXr[:, j, :], lhsT=WXr[:, g, :], rhs=Ug[:, g, :], start=True, stop=True), r=[("Ug", gq)], w=["pXr"])
                                  S.pe(lambda e, g=g, j=j: e.matmul(pXi[:, j, :], lhsT=WXi[:, g, :], rhs=Ug[:, g, :], start=True, stop=True), r=[("Ug", gq)], w=["pXi"])
                              S.act(lambda e, gq=gq: e.activation(out=Xr[:, gq * 4:gq * 4 + 4, :], in_=pXr[:], func=AF.Copy), r=["pXr"], w=["Xr"])
                              S.act(lambda e, gq=gq: e.activation(out=Xi[:, gq * 4:gq * 4 + 4, :], in_=pXi[:], func=AF.Copy), r=["pXi"], w=["Xi"])
                          for c in range(128):
                              w = (c // WN) % 2
                              k = c % WN
                              if k == 0:
                                  prv = lambda ri, w=w: Hw[(ri, 1 - w)][:, :, WN]
                                  rk = lambda ri, w=w: ("hw", ri, 1 - w)
                              else:
                                  prv = lambda ri, w=w, k=k: Hw[(ri, w)][:, :, k]
                                  rk = lambda ri, w=w: ("hw", ri, w)
                              S.dve(lambda e, prv=prv: e.tensor_tensor(out=sc[0][:], in0=A8r[:], in1=prv(0), op=ALU.mult), r=[rk(0)], w=["sc0"])
                              S.dve(lambda e, prv=prv: e.tensor_tensor(out=sc[1][:], in0=A8i[:], in1=prv(1), op=ALU.mult), r=[rk(1)], w=["sc1"])
                              S.dve(lambda e, prv=prv: e.tensor_tensor(out=sc[2][:], in0=A8r[:], in1=prv(1), op=ALU.mult), r=[rk(1)], w=["sc2"])
                              S.dve(lambda e, prv=prv: e.tensor_tensor(out=sc[3][:], in0=A8i[:], in1=prv(0), op=ALU.mult), r=[rk(0)], w=["sc3"])
                              S.dve(lambda e: e.tensor_tensor(out=sc[0][:], in0=sc[0][:], in1=sc[1][:], op=ALU.subtract), r=["sc0", "sc1"], w=["sc0"])
                              S.dve(lambda e: e.tensor_tensor(out=sc[2][:], in0=sc[2][:], in1=sc[3][:], op=ALU.add), r=["sc2", "sc3"], w=["sc2"])
                              cp = (c // 16) * 8 + (c % 8) + (64 if (c % 16) >= 8 else 0)
                              S.dve(lambda e, w=w, k=k, cp=cp: e.tensor_tensor(out=Hw[(0, w)][:, :, k + 1], in0=sc[0][:], in1=Xr[:, :, cp], op=ALU.add), r=["sc0", "Xr"], w=[("hw", 0, w)])
                              S.dve(lambda e, w=w, k=k, cp=cp: e.tensor_tensor(out=Hw[(1, w)][:, :, k + 1], in0=sc[2][:], in1=Xi[:, :, cp], op=ALU.add), r=["sc2", "Xi"], w=[("hw", 1, w)])
                              if k == WN - 1:
                                  tt = c // 16
                                  for ri in range(2):
                                      if w == 0:
                                          S.act(lambda e, ri=ri, tt=tt: e.activation(out=Hb[ri][:, :, tt * 8], in_=Hw[(ri, 0)][:, :, WN], func=AF.Copy), r=[("hw", ri, 0)], w=[("hb", ri)])
                                      else:
                                          S.act(lambda e, ri=ri, tt=tt: e.activation(out=Hb[ri][:, :, tt * 8 + 1:tt * 8 + 8], in_=Hw[(ri, 1)][:, :, 1:WN], func=AF.Copy), r=[("hw", ri, 1)], w=[("hb", ri)])
                          for gq in range(16):
                              b = gq % 2
                              for j in range(4):
                                  g = gq * 4 + j
                                  S.pe(lambda e, g=g, b=b, j=j: e.matmul(pY[b][:, j, :], lhsT=Ug[:, g, 64:128], rhs=Tm[:, g, :], start=True, stop=False), r=[("Ug", gq)], w=[f"pY{b}"])
                                  S.pe(lambda e, g=g, b=b, j=j: e.matmul(pY[b][:, j, :], lhsT=Hb[0][:, g, :], rhs=WYr[:, g, :], start=False, stop=False), r=[("hb", 0)], w=[f"pY{b}"])
                                  S.pe(lambda e, g=g, b=b, j=j: e.matmul(pY[b][:, j, :], lhsT=Hb[1][:, g, :], rhs=WYi[:, g, :], start=False, stop=True), r=[("hb", 1)], w=[f"pY{b}"])
                              S.act(lambda e, gq=gq, b=b: e.activation(out=Yg[:, :, gq * 64:(gq + 1) * 64].rearrange("c t (gl q) -> c gl t q", q=16), in_=pY[b][:].rearrange("c gl (t q) -> c gl t q", q=16), func=AF.Gelu),
                                    r=[f"pY{b}"], w=UCM)
                          for cb in range(8):
                              for th_ in range(2):
                                  b = nq % 2
                                  nq += 1
                                  for j in range(4):
                                      t = th_ * 4 + j
                                      S.pe(lambda e, cb=cb, t=t, b=b, j=j: e.transpose(out=ptr[b][:, j, 0:64], in_=Yg[:, t, cb * 128:(cb + 1) * 128], identity=ident[0:64, 0:64]), r=UCM, w=[f"ptr{b}"])
                                  S.dve(lambda e, cb=cb, th_=th_, b=b: e.tensor_copy(out=Yfm[:, cb, :].rearrange("p (c t) -> p t c", t=8)[:, th_ * 4:th_ * 4 + 4, :], in_=ptr[b][:, :, 0:64]),
                                        r=[f"ptr{b}"], w=[("Yfm", cb)])
                          YF = [("Yfm", cb) for cb in range(8)]
                          for nb in range(8):
                              b = nb % 2
                              for cb in range(8):
                                  S.pe(lambda e, nb=nb, cb=cb, b=b: e.matmul(pZ[b][:], lhsT=wglu[:, cb, nb * 128:(nb + 1) * 128], rhs=Yfm[:, cb, :], start=(cb == 0), stop=(cb == 7)), r=YF, w=[f"pZ{b}"])
                              S.act(lambda e, nb=nb, b=b: e.activation(out=sgm[b][:], in_=pZ[b][:], func=AF.Sigmoid, bias=bglu[:, nb:nb + 1]), r=[f"pZ{b}"], w=[f"sgm{b}"])
                              S.dve(lambda e, nb=nb, b=b: e.tensor_tensor(out=yo[b][:], in0=Yfm[:, nb, :], in1=sgm[b][:], op=ALU.mult), r=[f"sgm{b}"] + YF, w=[f"yo{b}"])
                              S.store(lambda e, nb=nb, b=b, ct=ct: e.dma_start(out=YC[nb * 128:(nb + 1) * 128, ct * 512:(ct + 1) * 512], in_=yo[b][:]), r=[f"yo{b}"], w=[("YCs", nb, ct)])
                      S.run()

        if "3" in phases:
            with ExitStack() as es:
                T = lambda name, shape, dt: es.enter_context(nc.sbuf_tensor(name, shape, dt))
                P = lambda name, shape, dt: es.enter_context(nc.psum_tensor(name, shape, dt))
                S = Sched(nc, es, "p3")
                kt = [T(f"kt{i}", [128, NTOK], BF16) for i in range(2)]
                qt = [T(f"qt{i}", [128, NOWN], BF16) for i in range(2)]
                vh = [T(f"vh{i}", [128, NT, 128], BF16) for i in range(2)]
                pb = {(c, i): T(f"pb{c}{i}", [128, 512], BF16) for c in range(2) for i in range(2)}
                Lr = [T(f"Lr{c}", [128, 512], F32) for c in range(2)]
                Aa = [T(f"Aa{c}", [128, 512], F32) for c in range(2)]
                Oc = T("Oc", [128, 512], F32)
                sq = T("sq3", [128, 512], BF16)
                rr = T("rr3", [128, 512], F32)
                ya = [T(f"ya{i}", [128, 512], BF16) for i in range(2)]
                pS = {(c, i): P(f"pS{c}{i}", [128, 512], F32) for c in range(2) for i in range(2)}
                pO = [P(f"pO{c}", [128, 512], F32) for c in range(2)]
                pD = [P(f"pD{c}", [128, 512], F32) for c in range(2)]
                nya = [0]

                def emit_loads(h):
                    hb = h % 2
                    S.load(lambda e, h=h, hb=hb: e.dma_start(out=kt[hb][:], in_=KT[h]), w=[f"kt{hb}"])
                    S.load(lambda e, h=h, hb=hb: e.dma_start(out=qt[hb][:], in_=QT[h]), w=[f"qt{hb}"])
                    S.load(lambda e, h=h, hb=hb: e.dma_start(out=vh[hb][:], in_=Vd.rearrange("(t p) c -> p t c", p=128)[:, :, h * 128:(h + 1) * 128]), w=[f"vh{hb}"])

                def emit_S(job):
                    h, j, t, b = job
                    hb = h % 2
                    c0 = 0 if t < 8 * j else 64 * (t - 8 * j)
                    for c in range(2):
                        S.pe(lambda e, c=c, b=b, t=t, c0=c0, j=j, hb=hb: e.matmul(pS[(c, b)][:, c0:512], lhsT=kt[hb][64 * c:64 * c + 64, t * 128:(t + 1) * 128], rhs=qt[hb][64 * c:64 * c + 64, j * 512 + c0:(j + 1) * 512], start=True, stop=True),
                             r=[f"kt{hb}", f"qt{hb}"], w=[f"pS{c}{b}"])
                    for c in range(2):
                        if t == 0:
                            S.act(lambda e, c=c, b=b, c0=c0: e.activation(out=pb[(c, b)][:, c0:512], in_=pS[(c, b)][:, c0:512], func=AF.Exp, bias=kbias0[:, 0:1]), r=[f"pS{c}{b}"], w=[f"pb{c}{b}"])
                        else:
                            S.act(lambda e, c=c, b=b, c0=c0: e.activation(out=pb[(c, b)][:, c0:512], in_=pS[(c, b)][:, c0:512], func=AF.Exp), r=[f"pS{c}{b}"], w=[f"pb{c}{b}"])

                def emit_PV(job):
                    h, j, t, b = job
                    hb = h % 2
                    last = 8 * j + 7
                    c0 = 0 if t < 8 * j else 64 * (t - 8 * j)
                    for c in range(2):
                        S.pe(lambda e, c=c, b=b, t=t, c0=c0, hb=hb, last=last: e.matmul(pO[c][:, c0:512], lhsT=vh[hb][:, t, :], rhs=pb[(c, b)][:, c0:512], start=(t == 0), stop=(t == last)),
                             r=[f"vh{hb}", f"pb{c}{b}"], w=[f"pO{c}"])
                        S.pe(lambda e, c=c, b=b, t=t, c0=c0, last=last: e.matmul(pD[c][:, c0:512], lhsT=onesb[:], rhs=pb[(c, b)][:, c0:512], start=(t == 0), stop=(t == last)),
                             r=[f"pb{c}{b}"], w=[f"pD{c}"])
                    if t != last:
                        return
                    for c in range(2):
                        S.act(lambda e, c=c: e.activation(out=Lr[c][:], in_=pD[c][:], func=AF.Ln), r=[f"pD{c}"], w=[f"Lr{c}"])
                        S.act(lambda e, c=c: e.activation(out=Lr[c][:], in_=Lr[c][:], func=AF.Exp, scale=-1.0), r=[f"Lr{c}"], w=[f"Lr{c}"])
                        S.dve(lambda e, c=c: e.tensor_tensor(out=Aa[c][:], in0=pO[c][:], in1=Lr[c][:], op=ALU.mult), r=[f"pO{c}", f"Lr{c}"], w=[f"Aa{c}"])
                    S.dve(lambda e: e.scalar_tensor_tensor(out=Oc[:], in0=Aa[1][:], scalar=nlam[:, 0:1], in1=Aa[0][:], op0=ALU.mult, op1=ALU.add), r=["Aa0", "Aa1"], w=["Oc"])
                    S.act(lambda e: e.activation(out=sq[:], in_=Oc[:], func=AF.Square), r=["Oc"], w=["sq"])
                    S.pe(lambda e: e.matmul(pD[0][:], lhsT=onesb[:], rhs=sq[:], start=True, stop=True), r=["sq"], w=["pD0"])
                    S.act(lambda e: e.activation(out=rr[:], in_=pD[0][:], func=AF.Ln, scale=1.0 / 128, bias=EPS), r=["pD0"], w=["rr"])
                    S.act(lambda e: e.activation(out=rr[:], in_=rr[:], func=AF.Exp, scale=-0.5), r=["rr"], w=["rr"])
                    yb = nya[0] % 2
                    nya[0] += 1
                    S.dve(lambda e, yb=yb: e.scalar_tensor_tensor(out=ya[yb][:], in0=Oc[:], scalar=gsub[:, 0:1], in1=rr[:], op0=ALU.mult, op1=ALU.mult), r=["Oc", "rr"], w=[f"ya{yb}"])
                    S.store(lambda e, yb=yb, h=h, j=j: e.dma_start(out=YC[1024 + h * 128:1024 + (h + 1) * 128, j * 512:(j + 1) * 512], in_=ya[yb][:]), r=[f"ya{yb}"], w=[("YCa", h, j)])

                jobs = []
                for h in range(NH):
                    for j in range(NQB):
                        for t in range(8 * j + 8):
                            jobs.append((h, j, t, len(jobs) % 2))
                prev = None
                curh = -1
                for job in jobs:
                    if job[0] != curh:
                        curh = job[0]
                        emit_loads(curh)
                    emit_S(job)
                    if prev is not None:
                        emit_PV(prev)
                    prev = job
                emit_PV(prev)
                S.run()

        if "4" in phases:
            with ExitStack() as es:
                T = lambda name, shape, dt: es.enter_context(nc.sbuf_tensor(name, shape, dt))
                P = lambda name, shape, dt: es.enter_context(nc.psum_tensor(name, shape, dt))
                S = Sched(nc, es, "p4")
                NWP = 4
                wp = [T(f"wp{i}", [128, 16, 512], BF16) for i in range(NWP)]
                aT = T("aT", [128, 64, 512], BF16)
                a16 = T("a16", [128, 16, 512], BF16)
                x1 = [T(f"x1{i}", [128, D], F32) for i in range(4)]
                XN2K = [("aT", 4), ("aT", 5), ("aT", 6), ("aT", 7)]
                junk = aT[:, 0:4, :].rearrange("p a b -> p (a b)")
                xn2v = aT[:, 4:8, :].rearrange("p a b -> p (a b)")
                gbc = [T(f"gbc{i}", [128, D], F32) for i in range(2)]
                stmp = [T(f"stmp{i}", [128, 512], BF16) for i in range(2)]
                ftmp = [T(f"ftmp{i}", [128, 512], F32) for i in range(2)]
                ot = [T(f"ot{i}", [128, 512], F32) for i in range(2)]
                ss = T("ss4", [128, 4], F32)
                rs = T("rs4", [128, 4], F32)
                pM = [P(f"pM{i}", [128, 512], F32) for i in range(4)]
                pA = [P(f"pA{i}", [128, 512], F32) for i in range(2)]
                pt = [P(f"pt4{i}", [128, 4, 128], BF16) for i in range(2)]
                S.load(lambda e: e.dma_start(out=gbc[0][:], in_=MODR[2:3, :].to_broadcast([128, D])), w=["gbc0"])
                S.load(lambda e: e.dma_start(out=gbc[1][:], in_=MODR[5:6, :].to_broadcast([128, D])), w=["gbc1"])
                xown = I["xs"].rearrange("(t two i) d -> t two i d", two=2, i=64)
                nwp = 0
                nf = 0
                na = 0
                no = 0
                for bk in range(NOWN // 512):
                    S.load(lambda e, bk=bk: e.dma_start(out=a16[:], in_=YC[:, bk * 512:(bk + 1) * 512].rearrange("(kb p) t -> p kb t", p=128)), w=["a16"])
                    for i in range(4):
                        for hh in range(2):
                            tl = bk * 8 + i * 2 + hh
                            S.load(lambda e, i=i, hh=hh, tl=tl: e.dma_start(out=x1[i][64 * hh:64 * hh + 64, :], in_=xown[tl, 1]), w=[f"x1{i}"])
                    for db in range(4):
                        wb = nwp % NWP
                        nwp += 1
                        S.load(lambda e, wb=wb, db=db: e.dma_start(out=wp[wb][:], in_=WOUT[:, db * 512:(db + 1) * 512].rearrange("(kb p) n -> p kb n", p=128)), w=[f"wp{wb}"])
                        for i in range(4):
                            for kb in range(NKB):
                                S.pe(lambda e, i=i, kb=kb, wb=wb: e.matmul(pM[i][:], lhsT=a16[:, kb, i * 128:(i + 1) * 128], rhs=wp[wb][:, kb, :], start=(kb == 0), stop=(kb == NKB - 1)),
                                     r=["a16", f"wp{wb}"], w=[f"pM{i}"])
                            fb = nf % 2
                            nf += 1
                            S.dve(lambda e, i=i, fb=fb, db=db: e.tensor_tensor(out=ftmp[fb][:], in0=pM[i][:], in1=gbc[0][:, db * 512:(db + 1) * 512], op=ALU.mult), r=[f"pM{i}", "gbc0"], w=[f"ftmp{fb}"])
                            S.pool(lambda e, i=i, fb=fb, db=db: e.tensor_tensor(out=x1[i][:, db * 512:(db + 1) * 512], in0=x1[i][:, db * 512:(db + 1) * 512], in1=ftmp[fb][:], op=ALU.add), r=[f"ftmp{fb}", f"x1{i}"], w=[f"x1{i}"])
                    for i in range(4):
                        S.act(lambda e, i=i: e.activation(out=junk, in_=x1[i][:], func=AF.Square, accum_out=ss[:, i:i + 1]), r=[f"x1{i}"], w=[("aT", 0), ("aT", 1), ("aT", 2), ("aT", 3), f"ss{i}"])
                        S.act(lambda e, i=i: e.activation(out=rs[:, i:i + 1], in_=ss[:, i:i + 1], func=AF.Ln, scale=1.0 / D, bias=EPS), r=[f"ss{i}"], w=[f"rs{i}"])
                        S.act(lambda e, i=i: e.activation(out=rs[:, i:i + 1], in_=rs[:, i:i + 1], func=AF.Exp, scale=-0.5), r=[f"rs{i}"], w=[f"rs{i}"])
                        S.act(lambda e, i=i: e.activation(out=xn2v, in_=x1[i][:], func=AF.Copy, scale=rs[:, i:i + 1]), r=[f"x1{i}", f"rs{i}"], w=XN2K)
                        for q in range(4):
                            pb_ = q % 2
                            for j in range(4):
                                kb = q * 4 + j
                                S.pe(lambda e, kb=kb, pb_=pb_, j=j: e.transpose(out=pt[pb_][:, j, :], in_=xn2v[:, kb * 128:(kb + 1) * 128], identity=ident[:]), r=XN2K, w=[f"pt{pb_}"])
                            for j in range(4):
                                kb = q * 4 + j
                                S.dve(lambda e, i=i, kb=kb, pb_=pb_, j=j: e.tensor_scalar(out=a16[:, kb, i * 128:(i + 1) * 128], in0=pt[pb_][:, j, :], scalar1=g2s[:, kb:kb + 1], scalar2=modv[:, 48 + kb:49 + kb], op0=ALU.mult, op1=ALU.add),
                                      r=[f"pt{pb_}"], w=["a16"])
                    for hp in range(16):
                        wb = nwp % NWP
                        nwp += 1
                        S.load(lambda e, wb=wb, hp=hp: e.dma_start(out=wp[wb][:], in_=W1[:, hp * 512:(hp + 1) * 512].rearrange("(kb p) n -> p kb n", p=128)), w=[f"wp{wb}"])
                        for hl in range(4):
                            hbk = hp * 4 + hl
                            ab = na % 2
                            na += 1
                            for kb in range(NKB):
                                S.pe(lambda e, hl=hl, kb=kb, wb=wb, ab=ab: e.matmul(pA[ab][:], lhsT=wp[wb][:, kb, hl * 128:(hl + 1) * 128], rhs=a16[:, kb, :], start=(kb == 0), stop=(kb == NKB - 1)),
                                     r=["a16", f"wp{wb}"], w=[f"pA{ab}"])
                            S.act(lambda e, ab=ab: e.activation(out=stmp[ab][:], in_=pA[ab][:], func=AF.Square), r=[f"pA{ab}"], w=[f"stmp{ab}"])
                            S.dve(lambda e, ab=ab, hbk=hbk: e.scalar_tensor_tensor(out=aT[:, hbk, :], in0=pA[ab][:], scalar=0.0, in1=stmp[ab][:], op0=ALU.is_gt, op1=ALU.mult), r=[f"pA{ab}", f"stmp{ab}"], w=[("aT", hbk)])
                    for db in range(4):
                        for hq in range(4):
                            wb = nwp % NWP
                            nwp += 1
                            S.load(lambda e, wb=wb, db=db, hq=hq: e.dma_start(out=wp[wb][:], in_=W2[hq * 2048:(hq + 1) * 2048, db * 512:(db + 1) * 512].rearrange("(hb p) n -> p hb n", p=128)), w=[f"wp{wb}"])
                            for i in range(4):
                                for hl in range(16):
                                    hbk = hq * 16 + hl
                                    S.pe(lambda e, i=i, hl=hl, hbk=hbk, wb=wb, hq=hq: e.matmul(pM[i][:], lhsT=aT[:, hbk, i * 128:(i + 1) * 128], rhs=wp[wb][:, hl, :], start=(hq == 0 and hl == 0), stop=(hq == 3 and hl == 15)),
                                         r=[("aT", hbk), f"wp{wb}"], w=[f"pM{i}"])
                        for i in range(4):
                            fb = nf % 2
                            nf += 1
                            ob = no % 2
                            no += 1
                            S.dve(lambda e, i=i, fb=fb, db=db: e.tensor_tensor(out=ftmp[fb][:], in0=pM[i][:], in1=gbc[1][:, db * 512:(db + 1) * 512], op=ALU.mult), r=[f"pM{i}", "gbc1"], w=[f"ftmp{fb}"])
                            S.pool(lambda e, i=i, fb=fb, db=db, ob=ob: e.tensor_tensor(out=ot[ob][:], in0=x1[i][:, db * 512:(db + 1) * 512], in1=ftmp[fb][:], op=ALU.add), r=[f"ftmp{fb}", f"x1{i}"], w=[f"ot{ob}"])
                            S.store(lambda e, i=i, db=db, ob=ob, bk=bk: e.dma_start(out=out[bk * 512 + i * 128:bk * 512 + (i + 1) * 128, db * 512:(db + 1) * 512], in_=ot[ob][:]), r=[f"ot{ob}"], w=[("out", bk, i, db)], eng="sp")
                S.run()
    return nc


def make_in_maps(inputs, nb=None, seq=None):
    x = np.asarray(inputs["x"], dtype=np.float32)
    B, Sq, _ = x.shape
    maps = []
    for b in range(B):
        for par in range(2):
            m = {}
            if par == 1:
                m["xs"] = np.ascontiguousarray(x[b])
            else:
                m["xs"] = np.ascontiguousarray(np.concatenate([np.zeros((64, D), np.float32), x[b][:-64]], axis=0))
            m["c"] = np.ascontiguousarray(np.asarray(inputs["c"], np.float32)[b])
            for n, shp in PARAMS:
                if n in ("c", "valid0", "kbias0"):
                    continue
                m[n] = np.ascontiguousarray(np.asarray(inputs[n], np.float32)[0])
            v0 = np.ones((128, 1), np.float32)
            k0 = np.zeros((128, 1), np.float32)
            if par == 0:
                v0[:64] = 0.0
                k0[:64] = -30000.0
            m["valid0"] = v0
            m["kbias0"] = k0
            maps.append(m)
    return maps


def assemble(results, B, Sq):
    NT = Sq // 128
    out = np.empty((B, Sq, D), np.float32)
    ov = out.reshape(B, NT, 2, 64, D)
    for b in range(B):
        for par in range(2):
            ov[b, :, par] = np.asarray(results[2 * b + par]["out"], np.float32).reshape(NT, 64, D)
    return out


def kernel(**inputs):
    x = inputs["x"]
    B, Sq, _ = x.shape
    nc = build(NT=Sq // 128)
    maps = make_in_maps(inputs)
    res = run_bass_kernel_spmd(nc, maps, core_ids=list(range(len(maps))))
    return assemble(res.results, B, Sq)
```
